# Optimizing a Trainium2 kernel written in Bass

```python
import math
import jax, jax.numpy as jnp
from jax import lax
import numpy as np

D_MODEL = 1024
BATCH = 8
SEQ = 4096
DEPTH = 1

HEAD_DIM = 64
N_HEADS = D_MODEL // HEAD_DIM
H_FOX = N_HEADS // 2
H_MOBA = N_HEADS - H_FOX
W_FOX = H_FOX * HEAD_DIM
W_MOBA = H_MOBA * HEAD_DIM
IN_COLS = 3 * W_FOX + H_FOX + 3 * W_MOBA
Q_BLOCK = 128
MOBA_BLOCK = 256
MOBA_TOPK = 3
N_BUCKETS = 32
MAX_DISTANCE = 128
D_FF = -(-8 * D_MODEL // (3 * 256)) * 256
EPS = 1e-6
FORGET_BIAS_INIT = 3.0

kernel_name = "hybrid_fox_moba_adaln_block"


def rms_norm(x, g):
    xf = x.astype(jnp.float32)
    y = xf * lax.rsqrt(jnp.mean(xf * xf, axis=-1, keepdims=True) + EPS)
    return (y * g.astype(jnp.float32)).astype(x.dtype)


def t5_bucket(dist):
    n = jnp.maximum(dist, 0)
    max_exact = N_BUCKETS // 2
    nf = jnp.maximum(n, 1).astype(jnp.float32)
    large = max_exact + (jnp.log(nf / max_exact) / math.log(MAX_DISTANCE / max_exact)
                         * (N_BUCKETS - max_exact)).astype(jnp.int32)
    large = jnp.minimum(large, N_BUCKETS - 1)
    return jnp.where(n < max_exact, n, large)


def fox_attention(q, k, v, log_f_cum):
    B, H, S, Dh = q.shape
    n_blocks = S // Q_BLOCK
    scale = HEAD_DIM ** -0.5
    k_pos = jnp.arange(S)

    def block(i):
        start = i * Q_BLOCK
        q_blk = lax.dynamic_slice_in_dim(q, start, Q_BLOCK, axis=2)
        f_blk = lax.dynamic_slice_in_dim(log_f_cum, start, Q_BLOCK, axis=2)
        s = jnp.einsum('bhqd,bhkd->bhqk', q_blk, k,
                       preferred_element_type=jnp.float32) * scale
        s = s + f_blk[..., :, None] - log_f_cum[..., None, :]
        q_pos = start + jnp.arange(Q_BLOCK)
        causal = k_pos[None, :] <= q_pos[:, None]
        s = jnp.where(causal, s, -jnp.inf)
        p = jax.nn.softmax(s, axis=-1)
        return jnp.einsum('bhqk,bhkd->bhqd', p.astype(v.dtype), v)

    out = lax.map(block, jnp.arange(n_blocks))
    return jnp.transpose(out, (1, 0, 3, 2, 4)).reshape(B, S, H * Dh)


def moba_attention(q, k, v, rel_bias):
    B, H, S, Dh = q.shape
    nb = -(-S // MOBA_BLOCK)
    pad = nb * MOBA_BLOCK - S
    kb = jnp.pad(k, ((0, 0), (0, 0), (0, pad), (0, 0))).reshape(B, H, nb, MOBA_BLOCK, Dh)
    vb = jnp.pad(v, ((0, 0), (0, 0), (0, pad), (0, 0))).reshape(B, H, nb, MOBA_BLOCK, Dh)
    counts = jnp.clip(S - jnp.arange(nb) * MOBA_BLOCK, 1, MOBA_BLOCK).astype(jnp.float32)
    k_mean = jnp.sum(kb.astype(jnp.float32), axis=3) / counts[None, None, :, None]
    n_q = S // Q_BLOCK
    k_sel = min(MOBA_TOPK, nb)
    scale = HEAD_DIM ** -0.5
    offs = jnp.arange(MOBA_BLOCK)
    blk_ids = jnp.arange(nb)
    h_ix3 = jnp.arange(H)[:, None, None]
    h_ix4 = jnp.arange(H)[:, None, None, None]
    bias_hb = rel_bias.T.astype(jnp.float32)

    def chunk(n):
        b = n // n_q
        i = n % n_q
        start = i * Q_BLOCK
        q_b = lax.dynamic_index_in_dim(q, b, axis=0, keepdims=False)
        kb_b = lax.dynamic_index_in_dim(kb, b, axis=0, keepdims=False)
        vb_b = lax.dynamic_index_in_dim(vb, b, axis=0, keepdims=False)
        km_b = lax.dynamic_index_in_dim(k_mean, b, axis=0, keepdims=False)
        q_c = lax.dynamic_slice_in_dim(q_b, start, Q_BLOCK, axis=1)
        q_pos = start + jnp.arange(Q_BLOCK)
        own = start // MOBA_BLOCK
        g = jnp.einsum('hqd,hnd->hqn', q_c.astype(jnp.float32), km_b)
        g = jnp.where(blk_ids[None, None, :] < own, g, -jnp.inf)
        _, idx = lax.top_k(g, k_sel)
        valid = idx < own
        k_g = kb_b[h_ix3, idx]
        v_g = vb_b[h_ix3, idx]
        s_sel = jnp.einsum('hqd,hqnkd->hqnk', q_c, k_g,
                           preferred_element_type=jnp.float32) * scale
        pos_sel = idx[..., None] * MOBA_BLOCK + offs
        s_sel = s_sel + bias_hb[h_ix4, t5_bucket(q_pos[None, :, None, None] - pos_sel)]
        s_sel = jnp.where(valid[..., None], s_sel, -jnp.inf)
        k_own = lax.dynamic_index_in_dim(kb_b, own, axis=1, keepdims=False)
        v_own = lax.dynamic_index_in_dim(vb_b, own, axis=1, keepdims=False)
        s_own = jnp.einsum('hqd,hkd->hqk', q_c, k_own,
                           preferred_element_type=jnp.float32) * scale
        pos_own = own * MOBA_BLOCK + offs
        bkt_own = t5_bucket(q_pos[:, None] - pos_own[None, :])
        s_own = s_own + jnp.transpose(rel_bias.astype(jnp.float32)[bkt_own], (2, 0, 1))
        s_own = jnp.where(pos_own[None, None, :] <= q_pos[None, :, None], s_own, -jnp.inf)
        s_all = jnp.concatenate([s_sel.reshape(H, Q_BLOCK, k_sel * MOBA_BLOCK), s_own], axis=-1)
        p = jax.nn.softmax(s_all, axis=-1).astype(v.dtype)
        p_sel = p[..., :k_sel * MOBA_BLOCK].reshape(H, Q_BLOCK, k_sel, MOBA_BLOCK)
        p_own = p[..., k_sel * MOBA_BLOCK:]
        return (jnp.einsum('hqnk,hqnkd->hqd', p_sel, v_g)
                + jnp.einsum('hqk,hkd->hqd', p_own, v_own))

    out = lax.map(chunk, jnp.arange(B * n_q))
    out = out.reshape(B, n_q, H, Q_BLOCK, Dh)
    return jnp.transpose(out, (0, 1, 3, 2, 4)).reshape(B, S, H * Dh)


def setup_inputs(seed: int = 0) -> dict:
    key = jax.random.key(seed)
    ks = jax.random.split(key, 20)
    f32 = jnp.float32
    nrm = lambda k, shape: jax.random.normal(k, shape, dtype=f32)
    return {
        "x": nrm(ks[0], (BATCH, SEQ, D_MODEL)),
        "c": nrm(ks[1], (BATCH, D_MODEL)),
        "w_ada": nrm(ks[2], (DEPTH, D_MODEL, 6 * D_MODEL)) * (0.5 * D_MODEL ** -0.5),
        "b_ada": nrm(ks[3], (DEPTH, 6 * D_MODEL)) * 0.02,
        "norm1": 1.0 + 0.02 * nrm(ks[4], (DEPTH, D_MODEL)),
        "norm2": 1.0 + 0.02 * nrm(ks[5], (DEPTH, D_MODEL)),
        "w_in": nrm(ks[6], (DEPTH, D_MODEL, IN_COLS)) * D_MODEL ** -0.5,
        "b_forget": FORGET_BIAS_INIT + 0.5 * nrm(ks[7], (DEPTH, H_FOX)),
        "q_norm_fox": 1.0 + 0.02 * nrm(ks[8], (DEPTH, HEAD_DIM)),
        "k_norm_fox": 1.0 + 0.02 * nrm(ks[9], (DEPTH, HEAD_DIM)),
        "q_norm_moba": 1.0 + 0.02 * nrm(ks[10], (DEPTH, HEAD_DIM)),
        "k_norm_moba": 1.0 + 0.02 * nrm(ks[11], (DEPTH, HEAD_DIM)),
        "rel_bias": 0.2 * nrm(ks[12], (N_BUCKETS, H_MOBA)),
        "w_o": nrm(ks[13], (DEPTH, D_MODEL, D_MODEL)) * D_MODEL ** -0.5,
        "w_gate": nrm(ks[14], (DEPTH, D_MODEL, D_FF)) * D_MODEL ** -0.5,
        "w_up": nrm(ks[15], (DEPTH, D_MODEL, D_FF)) * D_MODEL ** -0.5,
        "w_down": nrm(ks[16], (DEPTH, D_FF, D_MODEL)) * D_FF ** -0.5,
    }


def reference(x, c, w_ada, b_ada, norm1, norm2, w_in, b_forget, q_norm_fox, k_norm_fox,
              q_norm_moba, k_norm_moba, rel_bias, w_o, w_gate, w_up, w_down):
    B, S, D = x.shape

    def heads(t, h):
        return jnp.transpose(t.reshape(B, S, h, HEAD_DIM), (0, 2, 1, 3))

    for l in range(DEPTH):
        mod = (jax.nn.silu(c) @ w_ada[l] + b_ada[l]).reshape(B, 6, D)[:, :, None, :]
        shift1, scale1, gate1, shift2, scale2, gate2 = [mod[:, j] for j in range(6)]

        h = rms_norm(x, norm1[l]) * (1 + scale1) + shift1
        proj = h @ w_in[l]
        o0 = 0
        fq = proj[..., o0:o0 + W_FOX]; o0 += W_FOX
        fk = proj[..., o0:o0 + W_FOX]; o0 += W_FOX
        fv = proj[..., o0:o0 + W_FOX]; o0 += W_FOX
        ff = proj[..., o0:o0 + H_FOX]; o0 += H_FOX
        mq = proj[..., o0:o0 + W_MOBA]; o0 += W_MOBA
        mk = proj[..., o0:o0 + W_MOBA]; o0 += W_MOBA
        mv = proj[..., o0:o0 + W_MOBA]

        fq = rms_norm(heads(fq, H_FOX), q_norm_fox[l])
        fk = rms_norm(heads(fk, H_FOX), k_norm_fox[l])
        fv = heads(fv, H_FOX)
        log_f = jax.nn.log_sigmoid(ff.astype(jnp.float32) + b_forget[l].astype(jnp.float32))
        log_f_cum = jnp.transpose(jnp.cumsum(log_f, axis=1), (0, 2, 1))
        fox_out = fox_attention(fq, fk, fv, log_f_cum)

        mq = rms_norm(heads(mq, H_MOBA), q_norm_moba[l])
        mk = rms_norm(heads(mk, H_MOBA), k_norm_moba[l])
        mv = heads(mv, H_MOBA)
        moba_out = moba_attention(mq, mk, mv, rel_bias)

        mix = jnp.concatenate([fox_out, moba_out], axis=-1).astype(x.dtype) @ w_o[l]
        x = x + gate1 * mix

        h2 = rms_norm(x, norm2[l]) * (1 + scale2) + shift2
        ffn = (jax.nn.silu(h2 @ w_gate[l]) * (h2 @ w_up[l])) @ w_down[l]
        x = x + gate2 * ffn
    return x
```

```python
import math
from contextlib import ExitStack

import numpy as np
import concourse.bass as bass
import concourse.mybir as mybir
from concourse.bass_utils import run_bass_kernel_spmd

F32 = mybir.dt.float32
BF16 = mybir.dt.bfloat16
ALU = mybir.AluOpType
AF = mybir.ActivationFunctionType
AX = mybir.AxisListType

S = 4096
D = 1024
NCORES = 8
DFF = 2816
NF = 22
INC = 3080
EPS = 1e-6
NEG = -32768.0
N_BUCKETS = 32
MAX_DISTANCE = 128


class EngW:
    def __init__(self, nc, es, eng, name):
        self.eng = eng
        self.name = name
        self.sem = es.enter_context(nc.semaphore("sem_" + name))
        self.count = 0
        self.waited = {}
        self.pend_r = []
        self.pend_w = []

    def wait(self, tok):
        sem, val = tok
        key = id(sem)
        if self.waited.get(key, 0) >= val:
            return
        self.eng.wait_ge(sem, val)
        self.waited[key] = val


class Buf:
    def __init__(self, name):
        self.name = name
        self.w = {}
        self.r = {}
        self.dsem = None
        self.dcount = 0

    def toks_w(self):
        return list(self.w.values())

    def toks_all(self):
        return list(self.w.values()) + list(self.r.values())


def _put(d, tok):
    k = id(tok[0])
    if k not in d or d[k][1] < tok[1]:
        d[k] = tok


class Ctx:
    def __init__(self, nc, es):
        self.nc = nc
        self.es = es
        self.pe = EngW(nc, es, nc.tensor, "pe")
        self.act = EngW(nc, es, nc.scalar, "act")
        self.dve = EngW(nc, es, nc.vector, "dve")
        self.pool = EngW(nc, es, nc.gpsimd, "pool")
        self.sp = EngW(nc, es, nc.sync, "sp")
        self.engs = [self.pe, self.act, self.dve, self.pool, self.sp]
        self.dma_bufs = []
        self.swdge_inflight = []
        self.SWDGE_MAX = 5

    def op(self, ew, fn, reads=(), writes=(), signal=True):
        for b in reads:
            for t in b.toks_w():
                ew.wait(t)
        for b in writes:
            for t in b.toks_all():
                ew.wait(t)
        ins = fn(ew.eng)
        ew.pend_r += list(reads)
        ew.pend_w += list(writes)
        if signal:
            ew.count += 1
            ins.then_inc(ew.sem, 1)
            tok = (ew.sem, ew.count)
            for b in ew.pend_r:
                _put(b.r, tok)
            for b in ew.pend_w:
                b.w = {id(tok[0]): tok}
                b.r = {}
            ew.pend_r = []
            ew.pend_w = []
        return ins

    def dma(self, ew, fn, dst, srcs=()):
        if dst.dsem is None:
            dst.dsem = self.es.enter_context(self.nc.semaphore("dsem_" + dst.name))
            self.dma_bufs.append(dst)
        for b in srcs:
            for t in b.toks_w():
                ew.wait(t)
        for k, t in list(dst.w.items()) + list(dst.r.items()):
            if t[0] is dst.dsem:
                continue
            ew.wait(t)
        if ew is self.pool:
            while len(self.swdge_inflight) >= self.SWDGE_MAX:
                b0 = self.swdge_inflight.pop(0)
                ew.wait((b0.dsem, b0.dcount))
                self.swdge_inflight = [b for b in self.swdge_inflight if b is not b0]
        ins = fn(ew.eng)
        dst.dcount += 16
        ins.then_inc(dst.dsem, 16)
        tok = (dst.dsem, dst.dcount)
        if ew is self.pool:
            self.swdge_inflight.append(dst)
        for b in srcs:
            _put(b.r, tok)
        dst.w[id(dst.dsem)] = tok
        dst.r = {}
        return ins

    def barrier(self):
        toks = []
        for e in self.engs:
            assert not e.pend_r and not e.pend_w, e.name
            if e.count > 0:
                toks.append((e.sem, e.count))
        for b in self.dma_bufs:
            toks.append((b.dsem, b.dcount))
        for e in self.engs:
            for t in toks:
                e.wait(t)


def t5_bucket_np(n):
    n = np.maximum(n, 0)
    max_exact = N_BUCKETS // 2
    nf = np.maximum(n, 1).astype(np.float32)
    large = max_exact + (np.log(nf / np.float32(max_exact)) / np.float32(math.log(MAX_DISTANCE / max_exact))
                         * np.float32(N_BUCKETS - max_exact)).astype(np.int32)
    large = np.minimum(large, N_BUCKETS - 1)
    return np.where(n < max_exact, n, large)


def host_consts():
    p = np.arange(128)
    cf = np.zeros((128, 4 * 128), np.float32)
    cf[:, 0:128] = (p[:, None] + p[None, :] == 127)
    cf[:, 128:256] = (p[:, None] <= p[None, :])
    cf[:, 256:384] = 1.0
    cf[:, 384:512] = (p[:, None] == 127)
    cb = np.zeros((128, 3 * 128 + 1024), np.float32)
    cb[:, 0:128] = np.eye(128)
    cb[:, 128:256] = (p[:, None] // 64 == p[None, :] // 64)
    cb[:, 256:384] = np.where(p[:, None] <= p[None, :], 0.0, NEG)
    i = np.arange(32)[:, None]
    n = np.arange(16)[None, :]
    pm = np.where(n < i // 2, 0.0, -1e30).astype(np.float32)
    eo = np.where(n == i // 2, 0.0, -1.0).astype(np.float32)
    cb[:, 384:384 + 512] = pm.reshape(1, 512)
    cb[:, 896:896 + 512] = eo.reshape(1, 512)
    khot = (np.arange(4096)[None, :] // 256 == np.arange(16)[:, None]).astype(np.float32) * 32768.0
    tab = np.zeros((33, 384), np.float32)
    d = np.arange(384) - 127
    bk = t5_bucket_np(d)
    for j in range(384):
        if d[j] >= 0:
            tab[bk[j], j] += 1.0
            tab[31, j] -= 1.0
        else:
            tab[32, j] = NEG
    return cf, cb, khot, tab


class _Stop(Exception):
    pass


def build_program(stop=None):
    nc = bass.Bass("TRN2", target_bir_lowering=False)

    def chk(tag):
        if stop == tag:
            raise _Stop()

    def din(name, shape):
        return nc.dram_tensor(name, list(shape), F32, kind="ExternalInput").ap()

    x_d = din("x", [S, D])
    c_d = din("c_col", [128, 8])
    wada_d = din("w_ada", [D, 6 * D])
    bada_d = din("b_ada_col", [128, 48])
    norm_d = din("norm_col", [128, 16])
    win_d = din("w_in", [D, INC])
    bf_d = din("bf_bc", [128, 256])
    bfc_d = din("bf_col", [8, 1])
    qkg_d = din("qk_g", [128, 4])
    rb_d = din("rb33", [33, 8])
    wo_d = din("w_o", [D, D])
    wg_d = din("w_gate", [D, DFF])
    wu_d = din("w_up", [D, DFF])
    wd_d = din("w_down", [DFF, D])
    cf_d = din("cf32", [128, 512])
    cb_d = din("cb16", [128, 384 + 1024])
    khot_d = din("khot", [16, 4096])
    tab_d = din("t5tab", [33, 384])
    out_d = nc.dram_tensor("out", [S, D], F32, kind="ExternalOutput").ap()

    wgu_s = nc.dram_tensor("wgu_s", [NF, 128, 2, 8, 128], BF16).ap()
    wdn_s = nc.dram_tensor("wdn_s", [128, NF, D], BF16).ap()
    wo_s = nc.dram_tensor("wo_s", [128, 8, D], BF16).ap()
    fv_t = nc.dram_tensor("fv_s", [8, 384], F32)
    arow_s = nc.dram_tensor("arow_s", [8, S], BF16).ap()
    fv_s = fv_t.ap()

    win_r = win_d.rearrange("(kc p) n -> p kc n", p=128)
    x_r = x_d.rearrange("(t p) d -> p t d", p=128)
    out_r = out_d.rearrange("(t p) d -> p t d", p=128)

    with ExitStack() as es:
      c = Ctx(nc, es)
      es_h = ExitStack()
      hit = [False]
      try:
          pe, act, dve, pool, sp = c.pe, c.act, c.dve, c.pool, c.sp

          def sbt(stack, name, shape, dt, side=None):
              name = "s_" + name
              if side is None:
                  return stack.enter_context(nc.sbuf_tensor(name, list(shape), dt))
              return stack.enter_context(nc.sbuf_tensor(name, list(shape), dt, side=side))

          pb = [es.enter_context(nc.psum_tensor("pb%d" % i, [128, 512], F32)) for i in range(8)]
          pb16 = [t.bitcast(BF16) for t in pb]
          PB = [Buf("pb%d" % i) for i in range(8)]

          ident = sbt(es, "ident", [128, 128], BF16); IDENT = Buf("ident")
          modcol = sbt(es, "modcol", [128, 48], F32); MODCOL = Buf("modcol")
          normcol = sbt(es, "normcol", [128, 16], F32); NORMCOL = Buf("normcol")
          AB = sbt(es, "AB", [128, 32], F32); ABB = Buf("AB")
          onesf = sbt(es, "onesf", [128, 128], F32); ONESF = Buf("onesf")
          identf = sbt(es, "identf", [128, 128], F32); IDENTF = Buf("identf")

          c.dma(pool, lambda e: e.dma_start(out=ident[:], in_=cb_d[:, 0:128]), IDENT)
          c.dma(sp, lambda e: e.dma_start(out=normcol[:], in_=norm_d), NORMCOL)
          c.dma(sp, lambda e: e.dma_start(out=onesf[:], in_=cf_d[:, 256:384]), ONESF)
          c.dma(sp, lambda e: e.dma_start(out=identf[:], in_=cb_d[:, 0:128]), IDENTF)

          hT = sbt(es_h, "hT", [128, 8, S], BF16, side="right")
          HT = [Buf("hT%d" % g) for g in range(8)]

          with ExitStack() as e1:
              try:
                  cc = sbt(e1, "cc", [128, 8], F32); CC = Buf("cc")
                  silc = sbt(e1, "silc", [128, 8], F32); SILC = Buf("silc")
                  acc = sbt(e1, "acc", [128, 2048], F32); ACC = Buf("acc")
                  wa = [sbt(e1, "wa%d" % i, [128, 2048], F32) for i in range(2)]
                  WA = [Buf("wa%d" % i) for i in range(2)]
                  badac = sbt(e1, "badac", [128, 48], F32); BADAC = Buf("badac")
                  xt = [sbt(e1, "xt%d" % i, [128, 4, D], F32) for i in range(2)]
                  XT = [Buf("xt%d" % i) for i in range(2)]
                  xs = [sbt(e1, "xs%d" % i, [128, 4, D], BF16) for i in range(2)]
                  XS = [Buf("xs%d" % i) for i in range(2)]
                  junk = sbt(e1, "junk", [128, D], BF16); JUNK = Buf("junk")
                  ss = sbt(e1, "ss", [128, 32], F32); SS = Buf("ss")
                  lnv = sbt(e1, "lnv", [128, 32], F32); LNV = Buf("lnv")
                  rstd = sbt(e1, "rstd", [128, 32], F32); RSTD = Buf("rstd")

                  c.dma(sp, lambda e: e.dma_start(out=cc[:], in_=c_d), CC)
                  c.dma(sp, lambda e: e.dma_start(out=badac[:], in_=bada_d), BADAC)
                  c.op(act, lambda e: e.activation(silc[:], cc[:], AF.Silu), [CC], [SILC])

                  wcnt = [0]

                  def mod_chunk(ck):
                      for kc in range(8):
                          wi = wcnt[0] % 2
                          wcnt[0] += 1
                          c.dma(sp, lambda e: e.dma_start(out=wa[wi][:], in_=wada_d[kc * 128:(kc + 1) * 128,
                                                                                ck * 2048:(ck + 1) * 2048]), WA[wi])
                          if kc == 0:
                              c.op(dve, lambda e: e.tensor_scalar(acc[:], wa[wi][:], silc[:, 0:1], None, ALU.mult),
                                   [WA[wi], SILC], [ACC])
                          else:
                              c.op(dve, lambda e: e.scalar_tensor_tensor(acc[:], wa[wi][:], silc[:, kc:kc + 1], acc[:],
                                                                         ALU.mult, ALU.add),
                                   [WA[wi], SILC, ACC], [ACC])
                      for j in range(16):
                          c.op(pe, lambda e: e.matmul(pb[7][:, j:j + 1], acc[:, j * 128:(j + 1) * 128], onesf[:, 0:1],
                                                      start=True, stop=True),
                               [ACC, ONESF], [PB[7]], signal=(j == 15))
                      c.op(dve, lambda e: e.tensor_tensor(modcol[:, ck * 16:(ck + 1) * 16], pb[7][:, 0:16],
                                                          badac[:, ck * 16:(ck + 1) * 16], ALU.add),
                           [PB[7], BADAC], [MODCOL])

                  mod_chunk(0)
                  c.op(dve, lambda e: e.scalar_tensor_tensor(AB[:, 0:8], modcol[:, 8:16], 1.0, normcol[:, 0:8],
                                                             ALU.add, ALU.mult), [MODCOL, NORMCOL], [ABB])
                  c.op(dve, lambda e: e.tensor_copy(AB[:, 8:16], modcol[:, 0:8]), [MODCOL], [ABB])

                  for g in range(8):
                      bi = g % 2
                      c.dma(sp, lambda e: e.dma_start(out=xt[bi][:], in_=x_r[:, 4 * g:4 * g + 4, :]), XT[bi])
                      for s in range(4):
                          c.op(act, lambda e: e.activation(junk[:], xt[bi][:, s, :], AF.Square,
                                                           accum_out=ss[:, 4 * g + s:4 * g + s + 1]),
                               [XT[bi]], [JUNK, SS])
                      c.op(act, lambda e: e.activation(lnv[:, 4 * g:4 * g + 4], ss[:, 4 * g:4 * g + 4], AF.Ln,
                                                       bias=EPS, scale=1.0 / D), [SS], [LNV])
                      c.op(act, lambda e: e.activation(rstd[:, 4 * g:4 * g + 4], lnv[:, 4 * g:4 * g + 4], AF.Exp,
                                                       scale=-0.5), [LNV], [RSTD])
                      for s in range(4):
                          c.op(act, lambda e: e.activation(xs[bi][:, s, :], xt[bi][:, s, :], AF.Copy,
                                                           scale=rstd[:, 4 * g + s:4 * g + s + 1]),
                               [XT[bi], RSTD], [XS[bi]])
                      for kc in range(8):
                          bank = (g % 2) * 4 + kc // 2
                          half = kc % 2
                          for s in range(4):
                              c.op(pe, lambda e: e.transpose(pb16[bank][:, half * 512 + s * 128: half * 512 + (s + 1) * 128],
                                                             xs[bi][:, s, kc * 128:(kc + 1) * 128], ident[:]),
                                   [XS[bi], IDENT], [PB[bank]], signal=(s == 3))
                          c.op(dve, lambda e: e.tensor_scalar(hT[:, kc, g * 512:(g + 1) * 512],
                                                              pb16[bank][:, half * 512:(half + 1) * 512],
                                                              AB[:, kc:kc + 1], AB[:, 8 + kc:9 + kc], ALU.mult, ALU.add),
                               [PB[bank], ABB], [HT[g]])

                  mod_chunk(1)
                  mod_chunk(2)
                  c.op(dve, lambda e: e.scalar_tensor_tensor(AB[:, 16:24], modcol[:, 32:40], 1.0, normcol[:, 8:16],
                                                             ALU.add, ALU.mult), [MODCOL, NORMCOL], [ABB])
                  c.op(dve, lambda e: e.tensor_copy(AB[:, 24:32], modcol[:, 24:32]), [MODCOL], [ABB])
                  c.barrier()
                  chk("p1")
              except _Stop:
                  c.barrier()
                  hit[0] = True
          if hit[0]:
              raise _Stop()

          mixT = sbt(es, "mixT", [128, 8, S], BF16); MIXT = [Buf("mixT%d" % g) for g in range(8)]
          print("sbuf remaining before ATT scope:", nc.sbuf_bytes_remaining)

          WSCR = Buf("wscr")
          conv_jobs = []
          wg_r = wg_d.rearrange("(kc p) n -> p kc n", p=128)
          wu_r = wu_d.rearrange("(kc p) n -> p kc n", p=128)
          for f in range(NF):
              conv_jobs.append(lambda e, f=f: e.dma_start(out=wgu_s[f, :, 0, :, :], in_=wg_r[:, :, f * 128:(f + 1) * 128]))
              conv_jobs.append(lambda e, f=f: e.dma_start(out=wgu_s[f, :, 1, :, :], in_=wu_r[:, :, f * 128:(f + 1) * 128]))
          wd_r = wd_d.rearrange("(f p) n -> p f n", p=128)
          for f in range(0, NF, 2):
              conv_jobs.append(lambda e, f=f: e.dma_start(out=wdn_s[:, f:f + 2, :], in_=wd_r[:, f:f + 2, :]))
          wo_r = wo_d.rearrange("(kc p) n -> p kc n", p=128)
          for kc in range(0, 8, 2):
              conv_jobs.append(lambda e, kc=kc: e.dma_start(out=wo_s[:, kc:kc + 2, :], in_=wo_r[:, kc:kc + 2, :]))
          conv_pos = [0]

          def emit_conv(n):
              for _ in range(n):
                  if conv_pos[0] < len(conv_jobs):
                      c.dma(pool, conv_jobs[conv_pos[0]], WSCR)
                      conv_pos[0] += 1

          with ExitStack() as e2:
              try:
                  qb = sbt(e2, "qb", [128, S], BF16); QB = Buf("qb")
                  kb = sbt(e2, "kb", [128, S], BF16); KB = Buf("kb")
                  VO = [sbt(e2, "vo%d" % i, [128, 32, 128], BF16) for i in range(2)]
                  VOB = [Buf("vo%d" % i) for i in range(2)]
                  Pt = [sbt(e2, "pt%d" % i, [128, 512], BF16) for i in range(4)]
                  PT = [Buf("pt%d" % i) for i in range(4)]
                  sq = [sbt(e2, "sq%d" % i, [128, 512], BF16) for i in range(2)]
                  SQ = [Buf("sq%d" % i) for i in range(2)]
                  rs = sbt(e2, "rs", [128, 512], F32); RS = Buf("rs")
                  wbuf = [sbt(e2, "wb%d" % i, [128, 3, 8, 128], BF16) for i in range(2)]
                  WBUF = [Buf("wb%d" % i) for i in range(2)]
                  wf = sbt(e2, "wf", [128, 8, 8], BF16); WF = Buf("wf")
                  qkg = sbt(e2, "qkg", [128, 4], F32); QKG = Buf("qkg")
                  bfbc = sbt(e2, "bfbc", [128, 256], F32); BFBC = Buf("bfbc")
                  zl = sbt(e2, "zl", [128, 256], F32); ZL = Buf("zl")
                  Tsb = sbt(e2, "Tsb", [128, 256], F32); TSB = Buf("Tsb")
                  cs = sbt(e2, "cs", [128, 256], F32); CS = Buf("cs")
                  Gs = sbt(e2, "Gs", [128, 256], F32); GS_ = Buf("Gs")
                  GL = sbt(e2, "GL", [128, 256], F32); GLB = Buf("GL")
                  biasF = sbt(e2, "biasF", [128, 2, 8, 32], F32); BIASF = Buf("biasF")
                  T2 = sbt(e2, "T2", [128, 2, 256], BF16); T2B = [Buf("T2_0"), Buf("T2_1")]
                  Hh = sbt(e2, "Hh", [128, 256], F32); HHB = Buf("Hh")
                  rb33 = sbt(e2, "rb33", [33, 8], F32); RB33 = Buf("rb33")
                  tab = sbt(e2, "tab", [33, 384], F32); TAB = Buf("tab")
                  fvsb = sbt(e2, "fvsb", [8, 384], F32); FVSB = Buf("fvsb")
                  cf = sbt(e2, "cf", [128, 256], F32); CF = Buf("cf")
                  sel127 = sbt(e2, "sel127", [128, 128], F32); SEL127 = Buf("sel127")
                  cbb = sbt(e2, "cbb", [128, 256 + 1024], BF16); CBB = Buf("cbb")
                  gsb = sbt(e2, "gsb", [128, 512], F32); GSB = Buf("gsb")
                  m8 = sbt(e2, "m8", [128, 256], F32); M8 = Buf("m8")
                  thr = sbt(e2, "thr", [128, 32], F32); THR = Buf("thr")
                  selb = sbt(e2, "selb", [128, 512], BF16); SELB = Buf("selb")
                  kms = sbt(e2, "kms", [64, 16], F32); KMS = Buf("kms")
                  kmT = sbt(e2, "kmT", [64, 16], BF16); KMT = Buf("kmT")
                  rden = sbt(e2, "rden", [128, 512], F32); RDEN = Buf("rden")
                  FV = Buf("fv_dram")
                  ones8 = sbt(e2, "ones8", [8, 512], F32); ONES8 = Buf("ones8")
                  nbf = sbt(e2, "nbf", [8, 1], F32); NBF = Buf("nbf")
                  AROW = Buf("arow")
                  print("sbuf remaining inside ATT scope:", nc.sbuf_bytes_remaining)

                  J = cf[:, 0:128]
                  tri = cf[:, 128:256]
                  blockones = cbb[:, 0:128]
                  causal = cbb[:, 128:256]
                  pmask = cbb[:, 256:768]
                  eown = cbb[:, 768:1280]

                  c.dma(sp, lambda e: e.dma_start(out=cf[:], in_=cf_d[:, 0:256]), CF)
                  c.dma(sp, lambda e: e.dma_start(out=sel127[:], in_=cf_d[:, 384:512]), SEL127)
                  c.dma(pool, lambda e: e.dma_start(out=cbb[:], in_=cb_d[:, 128:128 + 1280]), CBB)
                  c.dma(sp, lambda e: e.dma_start(out=qkg[:], in_=qkg_d), QKG)
                  c.dma(sp, lambda e: e.dma_start(out=bfbc[:], in_=bf_d), BFBC)
                  c.dma(sp, lambda e: e.dma_start(out=rb33[:], in_=rb_d), RB33)
                  c.dma(sp, lambda e: e.dma_start(out=tab[:], in_=tab_d), TAB)
                  c.dma(pool, lambda e: e.dma_start(out=wf[:], in_=win_r[:, :, 1536:1544]), WF)
                  c.op(dve, lambda e: e.tensor_scalar(qkg[:, 0:1], qkg[:, 0:1], 0.125, None, ALU.mult), [QKG], [QKG])
                  c.op(dve, lambda e: e.tensor_scalar(qkg[:, 2:3], qkg[:, 2:3], 0.125, None, ALU.mult), [QKG], [QKG])
                  c.op(pool, lambda e: e.memset(VO[0][:, :, 64:128], 1.0), [], [VOB[0]])
                  c.op(pool, lambda e: e.memset(VO[1][:, :, 0:64], 1.0), [], [VOB[1]])

                  def load_pair_weights(hp, wi):
                      if hp < 4:
                          cq, ck, cv = hp * 128, 512 + hp * 128, 1024 + hp * 128
                      else:
                          cq, ck, cv = 1544 + (hp - 4) * 128, 2056 + (hp - 4) * 128, 2568 + (hp - 4) * 128
                      for j, c0 in enumerate((cq, ck, cv)):
                          c.dma(pool, lambda e: e.dma_start(out=wbuf[wi][:, j, :, :], in_=win_r[:, :, c0:c0 + 128]), WBUF[wi])

                  load_pair_weights(0, 0)

                  c.op(pe, lambda e: e.matmul(pb[6][0:8, 0:384], rb33[0:33, 0:8], tab[0:33, 0:384], start=True, stop=True),
                       [RB33, TAB], [PB[6]])
                  c.op(dve, lambda e: e.tensor_copy(fvsb[:], pb[6][0:8, 0:384]), [PB[6]], [FVSB])
                  c.dma(sp, lambda e: e.dma_start(out=fv_s, in_=fvsb[:]), FV, [FVSB])

                  def build_t2_load(h):
                      src = bass.AP(fv_t, h * 384, [[1, 128], [1, 256]])
                      c.dma(sp, lambda e: e.dma_start(out=Hh[:], in_=src), HHB, [FV])

                  def build_t2_finish(ti):
                      c.op(pe, lambda e: e.matmul(pb[7][:, 0:256], J, Hh[:], start=True, stop=True), [CF, HHB], [PB[7]])
                      c.op(dve, lambda e: e.tensor_copy(T2[:, ti, :], pb[7][:, 0:256]), [PB[7]], [T2B[ti]])

                  for t in range(32):
                      for kc in range(8):
                          c.op(pe, lambda e: e.matmul(pb[7][:, t * 8:(t + 1) * 8], hT[:, kc, t * 128:(t + 1) * 128],
                                                      wf[:, kc, :], start=(kc == 0), stop=(kc == 7)),
                               [HT[t // 4], WF], [PB[7]], signal=(kc == 7 and t % 4 == 3))
                  c.op(dve, lambda e: e.tensor_tensor(zl[:], pb[7][:, 0:256], bfbc[:], ALU.add), [PB[7], BFBC], [ZL])
                  c.op(act, lambda e: e.activation(zl[:], zl[:], AF.Exp, scale=-1.0), [ZL], [ZL])
                  c.op(act, lambda e: e.activation(zl[:], zl[:], AF.Ln, bias=1.0), [ZL], [ZL])
                  c.op(pe, lambda e: e.matmul(pb[7][:, 0:256], tri, zl[:], start=True, stop=True), [CF, ZL], [PB[7]])
                  c.op(pe, lambda e: e.matmul(pb[6][:, 0:256], onesf[:], zl[:], start=True, stop=True), [ONESF, ZL], [PB[6]])
                  c.op(dve, lambda e: e.tensor_copy(Tsb[:], pb[6][:, 0:256]), [PB[6]], [TSB])
                  T3 = Tsb[:].rearrange("p (t h) -> p t h", h=8)
                  cs3 = cs[:].rearrange("p (t h) -> p t h", h=8)
                  for h in range(8):
                      c.op(dve, lambda e: e.tensor_tensor_scan(cs3[:, :, h], onesf[:, 0:32], T3[:, :, h], 0.0,
                                                               ALU.mult, ALU.add), [ONESF, TSB], [CS])
                  c.op(dve, lambda e: e.tensor_tensor(Gs[:], pb[7][:, 0:256], cs[:], ALU.add), [PB[7], CS], [GS_])
                  c.op(dve, lambda e: e.tensor_tensor(Gs[:], Gs[:], Tsb[:], ALU.subtract), [GS_, TSB], [GS_])
                  c.op(pe, lambda e: e.matmul(pb[6][:, 0:256], sel127[:], Gs[:], start=True, stop=True), [SEL127, GS_], [PB[6]])
                  c.op(dve, lambda e: e.tensor_copy(GL[:], pb[6][:, 0:256]), [PB[6]], [GLB])
                  G3 = Gs[:].rearrange("p (t h) -> p t h", h=8)
                  c.op(pool, lambda e: e.memset(ones8[:], 1.0), [], [ONES8])
                  c.dma(sp, lambda e: e.dma_start(out=nbf[:], in_=bfc_d), NBF)
                  c.op(dve, lambda e: e.tensor_scalar(nbf[:], nbf[:], -1.0, None, ALU.mult), [NBF], [NBF])
                  for g in range(8):
                      for kc in range(8):
                          c.op(pe, lambda e: e.matmul(pb[5][0:8, :], wf[:, kc, :], hT[:, kc, g * 512:(g + 1) * 512],
                                                      start=(kc == 0), stop=(kc == 7)),
                               [HT[g], WF], [PB[5]], signal=(kc == 7))
                      c.op(act, lambda e: e.activation(rs[0:8, :], pb[5][0:8, :], AF.Exp, bias=nbf[0:8, 0:1], scale=-1.0),
                           [PB[5], NBF], [RS])
                      c.op(act, lambda e: e.activation(rs[0:8, :], rs[0:8, :], AF.Ln, bias=1.0), [RS], [RS])
                      c.op(dve, lambda e: e.tensor_tensor_scan(rden[0:8, :], ones8[:], rs[0:8, :], 0.0, ALU.mult, ALU.add),
                           [ONES8, RS], [RDEN])
                      c.op(dve, lambda e: e.tensor_scalar(sq[0][0:8, :], rden[0:8, :], -1.0, rden[0:8, 511:512],
                                                          ALU.mult, ALU.add), [RDEN], [SQ[0]])
                      c.dma(sp, lambda e: e.dma_start(out=arow_s[:, g * 512:(g + 1) * 512], in_=sq[0][0:8, :]), AROW, [SQ[0]])
                  chk("setup")

                  PBANKS = [0, 1, 4, 5]

                  def proj_qk(dst, DST, wap_fn, np_, gcol, hp, dst2=None, DST2=None):
                      def emit_proj(g):
                          bank = PBANKS[g % 4]
                          for kc in range(8):
                              c.op(pe, lambda e: e.matmul(pb[bank][0:np_, :], wap_fn(kc), hT[:, kc, g * 512:(g + 1) * 512],
                                                          start=(kc == 0), stop=(kc == 7)),
                                   [HT[g], WBUF[hp % 2]], [PB[bank]], signal=(kc == 7))

                      def emit_sq(g):
                          bank = PBANKS[g % 4]
                          c.op(act, lambda e: e.activation(sq[g % 2][0:np_, :], pb[bank][0:np_, :], AF.Square),
                               [PB[bank]], [SQ[g % 2]])

                      emit_proj(0)
                      emit_proj(1)
                      emit_sq(0)
                      for g in range(8):
                          bank = PBANKS[g % 4]
                          sbank = 2 + g % 2
                          c.op(pe, lambda e: e.matmul(pb[sbank][0:np_, :], blockones[0:np_, 0:np_], sq[g % 2][0:np_, :],
                                                      start=True, stop=True), [CBB, SQ[g % 2]], [PB[sbank]])
                          if g + 2 < 8:
                              emit_proj(g + 2)
                          if g + 1 < 8:
                              emit_sq(g + 1)
                          c.op(act, lambda e: e.activation(rs[0:np_, :], pb[sbank][0:np_, :], AF.Ln, bias=EPS,
                                                           scale=1.0 / 64), [PB[sbank]], [RS])
                          c.op(act, lambda e: e.activation(rs[0:np_, :], rs[0:np_, :], AF.Exp, scale=-0.5), [RS], [RS])
                          if dst2 is None:
                              c.op(dve, lambda e: e.scalar_tensor_tensor(dst[0:np_, g * 512:(g + 1) * 512], pb[bank][0:np_, :],
                                                                         qkg[0:np_, gcol:gcol + 1], rs[0:np_, :],
                                                                         ALU.mult, ALU.mult),
                                   [PB[bank], QKG, RS], [DST])
                          else:
                              c.op(dve, lambda e: e.scalar_tensor_tensor(dst[0:64, g * 512:(g + 1) * 512], pb[bank][0:64, :],
                                                                         qkg[0:64, gcol:gcol + 1], rs[0:64, :],
                                                                         ALU.mult, ALU.mult),
                                   [PB[bank], QKG, RS], [DST])
                              c.op(dve, lambda e: e.scalar_tensor_tensor(dst2[0:64, g * 512:(g + 1) * 512], pb[bank][64:128, :],
                                                                         qkg[64:128, gcol:gcol + 1], rs[64:128, :],
                                                                         ALU.mult, ALU.mult),
                                   [PB[bank], QKG, RS], [DST2])

                  def proj_v(hp):
                      wi = hp % 2
                      for t4 in range(8):
                          bank = 2 + t4 % 2
                          for s in range(4):
                              t = t4 * 4 + s
                              for kc in range(8):
                                  c.op(pe, lambda e: e.matmul(pb[bank][:, s * 128:(s + 1) * 128],
                                                              hT[:, kc, t * 128:(t + 1) * 128], wbuf[wi][:, 2, kc, :],
                                                              start=(kc == 0), stop=(kc == 7)),
                                       [HT[t // 4], WBUF[wi]], [PB[bank]], signal=(kc == 7 and s == 3))
                          pv = pb[bank][:].rearrange("p (t c) -> p t c", c=128)
                          c.op(act, lambda e: e.copy(VO[0][:, t4 * 4:t4 * 4 + 4, 0:64], pv[:, :, 0:64]), [PB[bank]], [VOB[0]])
                          c.op(act, lambda e: e.copy(VO[1][:, t4 * 4:t4 * 4 + 4, 64:128], pv[:, :, 64:128]),
                               [PB[bank]], [VOB[1]])

                  def vaug(hl, kt):
                      return VO[hl][:, kt, :]

                  state = {"item": 0, "qt": 0}

                  def run_attention(heads, hooks=None):
                      items = []
                      for hd in heads:
                          for Qi in range(8):
                              for kt in range(4 * Qi + 4):
                                  items.append((hd, Qi, kt))
                      DEPTH = 3
                      meta = {}

                      def emit_qk(idx):
                          hd, Qi, kt = items[idx]
                          gi = state["item"]
                          state["item"] += 1
                          sbk = gi % 4
                          j = kt - 4 * Qi
                          c0 = 128 * j if j > 0 else 0
                          p0, nk = hd["p0"], hd["nk"]
                          extra = None
                          if hd["kind"] == "fox":
                              if j >= 0:
                                  extra = (c0, 128, causal)
                          else:
                              if j >= 0:
                                  w = min(256, 512 - c0)
                                  extra = (c0, w, hd["t2"][:, 0:w])
                              elif j == -1:
                                  extra = (0, 128, hd["t2"][:, 128:256])
                          kq, kk = hd["Q"], hd["K"]
                          c.op(pe, lambda e: e.matmul(pb[sbk][:, c0:512], kk[p0:p0 + nk, kt * 128:(kt + 1) * 128],
                                                      kq[p0:p0 + nk, Qi * 512 + c0:(Qi + 1) * 512],
                                                      start=True, stop=(extra is None)),
                               [hd["KB"], hd["QB"]], [PB[sbk]], signal=(extra is None))
                          if extra is not None:
                              ec0, ew_, eap = extra
                              c.op(pe, lambda e: e.matmul(pb[sbk][:, ec0:ec0 + ew_], ident[:], eap, start=False, stop=True),
                                   [IDENT, CBB] + ([hd["T2B"]] if hd.get("T2B") is not None else []), [PB[sbk]])
                          bias = hd["bias"](Qi, kt)
                          if bias is None:
                              c.op(act, lambda e: e.activation(Pt[sbk][:, c0:512], pb[sbk][:, c0:512], AF.Exp),
                                   [PB[sbk]], [PT[sbk]])
                          else:
                              c.op(act, lambda e: e.activation(Pt[sbk][:, c0:512], pb[sbk][:, c0:512], AF.Exp, bias=bias),
                                   [PB[sbk], BIASF], [PT[sbk]])
                          meta[idx] = (sbk, c0)

                      def emit_pv(idx):
                          hd, Qi, kt = items[idx]
                          sbk, c0 = meta.pop(idx)
                          if kt == 0:
                              hd["obank"] = 4 + state["qt"] % 3
                              state["qt"] += 1
                          ob = hd["obank"]
                          last = (kt == 4 * Qi + 3)
                          c.op(pe, lambda e: e.matmul(pb[ob][:, c0:512], vaug(hd["hl"], kt), Pt[sbk][:, c0:512],
                                                      start=(kt == 0), stop=last),
                               [VOB[hd["hl"]], PT[sbk]], [PB[ob]])
                          if last:
                              if hd["hl"] == 0:
                                  orow, drow = slice(0, 64), slice(64, 128)
                              else:
                                  orow, drow = slice(64, 128), slice(0, 64)
                              c.op(dve, lambda e: e.reciprocal(rden[orow, :], pb[ob][drow, :]), [PB[ob]], [RDEN])
                              c.op(dve, lambda e: e.tensor_tensor(mixT[orow, hd["hp"], Qi * 512:(Qi + 1) * 512],
                                                                  pb[ob][orow, :], rden[orow, :], ALU.mult),
                                   [PB[ob], RDEN], [MIXT[hd["hp"]]])

                      for i in range(len(items) + DEPTH):
                          if hooks and i in hooks:
                              hooks[i]()
                          if i < len(items):
                              emit_qk(i)
                          if i >= DEPTH:
                              emit_pv(i - DEPTH)

                  class View2:
                      def __init__(self, t, ch):
                          self.t, self.ch = t, ch

                      def __getitem__(self, key):
                          r, cc_ = key
                          return self.t[r, self.ch, cc_]

                  qb2, kb2 = View2(mixT, 6), View2(mixT, 7)
                  QB2, KB2 = MIXT[6], MIXT[7]

                  def fox_prep(h, hl, Q, K, QBf, KBf):
                      c.op(pool, lambda e: e.memset(K[64:65, :], 1.0), [], [KBf])
                      c.dma(sp, lambda e: e.dma_start(out=Q[64:65, :], in_=arow_s[h:h + 1, :]), QBf, [AROW])
                      for Qi in range(8):
                          c.op(dve, lambda e: e.tensor_scalar(biasF[:, hl, Qi, :], G3[:, :, h],
                                                              GL[:, (4 * Qi + 3) * 8 + h:(4 * Qi + 3) * 8 + h + 1], None,
                                                              ALU.subtract), [GS_, GLB], [BIASF])

                  def fox_attn(hl, hp, Q, K, QBf, KBf):
                      heads = [dict(kind="fox", p0=0, nk=65, hl=hl, hp=hp, t2=None, Q=Q, K=K, QB=QBf, KB=KBf,
                                    bias=(lambda Qi, kt, hl=hl: biasF[:, hl, Qi, kt:kt + 1]))]
                      run_attention(heads)

                  def moba_stage1(hm, Q, K, QBf, KBf):
                      c.dma(pool, lambda e: e.dma_start(out=K[64:80, :].rearrange("p (a b) -> p a b", b=1024),
                                                        in_=khot_d.rearrange("p (a b) -> p a b", b=1024)), KBf)
                      c.op(dve, lambda e: e.tensor_reduce(kms[:, :], K[0:64, :].rearrange("p (n l) -> p n l", l=256),
                                                          AX.X, ALU.add), [KBf], [KMS])
                      c.op(dve, lambda e: e.tensor_scalar(kmT[:, :], kms[:, :], 1.0 / 256, None, ALU.mult), [KMS], [KMT])
                      build_t2_load(hm)

                  def moba_stage2(hm, Q, K, QBf, KBf):
                      for i in range(32):
                          c.op(pe, lambda e: e.matmul(pb[7][:, i * 16:(i + 1) * 16], Q[0:64, i * 128:(i + 1) * 128],
                                                      kmT[:, :], start=True, stop=True),
                               [QBf, KMT], [PB[7]], signal=(i == 31))
                      c.op(dve, lambda e: e.tensor_tensor(gsb[:], pb[7][:, :], pmask, ALU.add), [PB[7], CBB], [GSB])
                      for i in range(32):
                          c.op(dve, lambda e: e.max(m8[:, i * 8:(i + 1) * 8], gsb[:, i * 16:(i + 1) * 16]),
                               [GSB], [M8], signal=(i == 31))
                      m83 = m8[:].rearrange("p (i e) -> p i e", e=8)
                      c.op(dve, lambda e: e.tensor_scalar(thr[:], m83[:, :, 2], -1e29, None, ALU.max), [M8], [THR])
                      gs3 = gsb[:].rearrange("p (i n) -> p i n", n=16)
                      sel3 = selb[:].rearrange("p (i n) -> p i n", n=16)
                      c.op(dve, lambda e: e.tensor_tensor(sel3, gs3, thr[:].unsqueeze(2).to_broadcast([128, 32, 16]),
                                                          ALU.is_ge), [GSB, THR], [SELB])
                      c.op(dve, lambda e: e.scalar_tensor_tensor(selb[:], selb[:], -1.0, eown, ALU.add, ALU.max),
                           [SELB, CBB], [SELB])

                  def moba_stage3(hl, Q, K, QBf, KBf):
                      for i8 in range(4):
                          for i in range(8):
                              ii = i8 * 8 + i
                              c.op(pe, lambda e: e.transpose(pb16[7][0:16, i * 128:(i + 1) * 128],
                                                             selb[:, ii * 16:(ii + 1) * 16], ident[:]),
                                   [SELB, IDENT], [PB[7]], signal=(i == 7))
                          c.op(act, lambda e: e.copy(Q[64:80, i8 * 1024:(i8 + 1) * 1024], pb16[7][0:16, :]),
                               [PB[7]], [QBf])
                      build_t2_finish(hl)

                  def moba_attn(hl, hp, Q, K, QBf, KBf, hooks=None):
                      heads = [dict(kind="moba", p0=0, nk=80, hl=hl, hp=hp, t2=T2[:, hl, :], T2B=T2B[hl],
                                    Q=Q, K=K, QB=QBf, KB=KBf, bias=(lambda Qi, kt: None))]
                      run_attention(heads, hooks)

                  for hp in range(8):
                      wi = hp % 2
                      fox = hp < 4
                      gq, gk = (0, 1) if fox else (2, 3)
                      if hp + 1 < 8:
                          load_pair_weights(hp + 1, (hp + 1) % 2)
                      emit_conv(9)
                      setA = (qb, kb, QB, KB)
                      setB = (qb2, kb2, QB2, KB2)
                      if hp < 6:
                          proj_qk(qb, QB, lambda kc: wbuf[wi][:, 0, kc, :], 128, gq, hp, dst2=qb2, DST2=QB2)
                          proj_qk(kb, KB, lambda kc: wbuf[wi][:, 1, kc, :], 128, gk, hp, dst2=kb2, DST2=KB2)
                          if fox:
                              fox_prep(2 * hp, 0, *setA)
                              fox_prep(2 * hp + 1, 1, *setB)
                              proj_v(hp)
                              fox_attn(0, hp, *setA)
                              if hp == 0:
                                  chk("fox0")
                              fox_attn(1, hp, *setB)
                          else:
                              hm = 2 * (hp - 4)
                              moba_stage1(hm, *setA)
                              moba_stage2(hm, *setA)
                              proj_v(hp)
                              moba_stage3(0, *setA)
                              hooks = {2: (lambda: moba_stage1(hm + 1, *setB)),
                                       30: (lambda: moba_stage2(hm + 1, *setB)),
                                       70: (lambda: moba_stage3(1, *setB))}
                              moba_attn(0, hp, *setA, hooks=hooks)
                              if hp == 4:
                                  chk("moba0")
                              moba_attn(1, hp, *setB)
                      else:
                          for hl in range(2):
                              proj_qk(qb, QB, lambda kc: wbuf[wi][:, 0, kc, 64 * hl:64 * hl + 64], 64, gq, hp)
                              proj_qk(kb, KB, lambda kc: wbuf[wi][:, 1, kc, 64 * hl:64 * hl + 64], 64, gk, hp)
                              hm = 2 * (hp - 4) + hl
                              moba_stage1(hm, *setA)
                              moba_stage2(hm, *setA)
                              if hl == 0:
                                  proj_v(hp)
                              moba_stage3(hl, *setA)
                              moba_attn(hl, hp, *setA)
                  emit_conv(100)
                  c.barrier()
              except _Stop:
                  c.barrier()
                  hit[0] = True
          if hit[0]:
              raise _Stop()
          es_h.close()

          with ExitStack() as e3:
              try:
                  wo = sbt(e3, "wo", [128, 8, D], BF16); WO = Buf("wo")
                  wdn = sbt(e3, "wdn", [128, NF, D], BF16); WDN = Buf("wdn")
                  x1 = [sbt(e3, "x1_%d" % i, [128, D], F32) for i in range(6)]
                  X1 = [Buf("x1_%d" % i) for i in range(6)]
                  h2T = sbt(e3, "h2T", [128, 8, 512], BF16); H2T = Buf("h2T")
                  actT = sbt(e3, "actT", [128, NF, 512], BF16); ACTT = Buf("actT")
                  gbc = sbt(e3, "gbc", [128, D], F32); GBC = Buf("gbc")
                  sg = [sbt(e3, "sg%d" % i, [128, 512], F32) for i in range(2)]
                  SG = [Buf("sg%d" % i) for i in range(2)]
                  xs2 = sbt(e3, "xs2", [128, D], BF16); XS2 = Buf("xs2")
                  wgu = [sbt(e3, "wgu%d" % i, [128, 2, 8, 128], BF16) for i in range(3)]
                  WGU = [Buf("wgu%d" % i) for i in range(3)]
                  junk2 = sbt(e3, "junk2", [128, D], BF16); JUNK2 = Buf("junk2")
                  ss2 = sbt(e3, "ss2", [128, 32], F32); SS2 = Buf("ss2")
                  ln2 = sbt(e3, "ln2", [128, 32], F32); LN2 = Buf("ln2")
                  rstd2 = sbt(e3, "rstd2", [128, 32], F32); RSTD2 = Buf("rstd2")
                  colrep = sbt(e3, "colrep", [128, 128], F32); COLREP = Buf("colrep")
                  OUTB = [Buf("outb%d" % i) for i in range(6)]
                  print("sbuf remaining inside FFN scope:", nc.sbuf_bytes_remaining)

                  for kc in range(0, 8, 4):
                      c.dma(sp, lambda e: e.dma_start(out=wo[:, kc:kc + 4, :], in_=wo_s[:, kc:kc + 4, :]), WO, [WSCR])
                  for f in range(0, NF, 11):
                      c.dma(sp, lambda e: e.dma_start(out=wdn[:, f:f + 11, :], in_=wdn_s[:, f:f + 11, :]), WDN, [WSCR])

                  def fold_gate(col0, W, WBUF_, nchunk):
                      for kc in range(8):
                          c.op(dve, lambda e: e.tensor_copy(colrep[:], modcol[:, col0 + kc:col0 + kc + 1].to_broadcast([128, 128])),
                               [MODCOL], [COLREP])
                          c.op(pe, lambda e: e.matmul(pb[kc // 4][:, (kc % 4) * 128:(kc % 4 + 1) * 128],
                                                      colrep[:], identf[:], start=True, stop=True),
                               [COLREP, IDENTF], [PB[kc // 4]])
                      for hh in range(2):
                          c.op(dve, lambda e: e.tensor_copy(gbc[:, hh * 512:(hh + 1) * 512], pb[hh][:, :]), [PB[hh]], [GBC])
                      for j in range(nchunk):
                          c.op(pool, lambda e: e.tensor_tensor(W[:, j, :], W[:, j, :], gbc[:], ALU.mult), [WBUF_, GBC], [WBUF_])

                  fold_gate(16, wo, WO, 8)
                  fold_gate(40, wdn, WDN, NF)
                  chk("fold")

                  wgcnt = [0]
                  for T in range(8):
                      for s in range(4):
                          u = 4 * T + s
                          xi = u % 6
                          c.dma(sp, lambda e: e.dma_start(out=x1[xi][:], in_=x_r[:, u, :]), X1[xi])
                          for ch in range(2):
                              bank = ch
                              for kc in range(8):
                                  c.op(pe, lambda e: e.matmul(pb[bank][:, :], mixT[:, kc, u * 128:(u + 1) * 128],
                                                              wo[:, kc, ch * 512:(ch + 1) * 512],
                                                              start=(kc == 0), stop=(kc == 7)),
                                       [MIXT[kc], WO], [PB[bank]], signal=(kc == 7))
                              c.op(dve, lambda e: e.tensor_tensor(x1[xi][:, ch * 512:(ch + 1) * 512], pb[bank][:, :],
                                                                  x1[xi][:, ch * 512:(ch + 1) * 512], ALU.add),
                                   [PB[bank], X1[xi]], [X1[xi]])
                          c.op(act, lambda e: e.activation(junk2[:], x1[xi][:], AF.Square, accum_out=ss2[:, u:u + 1]),
                               [X1[xi]], [JUNK2, SS2])
                      c.op(act, lambda e: e.activation(ln2[:, 4 * T:4 * T + 4], ss2[:, 4 * T:4 * T + 4], AF.Ln, bias=EPS,
                                                       scale=1.0 / D), [SS2], [LN2])
                      c.op(act, lambda e: e.activation(rstd2[:, 4 * T:4 * T + 4], ln2[:, 4 * T:4 * T + 4], AF.Exp,
                                                       scale=-0.5), [LN2], [RSTD2])
                      for s in range(4):
                          u = 4 * T + s
                          xi = u % 6
                          c.op(act, lambda e: e.activation(xs2[:], x1[xi][:], AF.Copy, scale=rstd2[:, u:u + 1]),
                               [X1[xi], RSTD2], [XS2])
                          bank = 2 + s % 2
                          for kc in range(8):
                              c.op(pe, lambda e: e.transpose(pb16[bank][:, kc * 128:(kc + 1) * 128],
                                                             xs2[:, kc * 128:(kc + 1) * 128], ident[:]),
                                   [XS2, IDENT], [PB[bank]], signal=(kc == 7))
                          for kc in range(8):
                              c.op(dve, lambda e: e.tensor_scalar(h2T[:, kc, s * 128:(s + 1) * 128],
                                                                  pb16[bank][:, kc * 128:(kc + 1) * 128],
                                                                  AB[:, 16 + kc:17 + kc], AB[:, 24 + kc:25 + kc],
                                                                  ALU.mult, ALU.add),
                                   [PB[bank], ABB], [H2T], signal=(kc == 7))
                      for f in range(NF):
                          wi = wgcnt[0] % 3
                          wgcnt[0] += 1
                          c.dma(sp, lambda e: e.dma_start(out=wgu[wi][:], in_=wgu_s[f]), WGU[wi], [WSCR])
                          gb, ub = 4 + f % 2, 6 + f % 2
                          for kc in range(8):
                              c.op(pe, lambda e: e.matmul(pb[gb][:, :], wgu[wi][:, 0, kc, :], h2T[:, kc, :],
                                                          start=(kc == 0), stop=(kc == 7)),
                                   [WGU[wi], H2T], [PB[gb]], signal=(kc == 7))
                          for kc in range(8):
                              c.op(pe, lambda e: e.matmul(pb[ub][:, :], wgu[wi][:, 1, kc, :], h2T[:, kc, :],
                                                          start=(kc == 0), stop=(kc == 7)),
                                   [WGU[wi], H2T], [PB[ub]], signal=(kc == 7))
                          c.op(act, lambda e: e.activation(sg[f % 2][:], pb[gb][:, :], AF.Silu), [PB[gb]], [SG[f % 2]])
                          c.op(dve, lambda e: e.tensor_tensor(actT[:, f, :], pb[ub][:, :], sg[f % 2][:], ALU.mult),
                               [PB[ub], SG[f % 2]], [ACTT])
                      for s in range(4):
                          u = 4 * T + s
                          xi = u % 6
                          for ch in range(2):
                              bank = ch
                              for f in range(NF):
                                  c.op(pe, lambda e: e.matmul(pb[bank][:, :], actT[:, f, s * 128:(s + 1) * 128],
                                                              wdn[:, f, ch * 512:(ch + 1) * 512],
                                                              start=(f == 0), stop=(f == NF - 1)),
                                       [ACTT, WDN], [PB[bank]], signal=(f == NF - 1))
                              c.op(dve, lambda e: e.tensor_tensor(x1[xi][:, ch * 512:(ch + 1) * 512], pb[bank][:, :],
                                                                  x1[xi][:, ch * 512:(ch + 1) * 512], ALU.add),
                                   [PB[bank], X1[xi]], [X1[xi]])
                          c.dma(pool, lambda e: e.dma_start(out=out_r[:, u, :], in_=x1[xi][:]), OUTB[xi], [X1[xi]])
                  c.barrier()
              except _Stop:
                  c.barrier()
                  hit[0] = True
          if hit[0]:
              raise _Stop()
      except _Stop:
        es_h.close()
    return nc


_CACHE = {}


def kernel(**inputs):
    x = np.ascontiguousarray(inputs["x"], dtype=np.float32)
    cvec = np.asarray(inputs["c"], dtype=np.float32)
    f32c = lambda a: np.ascontiguousarray(a, dtype=np.float32)
    w_ada = f32c(inputs["w_ada"][0])
    b_ada = np.asarray(inputs["b_ada"][0], np.float32)
    cf, cb, khot, tab = host_consts()
    colform = lambda v, n: np.ascontiguousarray(v.reshape(n, 128).T)
    norm_col = np.concatenate([colform(np.asarray(inputs["norm1"][0], np.float32), 8),
                               colform(np.asarray(inputs["norm2"][0], np.float32), 8)], axis=1)
    dup = lambda g: np.concatenate([g, g]).astype(np.float32)
    qk_g = np.stack([dup(np.asarray(inputs["q_norm_fox"][0])), dup(np.asarray(inputs["k_norm_fox"][0])),
                     dup(np.asarray(inputs["q_norm_moba"][0])), dup(np.asarray(inputs["k_norm_moba"][0]))], axis=1)
    bf_bc = np.ascontiguousarray(np.broadcast_to(np.tile(np.asarray(inputs["b_forget"][0], np.float32), 32)[None, :],
                                                 (128, 256)))
    rb33 = np.concatenate([np.asarray(inputs["rel_bias"], np.float32), np.ones((1, 8), np.float32)], axis=0)
    shared = {
        "w_ada": w_ada,
        "b_ada_col": colform(b_ada, 48),
        "norm_col": f32c(norm_col),
        "w_in": f32c(inputs["w_in"][0]),
        "bf_bc": bf_bc,
        "bf_col": np.ascontiguousarray(np.asarray(inputs["b_forget"][0], np.float32).reshape(8, 1)),
        "qk_g": f32c(qk_g),
        "rb33": f32c(rb33),
        "w_o": f32c(inputs["w_o"][0]),
        "w_gate": f32c(inputs["w_gate"][0]),
        "w_up": f32c(inputs["w_up"][0]),
        "w_down": f32c(inputs["w_down"][0]),
        "cf32": cf, "cb16": cb, "khot": khot, "t5tab": tab,
    }
    if "nc" not in _CACHE:
        _CACHE["nc"] = build_program()
    nc = _CACHE["nc"]
    in_maps = []
    for b in range(NCORES):
        m = dict(shared)
        m["x"] = x[b]
        m["c_col"] = colform(cvec[b], 8)
        in_maps.append(m)
    res = run_bass_kernel_spmd(nc, in_maps, core_ids=list(range(NCORES)))
    out = np.stack([np.asarray(res.results[b]["out"], dtype=np.float32).reshape(S, D) for b in range(NCORES)], axis=0)
    return out
```

```python
import math
from contextlib import ExitStack

import numpy as np
import concourse.bass as bass
import concourse.mybir as mybir
from concourse.bass_utils import run_bass_kernel_spmd

F32 = mybir.dt.float32
BF16 = mybir.dt.bfloat16
ALU = mybir.AluOpType
AF = mybir.ActivationFunctionType
AX = mybir.AxisListType

S = 4096
D = 1024
NCORES = 8
DFF = 2816
NF = 22
INC = 3080
EPS = 1e-6
NEG = -32768.0
N_BUCKETS = 32
MAX_DISTANCE = 128


class EngW:
    def __init__(self, nc, es, eng, name):
        self.eng = eng
        self.name = name
        self.sem = es.enter_context(nc.semaphore("sem_" + name))
        self.count = 0
        self.waited = {}
        self.pend_r = []
        self.pend_w = []

    def wait(self, tok):
        sem, val = tok
        key = id(sem)
        if self.waited.get(key, 0) >= val:
            return
        self.eng.wait_ge(sem, val)
        self.waited[key] = val


class Buf:
    def __init__(self, name):
        self.name = name
        self.w = {}
        self.r = {}
        self.dsem = None
        self.dcount = 0

    def toks_w(self):
        return list(self.w.values())

    def toks_all(self):
        return list(self.w.values()) + list(self.r.values())


def _put(d, tok):
    k = id(tok[0])
    if k not in d or d[k][1] < tok[1]:
        d[k] = tok


class Ctx:
    def __init__(self, nc, es):
        self.nc = nc
        self.es = es
        self.pe = EngW(nc, es, nc.tensor, "pe")
        self.act = EngW(nc, es, nc.scalar, "act")
        self.dve = EngW(nc, es, nc.vector, "dve")
        self.pool = EngW(nc, es, nc.gpsimd, "pool")
        self.sp = EngW(nc, es, nc.sync, "sp")
        self.engs = [self.pe, self.act, self.dve, self.pool, self.sp]
        self.dma_bufs = []
        self.swdge_inflight = []
        self.SWDGE_MAX = 5

    def op(self, ew, fn, reads=(), writes=(), signal=True):
        for b in reads:
            for t in b.toks_w():
                ew.wait(t)
        for b in writes:
            for t in b.toks_all():
                ew.wait(t)
        ins = fn(ew.eng)
        ew.pend_r += list(reads)
        ew.pend_w += list(writes)
        if signal:
            ew.count += 1
            ins.then_inc(ew.sem, 1)
            tok = (ew.sem, ew.count)
            for b in ew.pend_r:
                _put(b.r, tok)
            for b in ew.pend_w:
                b.w = {id(tok[0]): tok}
                b.r = {}
            ew.pend_r = []
            ew.pend_w = []
        return ins

    def dma(self, ew, fn, dst, srcs=()):
        if dst.dsem is None:
            dst.dsem = self.es.enter_context(self.nc.semaphore("dsem_" + dst.name))
            self.dma_bufs.append(dst)
        for b in srcs:
            for t in b.toks_w():
                ew.wait(t)
        for k, t in list(dst.w.items()) + list(dst.r.items()):
            if t[0] is dst.dsem:
                continue
            ew.wait(t)
        if ew is self.pool:
            while len(self.swdge_inflight) >= self.SWDGE_MAX:
                b0 = self.swdge_inflight.pop(0)
                ew.wait((b0.dsem, b0.dcount))
                self.swdge_inflight = [b for b in self.swdge_inflight if b is not b0]
        ins = fn(ew.eng)
        dst.dcount += 16
        ins.then_inc(dst.dsem, 16)
        tok = (dst.dsem, dst.dcount)
        if ew is self.pool:
            self.swdge_inflight.append(dst)
        for b in srcs:
            _put(b.r, tok)
        dst.w[id(dst.dsem)] = tok
        dst.r = {}
        return ins

    def barrier(self):
        toks = []
        for e in self.engs:
            assert not e.pend_r and not e.pend_w, e.name
            if e.count > 0:
                toks.append((e.sem, e.count))
        for b in self.dma_bufs:
            toks.append((b.dsem, b.dcount))
        for e in self.engs:
            for t in toks:
                e.wait(t)


def t5_bucket_np(n):
    n = np.maximum(n, 0)
    max_exact = N_BUCKETS // 2
    nf = np.maximum(n, 1).astype(np.float32)
    large = max_exact + (np.log(nf / np.float32(max_exact)) / np.float32(math.log(MAX_DISTANCE / max_exact))
                         * np.float32(N_BUCKETS - max_exact)).astype(np.int32)
    large = np.minimum(large, N_BUCKETS - 1)
    return np.where(n < max_exact, n, large)


def host_consts():
    p = np.arange(128)
    cf = np.zeros((128, 4 * 128), np.float32)
    cf[:, 0:128] = (p[:, None] + p[None, :] == 127)
    cf[:, 128:256] = (p[:, None] <= p[None, :])
    cf[:, 256:384] = 1.0
    cf[:, 384:512] = (p[:, None] == 127)
    cb = np.zeros((128, 3 * 128 + 1024), np.float32)
    cb[:, 0:128] = np.eye(128)
    cb[:, 128:256] = (p[:, None] // 64 == p[None, :] // 64)
    cb[:, 256:384] = np.where(p[:, None] <= p[None, :], 0.0, NEG)
    i = np.arange(32)[:, None]
    n = np.arange(16)[None, :]
    pm = np.where(n < i // 2, 0.0, -1e30).astype(np.float32)
    eo = np.where(n == i // 2, 0.0, -1.0).astype(np.float32)
    cb[:, 384:384 + 512] = pm.reshape(1, 512)
    cb[:, 896:896 + 512] = eo.reshape(1, 512)
    khot = (np.arange(4096)[None, :] // 256 == np.arange(16)[:, None]).astype(np.float32) * 32768.0
    tab = np.zeros((33, 384), np.float32)
    d = np.arange(384) - 127
    bk = t5_bucket_np(d)
    for j in range(384):
        if d[j] >= 0:
            tab[bk[j], j] += 1.0
            tab[31, j] -= 1.0
        else:
            tab[32, j] = NEG
    return cf, cb, khot, tab


class _Stop(Exception):
    pass


def build_program(stop=None):
    nc = bass.Bass("TRN2", target_bir_lowering=False)

    def chk(tag):
        if stop == tag:
            raise _Stop()

    def din(name, shape):
        return nc.dram_tensor(name, list(shape), F32, kind="ExternalInput").ap()

    x_d = din("x", [S, D])
    c_d = din("c_col", [128, 8])
    wada_d = din("w_ada", [D, 6 * D])
    bada_d = din("b_ada_col", [128, 48])
    norm_d = din("norm_col", [128, 16])
    win_d = din("w_in", [D, INC])
    bf_d = din("bf_bc", [128, 256])
    bfc_d = din("bf_col", [8, 1])
    qkg_d = din("qk_g", [128, 4])
    rb_d = din("rb33", [33, 8])
    wo_d = din("w_o", [D, D])
    wg_d = din("w_gate", [D, DFF])
    wu_d = din("w_up", [D, DFF])
    wd_d = din("w_down", [DFF, D])
    cf_d = din("cf32", [128, 512])
    cb_d = din("cb16", [128, 384 + 1024])
    khot_d = din("khot", [16, 4096])
    tab_d = din("t5tab", [33, 384])
    out_d = nc.dram_tensor("out", [S, D], F32, kind="ExternalOutput").ap()

    wgu_s = nc.dram_tensor("wgu_s", [NF, 128, 2, 8, 128], BF16).ap()
    wdn_s = nc.dram_tensor("wdn_s", [128, NF, D], BF16).ap()
    wo_s = nc.dram_tensor("wo_s", [128, 8, D], BF16).ap()
    fv_t = nc.dram_tensor("fv_s", [8, 384], F32)
    arow_s = nc.dram_tensor("arow_s", [8, S], BF16).ap()
    fv_s = fv_t.ap()

    win_r = win_d.rearrange("(kc p) n -> p kc n", p=128)
    x_r = x_d.rearrange("(t p) d -> p t d", p=128)
    out_r = out_d.rearrange("(t p) d -> p t d", p=128)

    with ExitStack() as es:
      c = Ctx(nc, es)
      es_h = ExitStack()
      hit = [False]
      try:
          pe, act, dve, pool, sp = c.pe, c.act, c.dve, c.pool, c.sp

          def sbt(stack, name, shape, dt, side=None):
              name = "s_" + name
              if side is None:
                  return stack.enter_context(nc.sbuf_tensor(name, list(shape), dt))
              return stack.enter_context(nc.sbuf_tensor(name, list(shape), dt, side=side))

          pb = [es.enter_context(nc.psum_tensor("pb%d" % i, [128, 512], F32)) for i in range(8)]
          pb16 = [t.bitcast(BF16) for t in pb]
          PB = [Buf("pb%d" % i) for i in range(8)]

          ident = sbt(es, "ident", [128, 128], BF16); IDENT = Buf("ident")
          modcol = sbt(es, "modcol", [128, 48], F32); MODCOL = Buf("modcol")
          normcol = sbt(es, "normcol", [128, 16], F32); NORMCOL = Buf("normcol")
          AB = sbt(es, "AB", [128, 32], F32); ABB = Buf("AB")
          onesf = sbt(es, "onesf", [128, 128], F32); ONESF = Buf("onesf")
          identf = sbt(es, "identf", [128, 128], F32); IDENTF = Buf("identf")
          silc = sbt(es, "silc", [128, 8], F32); SILC = Buf("silc")
          badac = sbt(es, "badac", [128, 48], F32); BADAC = Buf("badac")

          c.dma(pool, lambda e: e.dma_start(out=ident[:], in_=cb_d[:, 0:128]), IDENT)
          c.dma(sp, lambda e: e.dma_start(out=normcol[:], in_=norm_d), NORMCOL)
          c.dma(sp, lambda e: e.dma_start(out=onesf[:], in_=cf_d[:, 256:384]), ONESF)
          c.dma(sp, lambda e: e.dma_start(out=identf[:], in_=cb_d[:, 0:128]), IDENTF)

          hT = sbt(es_h, "hT", [128, 8, S], BF16, side="right")
          HT = [Buf("hT%d" % g) for g in range(8)]

          with ExitStack() as e1:
              try:
                  cc = sbt(e1, "cc", [128, 8], F32); CC = Buf("cc")
                  acc = sbt(e1, "acc", [128, 2048], F32); ACC = Buf("acc")
                  wa = [sbt(e1, "wa%d" % i, [128, 2048], F32) for i in range(2)]
                  WA = [Buf("wa%d" % i) for i in range(2)]
                  xt = [sbt(e1, "xt%d" % i, [128, 4, D], F32) for i in range(2)]
                  XT = [Buf("xt%d" % i) for i in range(2)]
                  xs = [sbt(e1, "xs%d" % i, [128, 4, D], BF16) for i in range(2)]
                  XS = [Buf("xs%d" % i) for i in range(2)]
                  junk = sbt(e1, "junk", [128, D], BF16); JUNK = Buf("junk")
                  ss = sbt(e1, "ss", [128, 32], F32); SS = Buf("ss")
                  lnv = sbt(e1, "lnv", [128, 32], F32); LNV = Buf("lnv")
                  rstd = sbt(e1, "rstd", [128, 32], F32); RSTD = Buf("rstd")

                  c.dma(sp, lambda e: e.dma_start(out=cc[:], in_=c_d), CC)
                  c.dma(sp, lambda e: e.dma_start(out=badac[:], in_=bada_d), BADAC)
                  c.op(act, lambda e: e.activation(silc[:], cc[:], AF.Silu), [CC], [SILC])

                  wcnt = [0]

                  def mod_chunk(ck):
                      for kc in range(8):
                          wi = wcnt[0] % 2
                          wcnt[0] += 1
                          c.dma(sp, lambda e: e.dma_start(out=wa[wi][:], in_=wada_d[kc * 128:(kc + 1) * 128,
                                                                                ck * 2048:(ck + 1) * 2048]), WA[wi])
                          if kc == 0:
                              c.op(dve, lambda e: e.tensor_scalar(acc[:], wa[wi][:], silc[:, 0:1], None, ALU.mult),
                                   [WA[wi], SILC], [ACC])
                          else:
                              c.op(dve, lambda e: e.scalar_tensor_tensor(acc[:], wa[wi][:], silc[:, kc:kc + 1], acc[:],
                                                                         ALU.mult, ALU.add),
                                   [WA[wi], SILC, ACC], [ACC])
                      for j in range(16):
                          c.op(pe, lambda e: e.matmul(pb[7][:, j:j + 1], acc[:, j * 128:(j + 1) * 128], onesf[:, 0:1],
                                                      start=True, stop=True),
                               [ACC, ONESF], [PB[7]], signal=(j == 15))
                      c.op(dve, lambda e: e.tensor_tensor(modcol[:, ck * 16:(ck + 1) * 16], pb[7][:, 0:16],
                                                          badac[:, ck * 16:(ck + 1) * 16], ALU.add),
                           [PB[7], BADAC], [MODCOL])

                  mod_chunk(0)
                  c.op(dve, lambda e: e.scalar_tensor_tensor(AB[:, 0:8], modcol[:, 8:16], 1.0, normcol[:, 0:8],
                                                             ALU.add, ALU.mult), [MODCOL, NORMCOL], [ABB])
                  c.op(dve, lambda e: e.tensor_copy(AB[:, 8:16], modcol[:, 0:8]), [MODCOL], [ABB])

                  for g in range(8):
                      bi = g % 2
                      c.dma(sp, lambda e: e.dma_start(out=xt[bi][:], in_=x_r[:, 4 * g:4 * g + 4, :]), XT[bi])
                      for s in range(4):
                          c.op(act, lambda e: e.activation(junk[:], xt[bi][:, s, :], AF.Square,
                                                           accum_out=ss[:, 4 * g + s:4 * g + s + 1]),
                               [XT[bi]], [JUNK, SS])
                      c.op(act, lambda e: e.activation(lnv[:, 4 * g:4 * g + 4], ss[:, 4 * g:4 * g + 4], AF.Ln,
                                                       bias=EPS, scale=1.0 / D), [SS], [LNV])
                      c.op(act, lambda e: e.activation(rstd[:, 4 * g:4 * g + 4], lnv[:, 4 * g:4 * g + 4], AF.Exp,
                                                       scale=-0.5), [LNV], [RSTD])
                      for s in range(4):
                          c.op(act, lambda e: e.activation(xs[bi][:, s, :], xt[bi][:, s, :], AF.Copy,
                                                           scale=rstd[:, 4 * g + s:4 * g + s + 1]),
                               [XT[bi], RSTD], [XS[bi]])
                      for kc in range(8):
                          bank = (g % 2) * 4 + kc // 2
                          half = kc % 2
                          for s in range(4):
                              c.op(pe, lambda e: e.transpose(pb16[bank][:, half * 512 + s * 128: half * 512 + (s + 1) * 128],
                                                             xs[bi][:, s, kc * 128:(kc + 1) * 128], ident[:]),
                                   [XS[bi], IDENT], [PB[bank]], signal=(s == 3))
                          c.op(dve, lambda e: e.tensor_scalar(hT[:, kc, g * 512:(g + 1) * 512],
                                                              pb16[bank][:, half * 512:(half + 1) * 512],
                                                              AB[:, kc:kc + 1], AB[:, 8 + kc:9 + kc], ALU.mult, ALU.add),
                               [PB[bank], ABB], [HT[g]])

                  c.barrier()
                  chk("p1")
              except _Stop:
                  c.barrier()
                  hit[0] = True
          if hit[0]:
              raise _Stop()

          mixT = sbt(es, "mixT", [128, 8, S], BF16); MIXT = [Buf("mixT%d" % g) for g in range(8)]
          print("sbuf remaining before ATT scope:", nc.sbuf_bytes_remaining)

          WSCR = Buf("wscr")
          conv_jobs = []
          wg_r = wg_d.rearrange("(kc p) n -> p kc n", p=128)
          wu_r = wu_d.rearrange("(kc p) n -> p kc n", p=128)
          for f in range(NF):
              conv_jobs.append(lambda e, f=f: e.dma_start(out=wgu_s[f, :, 0, :, :], in_=wg_r[:, :, f * 128:(f + 1) * 128]))
              conv_jobs.append(lambda e, f=f: e.dma_start(out=wgu_s[f, :, 1, :, :], in_=wu_r[:, :, f * 128:(f + 1) * 128]))
          wd_r = wd_d.rearrange("(f p) n -> p f n", p=128)
          for f in range(0, NF, 2):
              conv_jobs.append(lambda e, f=f: e.dma_start(out=wdn_s[:, f:f + 2, :], in_=wd_r[:, f:f + 2, :]))
          wo_r = wo_d.rearrange("(kc p) n -> p kc n", p=128)
          for kc in range(0, 8, 2):
              conv_jobs.append(lambda e, kc=kc: e.dma_start(out=wo_s[:, kc:kc + 2, :], in_=wo_r[:, kc:kc + 2, :]))
          conv_pos = [0]

          def emit_conv(n):
              for _ in range(n):
                  if conv_pos[0] < len(conv_jobs):
                      c.dma(pool, conv_jobs[conv_pos[0]], WSCR)
                      conv_pos[0] += 1

          with ExitStack() as e2:
              try:
                  qb = sbt(e2, "qb", [128, S], BF16); QB = Buf("qb")
                  kb = sbt(e2, "kb", [128, S], BF16); KB = Buf("kb")
                  VO = [sbt(e2, "vo%d" % i, [128, 32, 128], BF16) for i in range(2)]
                  VOB = [Buf("vo%d" % i) for i in range(2)]
                  Pt = [sbt(e2, "pt%d" % i, [128, 512], BF16) for i in range(4)]
                  PT = [Buf("pt%d" % i) for i in range(4)]
                  sq = [sbt(e2, "sq%d" % i, [128, 512], BF16) for i in range(2)]
                  SQ = [Buf("sq%d" % i) for i in range(2)]
                  rs = sbt(e2, "rs", [128, 512], F32); RS = Buf("rs")
                  wbuf = [sbt(e2, "wb%d" % i, [128, 3, 8, 128], BF16) for i in range(2)]
                  WBUF = [Buf("wb%d" % i) for i in range(2)]
                  wf = sbt(e2, "wf", [128, 8, 8], BF16); WF = Buf("wf")
                  qkg = sbt(e2, "qkg", [128, 4], F32); QKG = Buf("qkg")
                  bfbc = sbt(e2, "bfbc", [128, 256], F32); BFBC = Buf("bfbc")
                  zl = sbt(e2, "zl", [128, 256], F32); ZL = Buf("zl")
                  Tsb = sbt(e2, "Tsb", [128, 256], F32); TSB = Buf("Tsb")
                  cs = sbt(e2, "cs", [128, 256], F32); CS = Buf("cs")
                  Gs = sbt(e2, "Gs", [128, 256], F32); GS_ = Buf("Gs")
                  GL = sbt(e2, "GL", [128, 256], F32); GLB = Buf("GL")
                  biasF = sbt(e2, "biasF", [128, 2, 8, 32], F32); BIASF = Buf("biasF")
                  T2 = sbt(e2, "T2", [128, 2, 256], BF16); T2B = [Buf("T2_0"), Buf("T2_1")]
                  Hh = sbt(e2, "Hh", [128, 256], F32); HHB = Buf("Hh")
                  rb33 = sbt(e2, "rb33", [33, 8], F32); RB33 = Buf("rb33")
                  tab = sbt(e2, "tab", [33, 384], F32); TAB = Buf("tab")
                  fvsb = sbt(e2, "fvsb", [8, 384], F32); FVSB = Buf("fvsb")
                  cf = sbt(e2, "cf", [128, 256], F32); CF = Buf("cf")
                  sel127 = sbt(e2, "sel127", [128, 128], F32); SEL127 = Buf("sel127")
                  cbb = sbt(e2, "cbb", [128, 256 + 1024], BF16); CBB = Buf("cbb")
                  gsb = sbt(e2, "gsb", [128, 512], F32); GSB = Buf("gsb")
                  m8 = sbt(e2, "m8", [128, 256], F32); M8 = Buf("m8")
                  thr = sbt(e2, "thr", [128, 32], F32); THR = Buf("thr")
                  selb = sbt(e2, "selb", [128, 512], BF16); SELB = Buf("selb")
                  kms = sbt(e2, "kms", [64, 16], F32); KMS = Buf("kms")
                  kmT = sbt(e2, "kmT", [64, 16], BF16); KMT = Buf("kmT")
                  rden = sbt(e2, "rden", [128, 512], F32); RDEN = Buf("rden")
                  FV = Buf("fv_dram")
                  ones8 = sbt(e2, "ones8", [8, 512], F32); ONES8 = Buf("ones8")
                  nbf = sbt(e2, "nbf", [8, 1], F32); NBF = Buf("nbf")
                  AROW = Buf("arow")
                  print("sbuf remaining inside ATT scope:", nc.sbuf_bytes_remaining)

                  J = cf[:, 0:128]
                  tri = cf[:, 128:256]
                  blockones = cbb[:, 0:128]
                  causal = cbb[:, 128:256]
                  pmask = cbb[:, 256:768]
                  eown = cbb[:, 768:1280]

                  c.dma(sp, lambda e: e.dma_start(out=cf[:], in_=cf_d[:, 0:256]), CF)
                  c.dma(sp, lambda e: e.dma_start(out=sel127[:], in_=cf_d[:, 384:512]), SEL127)
                  c.dma(pool, lambda e: e.dma_start(out=cbb[:], in_=cb_d[:, 128:128 + 1280]), CBB)
                  c.dma(sp, lambda e: e.dma_start(out=qkg[:], in_=qkg_d), QKG)
                  c.dma(sp, lambda e: e.dma_start(out=bfbc[:], in_=bf_d), BFBC)
                  c.dma(sp, lambda e: e.dma_start(out=rb33[:], in_=rb_d), RB33)
                  c.dma(sp, lambda e: e.dma_start(out=tab[:], in_=tab_d), TAB)
                  c.dma(pool, lambda e: e.dma_start(out=wf[:], in_=win_r[:, :, 1536:1544]), WF)
                  c.op(dve, lambda e: e.tensor_scalar(qkg[:, 0:1], qkg[:, 0:1], 0.125, None, ALU.mult), [QKG], [QKG])
                  c.op(dve, lambda e: e.tensor_scalar(qkg[:, 2:3], qkg[:, 2:3], 0.125, None, ALU.mult), [QKG], [QKG])
                  c.op(pool, lambda e: e.memset(VO[0][:, :, 64:128], 1.0), [], [VOB[0]])
                  c.op(pool, lambda e: e.memset(VO[1][:, :, 0:64], 1.0), [], [VOB[1]])

                  def load_pair_weights(hp, wi):
                      if hp < 4:
                          cq, ck, cv = hp * 128, 512 + hp * 128, 1024 + hp * 128
                      else:
                          cq, ck, cv = 1544 + (hp - 4) * 128, 2056 + (hp - 4) * 128, 2568 + (hp - 4) * 128
                      for j, c0 in enumerate((cq, ck, cv)):
                          c.dma(pool, lambda e: e.dma_start(out=wbuf[wi][:, j, :, :], in_=win_r[:, :, c0:c0 + 128]), WBUF[wi])

                  load_pair_weights(0, 0)

                  c.op(pe, lambda e: e.matmul(pb[6][0:8, 0:384], rb33[0:33, 0:8], tab[0:33, 0:384], start=True, stop=True),
                       [RB33, TAB], [PB[6]])
                  c.op(dve, lambda e: e.tensor_copy(fvsb[:], pb[6][0:8, 0:384]), [PB[6]], [FVSB])
                  c.dma(sp, lambda e: e.dma_start(out=fv_s, in_=fvsb[:]), FV, [FVSB])

                  def build_t2_load(h):
                      src = bass.AP(fv_t, h * 384, [[1, 128], [1, 256]])
                      c.dma(sp, lambda e: e.dma_start(out=Hh[:], in_=src), HHB, [FV])

                  def build_t2_finish(ti):
                      c.op(pe, lambda e: e.matmul(pb[7][:, 0:256], J, Hh[:], start=True, stop=True), [CF, HHB], [PB[7]])
                      c.op(dve, lambda e: e.tensor_copy(T2[:, ti, :], pb[7][:, 0:256]), [PB[7]], [T2B[ti]])

                  for t in range(32):
                      for kc in range(8):
                          c.op(pe, lambda e: e.matmul(pb[7][:, t * 8:(t + 1) * 8], hT[:, kc, t * 128:(t + 1) * 128],
                                                      wf[:, kc, :], start=(kc == 0), stop=(kc == 7)),
                               [HT[t // 4], WF], [PB[7]], signal=(kc == 7 and t % 4 == 3))
                  c.op(dve, lambda e: e.tensor_tensor(zl[:], pb[7][:, 0:256], bfbc[:], ALU.add), [PB[7], BFBC], [ZL])
                  c.op(act, lambda e: e.activation(zl[:], zl[:], AF.Exp, scale=-1.0), [ZL], [ZL])
                  c.op(act, lambda e: e.activation(zl[:], zl[:], AF.Ln, bias=1.0), [ZL], [ZL])
                  c.op(pe, lambda e: e.matmul(pb[7][:, 0:256], tri, zl[:], start=True, stop=True), [CF, ZL], [PB[7]])
                  c.op(pe, lambda e: e.matmul(pb[6][:, 0:256], onesf[:], zl[:], start=True, stop=True), [ONESF, ZL], [PB[6]])
                  c.op(dve, lambda e: e.tensor_copy(Tsb[:], pb[6][:, 0:256]), [PB[6]], [TSB])
                  T3 = Tsb[:].rearrange("p (t h) -> p t h", h=8)
                  cs3 = cs[:].rearrange("p (t h) -> p t h", h=8)
                  for h in range(8):
                      c.op(dve, lambda e: e.tensor_tensor_scan(cs3[:, :, h], onesf[:, 0:32], T3[:, :, h], 0.0,
                                                               ALU.mult, ALU.add), [ONESF, TSB], [CS])
                  c.op(dve, lambda e: e.tensor_tensor(Gs[:], pb[7][:, 0:256], cs[:], ALU.add), [PB[7], CS], [GS_])
                  c.op(dve, lambda e: e.tensor_tensor(Gs[:], Gs[:], Tsb[:], ALU.subtract), [GS_, TSB], [GS_])
                  c.op(pe, lambda e: e.matmul(pb[6][:, 0:256], sel127[:], Gs[:], start=True, stop=True), [SEL127, GS_], [PB[6]])
                  c.op(dve, lambda e: e.tensor_copy(GL[:], pb[6][:, 0:256]), [PB[6]], [GLB])
                  G3 = Gs[:].rearrange("p (t h) -> p t h", h=8)
                  c.op(pool, lambda e: e.memset(ones8[:], 1.0), [], [ONES8])
                  c.dma(sp, lambda e: e.dma_start(out=nbf[:], in_=bfc_d), NBF)
                  c.op(dve, lambda e: e.tensor_scalar(nbf[:], nbf[:], -1.0, None, ALU.mult), [NBF], [NBF])
                  for g in range(8):
                      for kc in range(8):
                          c.op(pe, lambda e: e.matmul(pb[5][0:8, :], wf[:, kc, :], hT[:, kc, g * 512:(g + 1) * 512],
                                                      start=(kc == 0), stop=(kc == 7)),
                               [HT[g], WF], [PB[5]], signal=(kc == 7))
                      c.op(act, lambda e: e.activation(rs[0:8, :], pb[5][0:8, :], AF.Exp, bias=nbf[0:8, 0:1], scale=-1.0),
                           [PB[5], NBF], [RS])
                      c.op(act, lambda e: e.activation(rs[0:8, :], rs[0:8, :], AF.Ln, bias=1.0), [RS], [RS])
                      c.op(dve, lambda e: e.tensor_tensor_scan(rden[0:8, :], ones8[:], rs[0:8, :], 0.0, ALU.mult, ALU.add),
                           [ONES8, RS], [RDEN])
                      c.op(dve, lambda e: e.tensor_scalar(sq[0][0:8, :], rden[0:8, :], -1.0, rden[0:8, 511:512],
                                                          ALU.mult, ALU.add), [RDEN], [SQ[0]])
                      c.dma(sp, lambda e: e.dma_start(out=arow_s[:, g * 512:(g + 1) * 512], in_=sq[0][0:8, :]), AROW, [SQ[0]])
                  chk("setup")

                  PBANKS = [0, 1, 4, 5]

                  def proj_qk(dst, DST, wap_fn, np_, gcol, hp, dst2=None, DST2=None):
                      def emit_proj(g):
                          bank = PBANKS[g % 4]
                          for kc in range(8):
                              c.op(pe, lambda e: e.matmul(pb[bank][0:np_, :], wap_fn(kc), hT[:, kc, g * 512:(g + 1) * 512],
                                                          start=(kc == 0), stop=(kc == 7)),
                                   [HT[g], WBUF[hp % 2]], [PB[bank]], signal=(kc == 7))

                      def emit_sq(g):
                          bank = PBANKS[g % 4]
                          c.op(act, lambda e: e.activation(sq[g % 2][0:np_, :], pb[bank][0:np_, :], AF.Square),
                               [PB[bank]], [SQ[g % 2]])

                      emit_proj(0)
                      emit_proj(1)
                      emit_sq(0)
                      for g in range(8):
                          bank = PBANKS[g % 4]
                          sbank = 2 + g % 2
                          c.op(pe, lambda e: e.matmul(pb[sbank][0:np_, :], blockones[0:np_, 0:np_], sq[g % 2][0:np_, :],
                                                      start=True, stop=True), [CBB, SQ[g % 2]], [PB[sbank]])
                          if g + 2 < 8:
                              emit_proj(g + 2)
                          if g + 1 < 8:
                              emit_sq(g + 1)
                          c.op(act, lambda e: e.activation(rs[0:np_, :], pb[sbank][0:np_, :], AF.Ln, bias=EPS,
                                                           scale=1.0 / 64), [PB[sbank]], [RS])
                          c.op(act, lambda e: e.activation(rs[0:np_, :], rs[0:np_, :], AF.Exp, scale=-0.5), [RS], [RS])
                          if dst2 is None:
                              c.op(dve, lambda e: e.scalar_tensor_tensor(dst[0:np_, g * 512:(g + 1) * 512], pb[bank][0:np_, :],
                                                                         qkg[0:np_, gcol:gcol + 1], rs[0:np_, :],
                                                                         ALU.mult, ALU.mult),
                                   [PB[bank], QKG, RS], [DST])
                          else:
                              c.op(dve, lambda e: e.scalar_tensor_tensor(dst[0:64, g * 512:(g + 1) * 512], pb[bank][0:64, :],
                                                                         qkg[0:64, gcol:gcol + 1], rs[0:64, :],
                                                                         ALU.mult, ALU.mult),
                                   [PB[bank], QKG, RS], [DST])
                              c.op(dve, lambda e: e.scalar_tensor_tensor(dst2[0:64, g * 512:(g + 1) * 512], pb[bank][64:128, :],
                                                                         qkg[64:128, gcol:gcol + 1], rs[64:128, :],
                                                                         ALU.mult, ALU.mult),
                                   [PB[bank], QKG, RS], [DST2])

                  def proj_v(hp):
                      wi = hp % 2
                      for t4 in range(8):
                          bank = 2 + t4 % 2
                          for s in range(4):
                              t = t4 * 4 + s
                              for kc in range(8):
                                  c.op(pe, lambda e: e.matmul(pb[bank][:, s * 128:(s + 1) * 128],
                                                              hT[:, kc, t * 128:(t + 1) * 128], wbuf[wi][:, 2, kc, :],
                                                              start=(kc == 0), stop=(kc == 7)),
                                       [HT[t // 4], WBUF[wi]], [PB[bank]], signal=(kc == 7 and s == 3))
                          pv = pb[bank][:].rearrange("p (t c) -> p t c", c=128)
                          c.op(act, lambda e: e.copy(VO[0][:, t4 * 4:t4 * 4 + 4, 0:64], pv[:, :, 0:64]), [PB[bank]], [VOB[0]])
                          c.op(act, lambda e: e.copy(VO[1][:, t4 * 4:t4 * 4 + 4, 64:128], pv[:, :, 64:128]),
                               [PB[bank]], [VOB[1]])

                  def vaug(hl, kt):
                      return VO[hl][:, kt, :]

                  state = {"item": 0, "qt": 0}

                  def run_attention(heads, hooks=None):
                      items = []
                      for hd in heads:
                          for Qi in range(8):
                              for kt in range(4 * Qi + 4):
                                  items.append((hd, Qi, kt))
                      DEPTH = 3
                      meta = {}

                      def emit_qk(idx):
                          hd, Qi, kt = items[idx]
                          gi = state["item"]
                          state["item"] += 1
                          sbk = gi % 4
                          j = kt - 4 * Qi
                          c0 = 128 * j if j > 0 else 0
                          p0, nk = hd["p0"], hd["nk"]
                          extra = None
                          if hd["kind"] == "fox":
                              if j >= 0:
                                  extra = (c0, 128, causal)
                          else:
                              if j >= 0:
                                  w = min(256, 512 - c0)
                                  extra = (c0, w, hd["t2"][:, 0:w])
                              elif j == -1:
                                  extra = (0, 128, hd["t2"][:, 128:256])
                          kq, kk = hd["Q"], hd["K"]
                          c.op(pe, lambda e: e.matmul(pb[sbk][:, c0:512], kk[p0:p0 + nk, kt * 128:(kt + 1) * 128],
                                                      kq[p0:p0 + nk, Qi * 512 + c0:(Qi + 1) * 512],
                                                      start=True, stop=(extra is None)),
                               [hd["KB"], hd["QB"]], [PB[sbk]], signal=(extra is None))
                          if extra is not None:
                              ec0, ew_, eap = extra
                              c.op(pe, lambda e: e.matmul(pb[sbk][:, ec0:ec0 + ew_], ident[:], eap, start=False, stop=True),
                                   [IDENT, CBB] + ([hd["T2B"]] if hd.get("T2B") is not None else []), [PB[sbk]])
                          bias = hd["bias"](Qi, kt)
                          if bias is None:
                              c.op(act, lambda e: e.activation(Pt[sbk][:, c0:512], pb[sbk][:, c0:512], AF.Exp),
                                   [PB[sbk]], [PT[sbk]])
                          else:
                              c.op(act, lambda e: e.activation(Pt[sbk][:, c0:512], pb[sbk][:, c0:512], AF.Exp, bias=bias),
                                   [PB[sbk], BIASF], [PT[sbk]])
                          meta[idx] = (sbk, c0)

                      def emit_pv(idx):
                          hd, Qi, kt = items[idx]
                          sbk, c0 = meta.pop(idx)
                          if kt == 0:
                              hd["obank"] = 4 + state["qt"] % 3
                              state["qt"] += 1
                          ob = hd["obank"]
                          last = (kt == 4 * Qi + 3)
                          c.op(pe, lambda e: e.matmul(pb[ob][:, c0:512], vaug(hd["hl"], kt), Pt[sbk][:, c0:512],
                                                      start=(kt == 0), stop=last),
                               [VOB[hd["hl"]], PT[sbk]], [PB[ob]])
                          if last:
                              if hd["hl"] == 0:
                                  orow, drow = slice(0, 64), slice(64, 128)
                              else:
                                  orow, drow = slice(64, 128), slice(0, 64)
                              c.op(dve, lambda e: e.reciprocal(rden[orow, :], pb[ob][drow, :]), [PB[ob]], [RDEN])
                              c.op(dve, lambda e: e.tensor_tensor(mixT[orow, hd["hp"], Qi * 512:(Qi + 1) * 512],
                                                                  pb[ob][orow, :], rden[orow, :], ALU.mult),
                                   [PB[ob], RDEN], [MIXT[hd["hp"]]])

                      for i in range(len(items) + DEPTH):
                          if hooks and i in hooks:
                              hooks[i]()
                          if i < len(items):
                              emit_qk(i)
                          if i >= DEPTH:
                              emit_pv(i - DEPTH)

                  class View2:
                      def __init__(self, t, ch):
                          self.t, self.ch = t, ch

                      def __getitem__(self, key):
                          r, cc_ = key
                          return self.t[r, self.ch, cc_]

                  qb2, kb2 = View2(mixT, 6), View2(mixT, 7)
                  QB2, KB2 = MIXT[6], MIXT[7]

                  def mod_bg():
                      v32 = mixT[:, 5, :].bitcast(F32)
                      accb = v32[:, 0:512]
                      wab = [v32[:, 512:1024], v32[:, 1024:1536], v32[:, 1536:2048]]
                      M5 = MIXT[5]
                      steps = [(ck, cp, kc) for ck in (1, 2) for cp in range(4) for kc in range(8)]

                      def dma_step(i):
                          ck, cp, kc = steps[i]
                          col0 = ck * 2048 + cp * 512
                          c.dma(sp, lambda e: e.dma_start(out=wab[i % 3], in_=wada_d[kc * 128:(kc + 1) * 128,
                                                                                   col0:col0 + 512]), M5)

                      dma_step(0)
                      dma_step(1)
                      yield
                      for i, (ck, cp, kc) in enumerate(steps):
                          if i + 2 < len(steps):
                              dma_step(i + 2)
                          if kc == 0:
                              c.op(dve, lambda e: e.tensor_scalar(accb, wab[i % 3], silc[:, 0:1], None, ALU.mult),
                                   [M5, SILC], [M5])
                          else:
                              c.op(dve, lambda e: e.scalar_tensor_tensor(accb, wab[i % 3], silc[:, kc:kc + 1], accb,
                                                                         ALU.mult, ALU.add), [M5, SILC], [M5])
                          yield
                          if kc == 7:
                              for j in range(4):
                                  c.op(pe, lambda e: e.matmul(pb[7][:, j:j + 1], accb[:, j * 128:(j + 1) * 128],
                                                              onesf[:, 0:1], start=True, stop=True),
                                       [M5, ONESF], [PB[7]], signal=(j == 3))
                              mc0 = ck * 16 + cp * 4
                              c.op(dve, lambda e: e.tensor_tensor(modcol[:, mc0:mc0 + 4], pb[7][:, 0:4],
                                                                  badac[:, mc0:mc0 + 4], ALU.add),
                                   [PB[7], BADAC], [MODCOL])
                              yield

                  bg = mod_bg()

                  def bg_step():
                      next(bg, None)

                  def fox_prep(h, hl, Q, K, QBf, KBf):
                      c.op(pool, lambda e: e.memset(K[64:65, :], 1.0), [], [KBf])
                      c.dma(sp, lambda e: e.dma_start(out=Q[64:65, :], in_=arow_s[h:h + 1, :]), QBf, [AROW])
                      for Qi in range(8):
                          c.op(dve, lambda e: e.tensor_scalar(biasF[:, hl, Qi, :], G3[:, :, h],
                                                              GL[:, (4 * Qi + 3) * 8 + h:(4 * Qi + 3) * 8 + h + 1], None,
                                                              ALU.subtract), [GS_, GLB], [BIASF])

                  def fox_attn(hl, hp, Q, K, QBf, KBf):
                      heads = [dict(kind="fox", p0=0, nk=65, hl=hl, hp=hp, t2=None, Q=Q, K=K, QB=QBf, KB=KBf,
                                    bias=(lambda Qi, kt, hl=hl: biasF[:, hl, Qi, kt:kt + 1]))]
                      run_attention(heads, {i: bg_step for i in range(6, 144, 8)})

                  def moba_stage1(hm, Q, K, QBf, KBf):
                      c.dma(pool, lambda e: e.dma_start(out=K[64:80, :].rearrange("p (a b) -> p a b", b=1024),
                                                        in_=khot_d.rearrange("p (a b) -> p a b", b=1024)), KBf)
                      c.op(dve, lambda e: e.tensor_reduce(kms[:, :], K[0:64, :].rearrange("p (n l) -> p n l", l=256),
                                                          AX.X, ALU.add), [KBf], [KMS])
                      c.op(dve, lambda e: e.tensor_scalar(kmT[:, :], kms[:, :], 1.0 / 256, None, ALU.mult), [KMS], [KMT])
                      build_t2_load(hm)

                  def moba_stage2(hm, Q, K, QBf, KBf):
                      for i in range(32):
                          c.op(pe, lambda e: e.matmul(pb[7][:, i * 16:(i + 1) * 16], Q[0:64, i * 128:(i + 1) * 128],
                                                      kmT[:, :], start=True, stop=True),
                               [QBf, KMT], [PB[7]], signal=(i == 31))
                      c.op(dve, lambda e: e.tensor_tensor(gsb[:], pb[7][:, :], pmask, ALU.add), [PB[7], CBB], [GSB])
                      for i in range(32):
                          c.op(dve, lambda e: e.max(m8[:, i * 8:(i + 1) * 8], gsb[:, i * 16:(i + 1) * 16]),
                               [GSB], [M8], signal=(i == 31))
                      m83 = m8[:].rearrange("p (i e) -> p i e", e=8)
                      c.op(dve, lambda e: e.tensor_scalar(thr[:], m83[:, :, 2], -1e29, None, ALU.max), [M8], [THR])
                      gs3 = gsb[:].rearrange("p (i n) -> p i n", n=16)
                      sel3 = selb[:].rearrange("p (i n) -> p i n", n=16)
                      c.op(dve, lambda e: e.tensor_tensor(sel3, gs3, thr[:].unsqueeze(2).to_broadcast([128, 32, 16]),
                                                          ALU.is_ge), [GSB, THR], [SELB])
                      c.op(dve, lambda e: e.scalar_tensor_tensor(selb[:], selb[:], -1.0, eown, ALU.add, ALU.max),
                           [SELB, CBB], [SELB])

                  def moba_stage3(hl, Q, K, QBf, KBf):
                      for i8 in range(4):
                          for i in range(8):
                              ii = i8 * 8 + i
                              c.op(pe, lambda e: e.transpose(pb16[7][0:16, i * 128:(i + 1) * 128],
                                                             selb[:, ii * 16:(ii + 1) * 16], ident[:]),
                                   [SELB, IDENT], [PB[7]], signal=(i == 7))
                          c.op(act, lambda e: e.copy(Q[64:80, i8 * 1024:(i8 + 1) * 1024], pb16[7][0:16, :]),
                               [PB[7]], [QBf])
                      build_t2_finish(hl)

                  def moba_attn(hl, hp, Q, K, QBf, KBf, hooks=None):
                      heads = [dict(kind="moba", p0=0, nk=80, hl=hl, hp=hp, t2=T2[:, hl, :], T2B=T2B[hl],
                                    Q=Q, K=K, QB=QBf, KB=KBf, bias=(lambda Qi, kt: None))]
                      run_attention(heads, hooks)

                  for hp in range(8):
                      wi = hp % 2
                      fox = hp < 4
                      gq, gk = (0, 1) if fox else (2, 3)
                      if hp + 1 < 8:
                          load_pair_weights(hp + 1, (hp + 1) % 2)
                      emit_conv(9)
                      if hp == 4:
                          for _ in bg:
                              pass
                      setA = (qb, kb, QB, KB)
                      setB = (qb2, kb2, QB2, KB2)
                      if hp < 6:
                          proj_qk(qb, QB, lambda kc: wbuf[wi][:, 0, kc, :], 128, gq, hp, dst2=qb2, DST2=QB2)
                          proj_qk(kb, KB, lambda kc: wbuf[wi][:, 1, kc, :], 128, gk, hp, dst2=kb2, DST2=KB2)
                          if fox:
                              fox_prep(2 * hp, 0, *setA)
                              fox_prep(2 * hp + 1, 1, *setB)
                              proj_v(hp)
                              fox_attn(0, hp, *setA)
                              if hp == 0:
                                  chk("fox0")
                              fox_attn(1, hp, *setB)
                          else:
                              hm = 2 * (hp - 4)
                              moba_stage1(hm, *setA)
                              moba_stage2(hm, *setA)
                              proj_v(hp)
                              moba_stage3(0, *setA)
                              hooks = {2: (lambda: moba_stage1(hm + 1, *setB)),
                                       30: (lambda: moba_stage2(hm + 1, *setB)),
                                       70: (lambda: moba_stage3(1, *setB))}
                              moba_attn(0, hp, *setA, hooks=hooks)
                              if hp == 4:
                                  chk("moba0")
                              moba_attn(1, hp, *setB)
                      else:
                          for hl in range(2):
                              proj_qk(qb, QB, lambda kc: wbuf[wi][:, 0, kc, 64 * hl:64 * hl + 64], 64, gq, hp)
                              proj_qk(kb, KB, lambda kc: wbuf[wi][:, 1, kc, 64 * hl:64 * hl + 64], 64, gk, hp)
                              hm = 2 * (hp - 4) + hl
                              moba_stage1(hm, *setA)
                              moba_stage2(hm, *setA)
                              if hl == 0:
                                  proj_v(hp)
                              moba_stage3(hl, *setA)
                              moba_attn(hl, hp, *setA)
                  emit_conv(100)
                  c.barrier()
              except _Stop:
                  c.barrier()
                  hit[0] = True
          if hit[0]:
              raise _Stop()
          es_h.close()

          with ExitStack() as e3:
              try:
                  wo = sbt(e3, "wo", [128, 8, D], BF16); WO = Buf("wo")
                  wdn = sbt(e3, "wdn", [128, NF, D], BF16); WDN = Buf("wdn")
                  x1 = [sbt(e3, "x1_%d" % i, [128, D], F32) for i in range(6)]
                  X1 = [Buf("x1_%d" % i) for i in range(6)]
                  h2T = sbt(e3, "h2T", [128, 8, 512], BF16); H2T = Buf("h2T")
                  actT = sbt(e3, "actT", [128, NF, 512], BF16); ACTT = Buf("actT")
                  gbc = sbt(e3, "gbc", [128, D], F32); GBC = Buf("gbc")
                  sg = [sbt(e3, "sg%d" % i, [128, 512], F32) for i in range(2)]
                  SG = [Buf("sg%d" % i) for i in range(2)]
                  xs2 = sbt(e3, "xs2", [128, D], BF16); XS2 = Buf("xs2")
                  wgu = [sbt(e3, "wgu%d" % i, [128, 2, 8, 128], BF16) for i in range(3)]
                  WGU = [Buf("wgu%d" % i) for i in range(3)]
                  junk2 = sbt(e3, "junk2", [128, D], BF16); JUNK2 = Buf("junk2")
                  ss2 = sbt(e3, "ss2", [128, 32], F32); SS2 = Buf("ss2")
                  ln2 = sbt(e3, "ln2", [128, 32], F32); LN2 = Buf("ln2")
                  rstd2 = sbt(e3, "rstd2", [128, 32], F32); RSTD2 = Buf("rstd2")
                  colrep = sbt(e3, "colrep", [128, 128], F32); COLREP = Buf("colrep")
                  OUTB = [Buf("outb%d" % i) for i in range(6)]
                  print("sbuf remaining inside FFN scope:", nc.sbuf_bytes_remaining)

                  for kc in range(0, 8, 4):
                      c.dma(sp, lambda e: e.dma_start(out=wo[:, kc:kc + 4, :], in_=wo_s[:, kc:kc + 4, :]), WO, [WSCR])
                  for f in range(0, NF, 11):
                      c.dma(sp, lambda e: e.dma_start(out=wdn[:, f:f + 11, :], in_=wdn_s[:, f:f + 11, :]), WDN, [WSCR])

                  def fold_gate(col0, W, WBUF_, nchunk):
                      for kc in range(8):
                          c.op(dve, lambda e: e.tensor_copy(colrep[:], modcol[:, col0 + kc:col0 + kc + 1].to_broadcast([128, 128])),
                               [MODCOL], [COLREP])
                          c.op(pe, lambda e: e.matmul(pb[kc // 4][:, (kc % 4) * 128:(kc % 4 + 1) * 128],
                                                      colrep[:], identf[:], start=True, stop=True),
                               [COLREP, IDENTF], [PB[kc // 4]])
                      for hh in range(2):
                          c.op(dve, lambda e: e.tensor_copy(gbc[:, hh * 512:(hh + 1) * 512], pb[hh][:, :]), [PB[hh]], [GBC])
                      for j in range(nchunk):
                          c.op(pool, lambda e: e.tensor_tensor(W[:, j, :], W[:, j, :], gbc[:], ALU.mult), [WBUF_, GBC], [WBUF_])

                  c.op(dve, lambda e: e.scalar_tensor_tensor(AB[:, 16:24], modcol[:, 32:40], 1.0, normcol[:, 8:16],
                                                             ALU.add, ALU.mult), [MODCOL, NORMCOL], [ABB])
                  c.op(dve, lambda e: e.tensor_copy(AB[:, 24:32], modcol[:, 24:32]), [MODCOL], [ABB])
                  fold_gate(16, wo, WO, 8)
                  fold_gate(40, wdn, WDN, NF)
                  chk("fold")

                  wgcnt = [0]
                  for T in range(8):
                      for s in range(4):
                          u = 4 * T + s
                          xi = u % 6
                          c.dma(sp, lambda e: e.dma_start(out=x1[xi][:], in_=x_r[:, u, :]), X1[xi])
                          for ch in range(2):
                              bank = ch
                              for kc in range(8):
                                  c.op(pe, lambda e: e.matmul(pb[bank][:, :], mixT[:, kc, u * 128:(u + 1) * 128],
                                                              wo[:, kc, ch * 512:(ch + 1) * 512],
                                                              start=(kc == 0), stop=(kc == 7)),
                                       [MIXT[kc], WO], [PB[bank]], signal=(kc == 7))
                              c.op(dve, lambda e: e.tensor_tensor(x1[xi][:, ch * 512:(ch + 1) * 512], pb[bank][:, :],
                                                                  x1[xi][:, ch * 512:(ch + 1) * 512], ALU.add),
                                   [PB[bank], X1[xi]], [X1[xi]])
                          c.op(act, lambda e: e.activation(junk2[:], x1[xi][:], AF.Square, accum_out=ss2[:, u:u + 1]),
                               [X1[xi]], [JUNK2, SS2])
                      c.op(act, lambda e: e.activation(ln2[:, 4 * T:4 * T + 4], ss2[:, 4 * T:4 * T + 4], AF.Ln, bias=EPS,
                                                       scale=1.0 / D), [SS2], [LN2])
                      c.op(act, lambda e: e.activation(rstd2[:, 4 * T:4 * T + 4], ln2[:, 4 * T:4 * T + 4], AF.Exp,
                                                       scale=-0.5), [LN2], [RSTD2])
                      for s in range(4):
                          u = 4 * T + s
                          xi = u % 6
                          c.op(act, lambda e: e.activation(xs2[:], x1[xi][:], AF.Copy, scale=rstd2[:, u:u + 1]),
                               [X1[xi], RSTD2], [XS2])
                          bank = 2 + s % 2
                          for kc in range(8):
                              c.op(pe, lambda e: e.transpose(pb16[bank][:, kc * 128:(kc + 1) * 128],
                                                             xs2[:, kc * 128:(kc + 1) * 128], ident[:]),
                                   [XS2, IDENT], [PB[bank]], signal=(kc == 7))
                          for kc in range(8):
                              c.op(dve, lambda e: e.tensor_scalar(h2T[:, kc, s * 128:(s + 1) * 128],
                                                                  pb16[bank][:, kc * 128:(kc + 1) * 128],
                                                                  AB[:, 16 + kc:17 + kc], AB[:, 24 + kc:25 + kc],
                                                                  ALU.mult, ALU.add),
                                   [PB[bank], ABB], [H2T], signal=(kc == 7))
                      for f in range(NF):
                          wi = wgcnt[0] % 3
                          wgcnt[0] += 1
                          c.dma(sp, lambda e: e.dma_start(out=wgu[wi][:], in_=wgu_s[f]), WGU[wi], [WSCR])
                          gb, ub = 4 + f % 2, 6 + f % 2
                          for kc in range(8):
                              c.op(pe, lambda e: e.matmul(pb[gb][:, :], wgu[wi][:, 0, kc, :], h2T[:, kc, :],
                                                          start=(kc == 0), stop=(kc == 7)),
                                   [WGU[wi], H2T], [PB[gb]], signal=(kc == 7))
                          for kc in range(8):
                              c.op(pe, lambda e: e.matmul(pb[ub][:, :], wgu[wi][:, 1, kc, :], h2T[:, kc, :],
                                                          start=(kc == 0), stop=(kc == 7)),
                                   [WGU[wi], H2T], [PB[ub]], signal=(kc == 7))
                          c.op(act, lambda e: e.activation(sg[f % 2][:], pb[gb][:, :], AF.Silu), [PB[gb]], [SG[f % 2]])
                          c.op(dve, lambda e: e.tensor_tensor(actT[:, f, :], pb[ub][:, :], sg[f % 2][:], ALU.mult),
                               [PB[ub], SG[f % 2]], [ACTT])
                      for s in range(4):
                          u = 4 * T + s
                          xi = u % 6
                          for ch in range(2):
                              bank = ch
                              for f in range(NF):
                                  c.op(pe, lambda e: e.matmul(pb[bank][:, :], actT[:, f, s * 128:(s + 1) * 128],
                                                              wdn[:, f, ch * 512:(ch + 1) * 512],
                                                              start=(f == 0), stop=(f == NF - 1)),
                                       [ACTT, WDN], [PB[bank]], signal=(f == NF - 1))
                              c.op(dve, lambda e: e.tensor_tensor(x1[xi][:, ch * 512:(ch + 1) * 512], pb[bank][:, :],
                                                                  x1[xi][:, ch * 512:(ch + 1) * 512], ALU.add),
                                   [PB[bank], X1[xi]], [X1[xi]])
                          c.dma(pool, lambda e: e.dma_start(out=out_r[:, u, :], in_=x1[xi][:]), OUTB[xi], [X1[xi]])
                  c.barrier()
              except _Stop:
                  c.barrier()
                  hit[0] = True
          if hit[0]:
              raise _Stop()
      except _Stop:
        es_h.close()
    return nc


_CACHE = {}


def kernel(**inputs):
    x = np.ascontiguousarray(inputs["x"], dtype=np.float32)
    cvec = np.asarray(inputs["c"], dtype=np.float32)
    f32c = lambda a: np.ascontiguousarray(a, dtype=np.float32)
    w_ada = f32c(inputs["w_ada"][0])
    b_ada = np.asarray(inputs["b_ada"][0], np.float32)
    cf, cb, khot, tab = host_consts()
    colform = lambda v, n: np.ascontiguousarray(v.reshape(n, 128).T)
    norm_col = np.concatenate([colform(np.asarray(inputs["norm1"][0], np.float32), 8),
                               colform(np.asarray(inputs["norm2"][0], np.float32), 8)], axis=1)
    dup = lambda g: np.concatenate([g, g]).astype(np.float32)
    qk_g = np.stack([dup(np.asarray(inputs["q_norm_fox"][0])), dup(np.asarray(inputs["k_norm_fox"][0])),
                     dup(np.asarray(inputs["q_norm_moba"][0])), dup(np.asarray(inputs["k_norm_moba"][0]))], axis=1)
    bf_bc = np.ascontiguousarray(np.broadcast_to(np.tile(np.asarray(inputs["b_forget"][0], np.float32), 32)[None, :],
                                                 (128, 256)))
    rb33 = np.concatenate([np.asarray(inputs["rel_bias"], np.float32), np.ones((1, 8), np.float32)], axis=0)
    shared = {
        "w_ada": w_ada,
        "b_ada_col": colform(b_ada, 48),
        "norm_col": f32c(norm_col),
        "w_in": f32c(inputs["w_in"][0]),
        "bf_bc": bf_bc,
        "bf_col": np.ascontiguousarray(np.asarray(inputs["b_forget"][0], np.float32).reshape(8, 1)),
        "qk_g": f32c(qk_g),
        "rb33": f32c(rb33),
        "w_o": f32c(inputs["w_o"][0]),
        "w_gate": f32c(inputs["w_gate"][0]),
        "w_up": f32c(inputs["w_up"][0]),
        "w_down": f32c(inputs["w_down"][0]),
        "cf32": cf, "cb16": cb, "khot": khot, "t5tab": tab,
    }
    if "nc" not in _CACHE:
        _CACHE["nc"] = build_program()
    nc = _CACHE["nc"]
    in_maps = []
    for b in range(NCORES):
        m = dict(shared)
        m["x"] = x[b]
        m["c_col"] = colform(cvec[b], 8)
        in_maps.append(m)
    res = run_bass_kernel_spmd(nc, in_maps, core_ids=list(range(NCORES)))
    out = np.stack([np.asarray(res.results[b]["out"], dtype=np.float32).reshape(S, D) for b in range(NCORES)], axis=0)
    return out
```

```python
import math
from contextlib import ExitStack

import numpy as np
import concourse.bass as bass
import concourse.mybir as mybir
from concourse.bass_utils import run_bass_kernel_spmd

F32 = mybir.dt.float32
BF16 = mybir.dt.bfloat16
ALU = mybir.AluOpType
AF = mybir.ActivationFunctionType
AX = mybir.AxisListType

S = 4096
D = 1024
NCORES = 8
DFF = 2816
NF = 22
INC = 3080
EPS = 1e-6
NEG = -32768.0
N_BUCKETS = 32
MAX_DISTANCE = 128


class EngW:
    def __init__(self, nc, es, eng, name):
        self.eng = eng
        self.name = name
        self.sem = es.enter_context(nc.semaphore("sem_" + name))
        self.count = 0
        self.waited = {}
        self.pend_r = []
        self.pend_w = []

    def wait(self, tok):
        sem, val = tok
        key = id(sem)
        if self.waited.get(key, 0) >= val:
            return
        self.eng.wait_ge(sem, val)
        self.waited[key] = val


class Buf:
    def __init__(self, name):
        self.name = name
        self.w = {}
        self.r = {}
        self.dsem = None
        self.dcount = 0

    def toks_w(self):
        return list(self.w.values())

    def toks_all(self):
        return list(self.w.values()) + list(self.r.values())


def _put(d, tok):
    k = id(tok[0])
    if k not in d or d[k][1] < tok[1]:
        d[k] = tok


class Ctx:
    def __init__(self, nc, es):
        self.nc = nc
        self.es = es
        self.pe = EngW(nc, es, nc.tensor, "pe")
        self.act = EngW(nc, es, nc.scalar, "act")
        self.dve = EngW(nc, es, nc.vector, "dve")
        self.pool = EngW(nc, es, nc.gpsimd, "pool")
        self.sp = EngW(nc, es, nc.sync, "sp")
        self.engs = [self.pe, self.act, self.dve, self.pool, self.sp]
        self.dma_bufs = []
        self.swdge_inflight = []
        self.SWDGE_MAX = 5

    def op(self, ew, fn, reads=(), writes=(), signal=True):
        for b in reads:
            for t in b.toks_w():
                ew.wait(t)
        for b in writes:
            for t in b.toks_all():
                ew.wait(t)
        ins = fn(ew.eng)
        ew.pend_r += list(reads)
        ew.pend_w += list(writes)
        if signal:
            ew.count += 1
            ins.then_inc(ew.sem, 1)
            tok = (ew.sem, ew.count)
            for b in ew.pend_r:
                _put(b.r, tok)
            for b in ew.pend_w:
                b.w = {id(tok[0]): tok}
                b.r = {}
            ew.pend_r = []
            ew.pend_w = []
        return ins

    def dma(self, ew, fn, dst, srcs=()):
        if dst.dsem is None:
            dst.dsem = self.es.enter_context(self.nc.semaphore("dsem_" + dst.name))
            self.dma_bufs.append(dst)
        for b in srcs:
            for t in b.toks_w():
                ew.wait(t)
        for k, t in list(dst.w.items()) + list(dst.r.items()):
            if t[0] is dst.dsem:
                continue
            ew.wait(t)
        if ew is self.pool:
            while len(self.swdge_inflight) >= self.SWDGE_MAX:
                b0 = self.swdge_inflight.pop(0)
                ew.wait((b0.dsem, b0.dcount))
                self.swdge_inflight = [b for b in self.swdge_inflight if b is not b0]
        ins = fn(ew.eng)
        dst.dcount += 16
        ins.then_inc(dst.dsem, 16)
        tok = (dst.dsem, dst.dcount)
        if ew is self.pool:
            self.swdge_inflight.append(dst)
        for b in srcs:
            _put(b.r, tok)
        dst.w[id(dst.dsem)] = tok
        dst.r = {}
        return ins

    def barrier(self):
        toks = []
        for e in self.engs:
            assert not e.pend_r and not e.pend_w, e.name
            if e.count > 0:
                toks.append((e.sem, e.count))
        for b in self.dma_bufs:
            toks.append((b.dsem, b.dcount))
        for e in self.engs:
            for t in toks:
                e.wait(t)


def t5_bucket_np(n):
    n = np.maximum(n, 0)
    max_exact = N_BUCKETS // 2
    nf = np.maximum(n, 1).astype(np.float32)
    large = max_exact + (np.log(nf / np.float32(max_exact)) / np.float32(math.log(MAX_DISTANCE / max_exact))
                         * np.float32(N_BUCKETS - max_exact)).astype(np.int32)
    large = np.minimum(large, N_BUCKETS - 1)
    return np.where(n < max_exact, n, large)


def host_consts():
    p = np.arange(128)
    cf = np.zeros((128, 4 * 128), np.float32)
    cf[:, 0:128] = (p[:, None] + p[None, :] == 127)
    cf[:, 128:256] = (p[:, None] <= p[None, :])
    cf[:, 256:384] = 1.0
    cf[:, 384:512] = (p[:, None] == 127)
    cb = np.zeros((128, 3 * 128 + 1024), np.float32)
    cb[:, 0:128] = np.eye(128)
    cb[:, 128:256] = (p[:, None] // 64 == p[None, :] // 64)
    cb[:, 256:384] = np.where(p[:, None] <= p[None, :], 0.0, NEG)
    i = np.arange(32)[:, None]
    n = np.arange(16)[None, :]
    pm = np.where(n < i // 2, 0.0, -1e30).astype(np.float32)
    eo = np.where(n == i // 2, 0.0, -1.0).astype(np.float32)
    cb[:, 384:384 + 512] = pm.reshape(1, 512)
    cb[:, 896:896 + 512] = eo.reshape(1, 512)
    khot = (np.arange(4096)[None, :] // 256 == np.arange(16)[:, None]).astype(np.float32) * 32768.0
    tab = np.zeros((33, 384), np.float32)
    d = np.arange(384) - 127
    bk = t5_bucket_np(d)
    for j in range(384):
        if d[j] >= 0:
            tab[bk[j], j] += 1.0
            tab[31, j] -= 1.0
        else:
            tab[32, j] = NEG
    return cf, cb, khot, tab


class _Stop(Exception):
    pass


def build_program(stop=None):
    nc = bass.Bass("TRN2", target_bir_lowering=False)

    def chk(tag):
        if stop == tag:
            raise _Stop()

    def din(name, shape):
        return nc.dram_tensor(name, list(shape), F32, kind="ExternalInput").ap()

    x_d = din("x", [S, D])
    c_d = din("c_col", [128, 8])
    wada_d = din("w_ada", [D, 6 * D])
    bada_d = din("b_ada_col", [128, 48])
    norm_d = din("norm_col", [128, 16])
    win_d = din("w_in", [D, INC])
    bf_d = din("bf_bc", [128, 256])
    bfc_d = din("bf_col", [8, 1])
    qkg_d = din("qk_g", [128, 4])
    rb_d = din("rb33", [33, 8])
    wo_d = din("w_o", [D, D])
    wg_d = din("w_gate", [D, DFF])
    wu_d = din("w_up", [D, DFF])
    wd_d = din("w_down", [DFF, D])
    cf_d = din("cf32", [128, 512])
    cb_d = din("cb16", [128, 384 + 1024])
    khot_d = din("khot", [16, 4096])
    tab_d = din("t5tab", [33, 384])
    out_d = nc.dram_tensor("out", [S, D], F32, kind="ExternalOutput").ap()

    wgu_s = nc.dram_tensor("wgu_s", [NF, 128, 2, 8, 128], BF16).ap()
    wdn_s = nc.dram_tensor("wdn_s", [128, NF, D], BF16).ap()
    wo_s = nc.dram_tensor("wo_s", [128, 8, D], BF16).ap()
    fv_t = nc.dram_tensor("fv_s", [8, 384], F32)
    arow_s = nc.dram_tensor("arow_s", [8, S], BF16).ap()
    fv_s = fv_t.ap()

    win_r = win_d.rearrange("(kc p) n -> p kc n", p=128)
    x_r = x_d.rearrange("(t p) d -> p t d", p=128)
    out_r = out_d.rearrange("(t p) d -> p t d", p=128)

    with ExitStack() as es:
      c = Ctx(nc, es)
      es_h = ExitStack()
      hit = [False]
      try:
          pe, act, dve, pool, sp = c.pe, c.act, c.dve, c.pool, c.sp

          def sbt(stack, name, shape, dt, side=None):
              name = "s_" + name
              if side is None:
                  return stack.enter_context(nc.sbuf_tensor(name, list(shape), dt))
              return stack.enter_context(nc.sbuf_tensor(name, list(shape), dt, side=side))

          pb = [es.enter_context(nc.psum_tensor("pb%d" % i, [128, 512], F32)) for i in range(8)]
          pb16 = [t.bitcast(BF16) for t in pb]
          PB = [Buf("pb%d" % i) for i in range(8)]

          ident = sbt(es, "ident", [128, 128], BF16); IDENT = Buf("ident")
          modcol = sbt(es, "modcol", [128, 48], F32); MODCOL = Buf("modcol")
          normcol = sbt(es, "normcol", [128, 16], F32); NORMCOL = Buf("normcol")
          AB = sbt(es, "AB", [128, 32], F32); ABB = Buf("AB")
          onesf = sbt(es, "onesf", [128, 128], F32); ONESF = Buf("onesf")
          identf = sbt(es, "identf", [128, 128], F32); IDENTF = Buf("identf")
          silc = sbt(es, "silc", [128, 8], F32); SILC = Buf("silc")
          badac = sbt(es, "badac", [128, 48], F32); BADAC = Buf("badac")

          c.dma(pool, lambda e: e.dma_start(out=ident[:], in_=cb_d[:, 0:128]), IDENT)
          c.dma(sp, lambda e: e.dma_start(out=normcol[:], in_=norm_d), NORMCOL)
          c.dma(sp, lambda e: e.dma_start(out=onesf[:], in_=cf_d[:, 256:384]), ONESF)
          c.dma(sp, lambda e: e.dma_start(out=identf[:], in_=cb_d[:, 0:128]), IDENTF)

          hT = sbt(es_h, "hT", [128, 8, S], BF16, side="right")
          HT = [Buf("hT%d" % g) for g in range(8)]

          with ExitStack() as e1:
              try:
                  cc = sbt(e1, "cc", [128, 8], F32); CC = Buf("cc")
                  acc = sbt(e1, "acc", [128, 2048], F32); ACC = Buf("acc")
                  wa = [sbt(e1, "wa%d" % i, [128, 2048], F32) for i in range(2)]
                  WA = [Buf("wa%d" % i) for i in range(2)]
                  xt = [sbt(e1, "xt%d" % i, [128, 4, D], F32) for i in range(2)]
                  XT = [Buf("xt%d" % i) for i in range(2)]
                  xs = [sbt(e1, "xs%d" % i, [128, 4, D], BF16) for i in range(2)]
                  XS = [Buf("xs%d" % i) for i in range(2)]
                  junk = sbt(e1, "junk", [128, D], BF16); JUNK = Buf("junk")
                  ss = sbt(e1, "ss", [128, 32], F32); SS = Buf("ss")
                  lnv = sbt(e1, "lnv", [128, 32], F32); LNV = Buf("lnv")
                  rstd = sbt(e1, "rstd", [128, 32], F32); RSTD = Buf("rstd")

                  c.dma(sp, lambda e: e.dma_start(out=cc[:], in_=c_d), CC)
                  c.dma(sp, lambda e: e.dma_start(out=badac[:], in_=bada_d), BADAC)
                  c.op(act, lambda e: e.activation(silc[:], cc[:], AF.Silu), [CC], [SILC])

                  wcnt = [0]

                  def mod_chunk(ck):
                      for kc in range(8):
                          wi = wcnt[0] % 2
                          wcnt[0] += 1
                          c.dma(sp, lambda e: e.dma_start(out=wa[wi][:], in_=wada_d[kc * 128:(kc + 1) * 128,
                                                                                ck * 2048:(ck + 1) * 2048]), WA[wi])
                          if kc == 0:
                              c.op(dve, lambda e: e.tensor_scalar(acc[:], wa[wi][:], silc[:, 0:1], None, ALU.mult),
                                   [WA[wi], SILC], [ACC])
                          else:
                              c.op(dve, lambda e: e.scalar_tensor_tensor(acc[:], wa[wi][:], silc[:, kc:kc + 1], acc[:],
                                                                         ALU.mult, ALU.add),
                                   [WA[wi], SILC, ACC], [ACC])
                      for j in range(16):
                          c.op(pe, lambda e: e.matmul(pb[7][:, j:j + 1], acc[:, j * 128:(j + 1) * 128], onesf[:, 0:1],
                                                      start=True, stop=True),
                               [ACC, ONESF], [PB[7]], signal=(j == 15))
                      c.op(dve, lambda e: e.tensor_tensor(modcol[:, ck * 16:(ck + 1) * 16], pb[7][:, 0:16],
                                                          badac[:, ck * 16:(ck + 1) * 16], ALU.add),
                           [PB[7], BADAC], [MODCOL])

                  mod_chunk(0)
                  c.op(dve, lambda e: e.scalar_tensor_tensor(AB[:, 0:8], modcol[:, 8:16], 1.0, normcol[:, 0:8],
                                                             ALU.add, ALU.mult), [MODCOL, NORMCOL], [ABB])
                  c.op(dve, lambda e: e.tensor_copy(AB[:, 8:16], modcol[:, 0:8]), [MODCOL], [ABB])

                  for g in range(8):
                      bi = g % 2
                      c.dma(sp, lambda e: e.dma_start(out=xt[bi][:], in_=x_r[:, 4 * g:4 * g + 4, :]), XT[bi])
                      for s in range(4):
                          c.op(act, lambda e: e.activation(junk[:], xt[bi][:, s, :], AF.Square,
                                                           accum_out=ss[:, 4 * g + s:4 * g + s + 1]),
                               [XT[bi]], [JUNK, SS])
                      c.op(act, lambda e: e.activation(lnv[:, 4 * g:4 * g + 4], ss[:, 4 * g:4 * g + 4], AF.Ln,
                                                       bias=EPS, scale=1.0 / D), [SS], [LNV])
                      c.op(act, lambda e: e.activation(rstd[:, 4 * g:4 * g + 4], lnv[:, 4 * g:4 * g + 4], AF.Exp,
                                                       scale=-0.5), [LNV], [RSTD])
                      for s in range(4):
                          c.op(act, lambda e: e.activation(xs[bi][:, s, :], xt[bi][:, s, :], AF.Copy,
                                                           scale=rstd[:, 4 * g + s:4 * g + s + 1]),
                               [XT[bi], RSTD], [XS[bi]])
                      for kc in range(8):
                          bank = (g % 2) * 4 + kc // 2
                          half = kc % 2
                          for s in range(4):
                              c.op(pe, lambda e: e.transpose(pb16[bank][:, half * 512 + s * 128: half * 512 + (s + 1) * 128],
                                                             xs[bi][:, s, kc * 128:(kc + 1) * 128], ident[:]),
                                   [XS[bi], IDENT], [PB[bank]], signal=(s == 3))
                          c.op(dve, lambda e: e.tensor_scalar(hT[:, kc, g * 512:(g + 1) * 512],
                                                              pb16[bank][:, half * 512:(half + 1) * 512],
                                                              AB[:, kc:kc + 1], AB[:, 8 + kc:9 + kc], ALU.mult, ALU.add),
                               [PB[bank], ABB], [HT[g]])

                  c.barrier()
                  chk("p1")
              except _Stop:
                  c.barrier()
                  hit[0] = True
          if hit[0]:
              raise _Stop()

          mixT = sbt(es, "mixT", [128, 8, S], BF16); MIXT = [Buf("mixT%d" % g) for g in range(8)]
          print("sbuf remaining before ATT scope:", nc.sbuf_bytes_remaining)

          WSCR = Buf("wscr")
          conv_jobs = []
          wg_r = wg_d.rearrange("(kc p) n -> p kc n", p=128)
          wu_r = wu_d.rearrange("(kc p) n -> p kc n", p=128)
          for f in range(NF):
              conv_jobs.append(lambda e, f=f: e.dma_start(out=wgu_s[f, :, 0, :, :], in_=wg_r[:, :, f * 128:(f + 1) * 128]))
              conv_jobs.append(lambda e, f=f: e.dma_start(out=wgu_s[f, :, 1, :, :], in_=wu_r[:, :, f * 128:(f + 1) * 128]))
          wd_r = wd_d.rearrange("(f p) n -> p f n", p=128)
          for f in range(0, NF, 2):
              conv_jobs.append(lambda e, f=f: e.dma_start(out=wdn_s[:, f:f + 2, :], in_=wd_r[:, f:f + 2, :]))
          wo_r = wo_d.rearrange("(kc p) n -> p kc n", p=128)
          for kc in range(0, 8, 2):
              conv_jobs.append(lambda e, kc=kc: e.dma_start(out=wo_s[:, kc:kc + 2, :], in_=wo_r[:, kc:kc + 2, :]))
          conv_pos = [0]

          def emit_conv(n):
              for _ in range(n):
                  if conv_pos[0] < len(conv_jobs):
                      c.dma(pool, conv_jobs[conv_pos[0]], WSCR)
                      conv_pos[0] += 1

          with ExitStack() as e2:
              try:
                  qb = sbt(e2, "qb", [128, S], BF16); QB = Buf("qb")
                  kb = sbt(e2, "kb", [128, S], BF16); KB = Buf("kb")
                  VO = [sbt(e2, "vo%d" % i, [128, 32, 128], BF16) for i in range(2)]
                  VOB = [Buf("vo%d" % i) for i in range(2)]
                  Pt = [sbt(e2, "pt%d" % i, [128, 512], BF16) for i in range(4)]
                  PT = [Buf("pt%d" % i) for i in range(4)]
                  sq = [sbt(e2, "sq%d" % i, [128, 512], BF16) for i in range(2)]
                  SQ = [Buf("sq%d" % i) for i in range(2)]
                  rs = sbt(e2, "rs", [128, 512], F32); RS = Buf("rs")
                  wbuf = [sbt(e2, "wb%d" % i, [128, 3, 8, 128], BF16) for i in range(2)]
                  WBUF = [Buf("wb%d" % i) for i in range(2)]
                  wf = sbt(e2, "wf", [128, 8, 8], BF16); WF = Buf("wf")
                  qkg = sbt(e2, "qkg", [128, 4], F32); QKG = Buf("qkg")
                  bfbc = sbt(e2, "bfbc", [128, 256], F32); BFBC = Buf("bfbc")
                  zl = sbt(e2, "zl", [128, 256], F32); ZL = Buf("zl")
                  Tsb = sbt(e2, "Tsb", [128, 256], F32); TSB = Buf("Tsb")
                  cs = sbt(e2, "cs", [128, 256], F32); CS = Buf("cs")
                  Gs = sbt(e2, "Gs", [128, 256], F32); GS_ = Buf("Gs")
                  GL = sbt(e2, "GL", [128, 256], F32); GLB = Buf("GL")
                  biasF = sbt(e2, "biasF", [128, 2, 8, 32], F32); BIASF = Buf("biasF")
                  T2 = sbt(e2, "T2", [128, 2, 256], BF16); T2B = [Buf("T2_0"), Buf("T2_1")]
                  Hh = sbt(e2, "Hh", [128, 256], F32); HHB = Buf("Hh")
                  rb33 = sbt(e2, "rb33", [33, 8], F32); RB33 = Buf("rb33")
                  tab = sbt(e2, "tab", [33, 384], F32); TAB = Buf("tab")
                  fvsb = sbt(e2, "fvsb", [8, 384], F32); FVSB = Buf("fvsb")
                  cf = sbt(e2, "cf", [128, 256], F32); CF = Buf("cf")
                  sel127 = sbt(e2, "sel127", [128, 128], F32); SEL127 = Buf("sel127")
                  cbb = sbt(e2, "cbb", [128, 256 + 1024], BF16); CBB = Buf("cbb")
                  gsb = sbt(e2, "gsb", [128, 512], F32); GSB = Buf("gsb")
                  m8 = sbt(e2, "m8", [128, 256], F32); M8 = Buf("m8")
                  thr = sbt(e2, "thr", [128, 32], F32); THR = Buf("thr")
                  selb = sbt(e2, "selb", [128, 512], BF16); SELB = Buf("selb")
                  kms = sbt(e2, "kms", [64, 16], F32); KMS = Buf("kms")
                  kmT = sbt(e2, "kmT", [64, 16], BF16); KMT = Buf("kmT")
                  rden = sbt(e2, "rden", [128, 512], F32); RDEN = Buf("rden")
                  FV = Buf("fv_dram")
                  ones8 = sbt(e2, "ones8", [8, 512], F32); ONES8 = Buf("ones8")
                  nbf = sbt(e2, "nbf", [8, 1], F32); NBF = Buf("nbf")
                  AROW = Buf("arow")
                  print("sbuf remaining inside ATT scope:", nc.sbuf_bytes_remaining)

                  J = cf[:, 0:128]
                  tri = cf[:, 128:256]
                  blockones = cbb[:, 0:128]
                  causal = cbb[:, 128:256]
                  pmask = cbb[:, 256:768]
                  eown = cbb[:, 768:1280]

                  c.dma(sp, lambda e: e.dma_start(out=cf[:], in_=cf_d[:, 0:256]), CF)
                  c.dma(sp, lambda e: e.dma_start(out=sel127[:], in_=cf_d[:, 384:512]), SEL127)
                  c.dma(pool, lambda e: e.dma_start(out=cbb[:], in_=cb_d[:, 128:128 + 1280]), CBB)
                  c.dma(sp, lambda e: e.dma_start(out=qkg[:], in_=qkg_d), QKG)
                  c.dma(sp, lambda e: e.dma_start(out=bfbc[:], in_=bf_d), BFBC)
                  c.dma(sp, lambda e: e.dma_start(out=rb33[:], in_=rb_d), RB33)
                  c.dma(sp, lambda e: e.dma_start(out=tab[:], in_=tab_d), TAB)
                  c.dma(pool, lambda e: e.dma_start(out=wf[:], in_=win_r[:, :, 1536:1544]), WF)
                  c.op(dve, lambda e: e.tensor_scalar(qkg[:, 0:1], qkg[:, 0:1], 0.125, None, ALU.mult), [QKG], [QKG])
                  c.op(dve, lambda e: e.tensor_scalar(qkg[:, 2:3], qkg[:, 2:3], 0.125, None, ALU.mult), [QKG], [QKG])
                  c.op(pool, lambda e: e.memset(VO[0][:, :, 64:128], 1.0), [], [VOB[0]])
                  c.op(pool, lambda e: e.memset(VO[1][:, :, 0:64], 1.0), [], [VOB[1]])

                  def load_pair_weights(hp, wi):
                      if hp < 4:
                          cq, ck, cv = hp * 128, 512 + hp * 128, 1024 + hp * 128
                      else:
                          cq, ck, cv = 1544 + (hp - 4) * 128, 2056 + (hp - 4) * 128, 2568 + (hp - 4) * 128
                      for j, c0 in enumerate((cq, ck, cv)):
                          c.dma(pool, lambda e: e.dma_start(out=wbuf[wi][:, j, :, :], in_=win_r[:, :, c0:c0 + 128]), WBUF[wi])

                  load_pair_weights(0, 0)

                  c.op(pe, lambda e: e.matmul(pb[6][0:8, 0:384], rb33[0:33, 0:8], tab[0:33, 0:384], start=True, stop=True),
                       [RB33, TAB], [PB[6]])
                  c.op(dve, lambda e: e.tensor_copy(fvsb[:], pb[6][0:8, 0:384]), [PB[6]], [FVSB])
                  c.dma(sp, lambda e: e.dma_start(out=fv_s, in_=fvsb[:]), FV, [FVSB])

                  def build_t2_load(h):
                      src = bass.AP(fv_t, h * 384, [[1, 128], [1, 256]])
                      c.dma(sp, lambda e: e.dma_start(out=Hh[:], in_=src), HHB, [FV])

                  def build_t2_finish(ti):
                      c.op(pe, lambda e: e.matmul(pb[7][:, 0:256], J, Hh[:], start=True, stop=True), [CF, HHB], [PB[7]])
                      c.op(dve, lambda e: e.tensor_copy(T2[:, ti, :], pb[7][:, 0:256]), [PB[7]], [T2B[ti]])

                  for t in range(32):
                      for kc in range(8):
                          c.op(pe, lambda e: e.matmul(pb[7][:, t * 8:(t + 1) * 8], hT[:, kc, t * 128:(t + 1) * 128],
                                                      wf[:, kc, :], start=(kc == 0), stop=(kc == 7)),
                               [HT[t // 4], WF], [PB[7]], signal=(kc == 7 and t % 4 == 3))
                  c.op(dve, lambda e: e.tensor_tensor(zl[:], pb[7][:, 0:256], bfbc[:], ALU.add), [PB[7], BFBC], [ZL])
                  c.op(act, lambda e: e.activation(zl[:], zl[:], AF.Exp, scale=-1.0), [ZL], [ZL])
                  c.op(act, lambda e: e.activation(zl[:], zl[:], AF.Ln, bias=1.0), [ZL], [ZL])
                  c.op(pe, lambda e: e.matmul(pb[7][:, 0:256], tri, zl[:], start=True, stop=True), [CF, ZL], [PB[7]])
                  c.op(pe, lambda e: e.matmul(pb[6][:, 0:256], onesf[:], zl[:], start=True, stop=True), [ONESF, ZL], [PB[6]])
                  c.op(dve, lambda e: e.tensor_copy(Tsb[:], pb[6][:, 0:256]), [PB[6]], [TSB])
                  T3 = Tsb[:].rearrange("p (t h) -> p t h", h=8)
                  cs3 = cs[:].rearrange("p (t h) -> p t h", h=8)
                  for h in range(8):
                      c.op(dve, lambda e: e.tensor_tensor_scan(cs3[:, :, h], onesf[:, 0:32], T3[:, :, h], 0.0,
                                                               ALU.mult, ALU.add), [ONESF, TSB], [CS])
                  c.op(dve, lambda e: e.tensor_tensor(Gs[:], pb[7][:, 0:256], cs[:], ALU.add), [PB[7], CS], [GS_])
                  c.op(dve, lambda e: e.tensor_tensor(Gs[:], Gs[:], Tsb[:], ALU.subtract), [GS_, TSB], [GS_])
                  c.op(pe, lambda e: e.matmul(pb[6][:, 0:256], sel127[:], Gs[:], start=True, stop=True), [SEL127, GS_], [PB[6]])
                  c.op(dve, lambda e: e.tensor_copy(GL[:], pb[6][:, 0:256]), [PB[6]], [GLB])
                  G3 = Gs[:].rearrange("p (t h) -> p t h", h=8)
                  c.op(pool, lambda e: e.memset(ones8[:], 1.0), [], [ONES8])
                  c.dma(sp, lambda e: e.dma_start(out=nbf[:], in_=bfc_d), NBF)
                  c.op(dve, lambda e: e.tensor_scalar(nbf[:], nbf[:], -1.0, None, ALU.mult), [NBF], [NBF])
                  for g in range(8):
                      for kc in range(8):
                          c.op(pe, lambda e: e.matmul(pb[5][0:8, :], wf[:, kc, :], hT[:, kc, g * 512:(g + 1) * 512],
                                                      start=(kc == 0), stop=(kc == 7)),
                               [HT[g], WF], [PB[5]], signal=(kc == 7))
                      c.op(act, lambda e: e.activation(rs[0:8, :], pb[5][0:8, :], AF.Exp, bias=nbf[0:8, 0:1], scale=-1.0),
                           [PB[5], NBF], [RS])
                      c.op(act, lambda e: e.activation(rs[0:8, :], rs[0:8, :], AF.Ln, bias=1.0), [RS], [RS])
                      c.op(dve, lambda e: e.tensor_tensor_scan(rden[0:8, :], ones8[:], rs[0:8, :], 0.0, ALU.mult, ALU.add),
                           [ONES8, RS], [RDEN])
                      c.op(dve, lambda e: e.tensor_scalar(sq[0][0:8, :], rden[0:8, :], -1.0, rden[0:8, 511:512],
                                                          ALU.mult, ALU.add), [RDEN], [SQ[0]])
                      c.dma(sp, lambda e: e.dma_start(out=arow_s[:, g * 512:(g + 1) * 512], in_=sq[0][0:8, :]), AROW, [SQ[0]])
                  chk("setup")

                  PBANKS = [0, 1, 4, 5]

                  def proj_qk(dst, DST, wap_fn, np_, gcol, hp, dst2=None, DST2=None):
                      def emit_proj(g):
                          bank = PBANKS[g % 4]
                          for kc in range(8):
                              c.op(pe, lambda e: e.matmul(pb[bank][0:np_, :], wap_fn(kc), hT[:, kc, g * 512:(g + 1) * 512],
                                                          start=(kc == 0), stop=(kc == 7)),
                                   [HT[g], WBUF[hp % 2]], [PB[bank]], signal=(kc == 7))

                      def emit_sq(g):
                          bank = PBANKS[g % 4]
                          c.op(act, lambda e: e.activation(sq[g % 2][0:np_, :], pb[bank][0:np_, :], AF.Square),
                               [PB[bank]], [SQ[g % 2]])

                      emit_proj(0)
                      emit_proj(1)
                      emit_sq(0)
                      for g in range(8):
                          bank = PBANKS[g % 4]
                          sbank = 2 + g % 2
                          c.op(pe, lambda e: e.matmul(pb[sbank][0:np_, :], blockones[0:np_, 0:np_], sq[g % 2][0:np_, :],
                                                      start=True, stop=True), [CBB, SQ[g % 2]], [PB[sbank]])
                          if g + 2 < 8:
                              emit_proj(g + 2)
                          if g + 1 < 8:
                              emit_sq(g + 1)
                          c.op(act, lambda e: e.activation(rs[0:np_, :], pb[sbank][0:np_, :], AF.Ln, bias=EPS,
                                                           scale=1.0 / 64), [PB[sbank]], [RS])
                          c.op(act, lambda e: e.activation(rs[0:np_, :], rs[0:np_, :], AF.Exp, scale=-0.5), [RS], [RS])
                          if dst2 is None:
                              c.op(dve, lambda e: e.scalar_tensor_tensor(dst[0:np_, g * 512:(g + 1) * 512], pb[bank][0:np_, :],
                                                                         qkg[0:np_, gcol:gcol + 1], rs[0:np_, :],
                                                                         ALU.mult, ALU.mult),
                                   [PB[bank], QKG, RS], [DST])
                          else:
                              c.op(dve, lambda e: e.scalar_tensor_tensor(dst[0:64, g * 512:(g + 1) * 512], pb[bank][0:64, :],
                                                                         qkg[0:64, gcol:gcol + 1], rs[0:64, :],
                                                                         ALU.mult, ALU.mult),
                                   [PB[bank], QKG, RS], [DST])
                              c.op(dve, lambda e: e.scalar_tensor_tensor(dst2[0:64, g * 512:(g + 1) * 512], pb[bank][64:128, :],
                                                                         qkg[64:128, gcol:gcol + 1], rs[64:128, :],
                                                                         ALU.mult, ALU.mult),
                                   [PB[bank], QKG, RS], [DST2])

                  def proj_v(hp):
                      wi = hp % 2
                      for t4 in range(8):
                          bank = (2, 3, 6, 7)[t4 % 4]
                          for s in range(4):
                              t = t4 * 4 + s
                              for kc in range(8):
                                  c.op(pe, lambda e: e.matmul(pb[bank][:, s * 128:(s + 1) * 128],
                                                              hT[:, kc, t * 128:(t + 1) * 128], wbuf[wi][:, 2, kc, :],
                                                              start=(kc == 0), stop=(kc == 7)),
                                       [HT[t // 4], WBUF[wi]], [PB[bank]], signal=(kc == 7 and s == 3))
                          pv = pb[bank][:].rearrange("p (t c) -> p t c", c=128)
                          c.op(act, lambda e: e.copy(VO[0][:, t4 * 4:t4 * 4 + 4, 0:64], pv[:, :, 0:64]), [PB[bank]], [VOB[0]])
                          c.op(act, lambda e: e.copy(VO[1][:, t4 * 4:t4 * 4 + 4, 64:128], pv[:, :, 64:128]),
                               [PB[bank]], [VOB[1]])

                  def vaug(hl, kt):
                      return VO[hl][:, kt, :]

                  state = {"item": 0, "qt": 0}

                  def run_attention(heads, hooks=None):
                      items = []
                      for hd in heads:
                          for Qi in range(8):
                              for kt in range(4 * Qi + 4):
                                  items.append((hd, Qi, kt))
                      DEPTH = 3
                      meta = {}

                      def emit_qk(idx):
                          hd, Qi, kt = items[idx]
                          gi = state["item"]
                          state["item"] += 1
                          sbk = gi % 4
                          j = kt - 4 * Qi
                          c0 = 128 * j if j > 0 else 0
                          p0, nk = hd["p0"], hd["nk"]
                          extra = None
                          if hd["kind"] == "fox":
                              if j >= 0:
                                  extra = (c0, 128, causal)
                          else:
                              if j >= 0:
                                  w = min(256, 512 - c0)
                                  extra = (c0, w, hd["t2"][:, 0:w])
                              elif j == -1:
                                  extra = (0, 128, hd["t2"][:, 128:256])
                          kq, kk = hd["Q"], hd["K"]
                          c.op(pe, lambda e: e.matmul(pb[sbk][:, c0:512], kk[p0:p0 + nk, kt * 128:(kt + 1) * 128],
                                                      kq[p0:p0 + nk, Qi * 512 + c0:(Qi + 1) * 512],
                                                      start=True, stop=(extra is None)),
                               [hd["KB"], hd["QB"]], [PB[sbk]], signal=(extra is None))
                          if extra is not None:
                              ec0, ew_, eap = extra
                              c.op(pe, lambda e: e.matmul(pb[sbk][:, ec0:ec0 + ew_], ident[:], eap, start=False, stop=True),
                                   [IDENT, CBB] + ([hd["T2B"]] if hd.get("T2B") is not None else []), [PB[sbk]])
                          bias = hd["bias"](Qi, kt)
                          if bias is None:
                              c.op(act, lambda e: e.activation(Pt[sbk][:, c0:512], pb[sbk][:, c0:512], AF.Exp),
                                   [PB[sbk]], [PT[sbk]])
                          else:
                              c.op(act, lambda e: e.activation(Pt[sbk][:, c0:512], pb[sbk][:, c0:512], AF.Exp, bias=bias),
                                   [PB[sbk], BIASF], [PT[sbk]])
                          meta[idx] = (sbk, c0)

                      def emit_pv(idx):
                          hd, Qi, kt = items[idx]
                          sbk, c0 = meta.pop(idx)
                          if kt == 0:
                              hd["obank"] = 4 + state["qt"] % 3
                              state["qt"] += 1
                          ob = hd["obank"]
                          last = (kt == 4 * Qi + 3)
                          c.op(pe, lambda e: e.matmul(pb[ob][:, c0:512], vaug(hd["hl"], kt), Pt[sbk][:, c0:512],
                                                      start=(kt == 0), stop=last),
                               [VOB[hd["hl"]], PT[sbk]], [PB[ob]])
                          if last:
                              if hd["hl"] == 0:
                                  orow, drow = slice(0, 64), slice(64, 128)
                              else:
                                  orow, drow = slice(64, 128), slice(0, 64)
                              c.op(dve, lambda e: e.reciprocal(rden[orow, :], pb[ob][drow, :]), [PB[ob]], [RDEN])
                              c.op(dve, lambda e: e.tensor_tensor(mixT[orow, hd["hp"], Qi * 512:(Qi + 1) * 512],
                                                                  pb[ob][orow, :], rden[orow, :], ALU.mult),
                                   [PB[ob], RDEN], [MIXT[hd["hp"]]])

                      for i in range(len(items) + DEPTH):
                          if hooks and i in hooks:
                              hooks[i]()
                          if i < len(items):
                              emit_qk(i)
                          if i >= DEPTH:
                              emit_pv(i - DEPTH)

                  class View2:
                      def __init__(self, t, ch):
                          self.t, self.ch = t, ch

                      def __getitem__(self, key):
                          r, cc_ = key
                          return self.t[r, self.ch, cc_]

                  qb2, kb2 = View2(mixT, 6), View2(mixT, 7)
                  QB2, KB2 = MIXT[6], MIXT[7]

                  def mod_bg():
                      v32 = mixT[:, 5, :].bitcast(F32)
                      accb = v32[:, 0:512]
                      wab = [v32[:, 512:1024], v32[:, 1024:1536], v32[:, 1536:2048]]
                      M5 = MIXT[5]
                      steps = [(ck, cp, kc) for ck in (1, 2) for cp in range(4) for kc in range(8)]

                      def dma_step(i):
                          ck, cp, kc = steps[i]
                          col0 = ck * 2048 + cp * 512
                          c.dma(sp, lambda e: e.dma_start(out=wab[i % 3], in_=wada_d[kc * 128:(kc + 1) * 128,
                                                                                   col0:col0 + 512]), M5)

                      dma_step(0)
                      dma_step(1)
                      yield
                      for i, (ck, cp, kc) in enumerate(steps):
                          if i + 2 < len(steps):
                              dma_step(i + 2)
                          if kc == 0:
                              c.op(dve, lambda e: e.tensor_scalar(accb, wab[i % 3], silc[:, 0:1], None, ALU.mult),
                                   [M5, SILC], [M5])
                          else:
                              c.op(dve, lambda e: e.scalar_tensor_tensor(accb, wab[i % 3], silc[:, kc:kc + 1], accb,
                                                                         ALU.mult, ALU.add), [M5, SILC], [M5])
                          yield
                          if kc == 7:
                              for j in range(4):
                                  c.op(pe, lambda e: e.matmul(pb[7][:, j:j + 1], accb[:, j * 128:(j + 1) * 128],
                                                              onesf[:, 0:1], start=True, stop=True),
                                       [M5, ONESF], [PB[7]], signal=(j == 3))
                              mc0 = ck * 16 + cp * 4
                              c.op(dve, lambda e: e.tensor_tensor(modcol[:, mc0:mc0 + 4], pb[7][:, 0:4],
                                                                  badac[:, mc0:mc0 + 4], ALU.add),
                                   [PB[7], BADAC], [MODCOL])
                              yield

                  bg = mod_bg()

                  def bg_step():
                      next(bg, None)

                  def fox_prep(h, hl, Q, K, QBf, KBf):
                      c.op(pool, lambda e: e.memset(K[64:65, :], 1.0), [], [KBf])
                      c.dma(sp, lambda e: e.dma_start(out=Q[64:65, :], in_=arow_s[h:h + 1, :]), QBf, [AROW])
                      for Qi in range(8):
                          c.op(dve, lambda e: e.tensor_scalar(biasF[:, hl, Qi, :], G3[:, :, h],
                                                              GL[:, (4 * Qi + 3) * 8 + h:(4 * Qi + 3) * 8 + h + 1], None,
                                                              ALU.subtract), [GS_, GLB], [BIASF])

                  def fox_attn(hl, hp, Q, K, QBf, KBf):
                      heads = [dict(kind="fox", p0=0, nk=65, hl=hl, hp=hp, t2=None, Q=Q, K=K, QB=QBf, KB=KBf,
                                    bias=(lambda Qi, kt, hl=hl: biasF[:, hl, Qi, kt:kt + 1]))]
                      run_attention(heads, {i: bg_step for i in range(6, 144, 8)})

                  def moba_stage1(hm, Q, K, QBf, KBf):
                      c.dma(pool, lambda e: e.dma_start(out=K[64:80, :].rearrange("p (a b) -> p a b", b=1024),
                                                        in_=khot_d.rearrange("p (a b) -> p a b", b=1024)), KBf)
                      c.op(dve, lambda e: e.tensor_reduce(kms[:, :], K[0:64, :].rearrange("p (n l) -> p n l", l=256),
                                                          AX.X, ALU.add), [KBf], [KMS])
                      c.op(dve, lambda e: e.tensor_scalar(kmT[:, :], kms[:, :], 1.0 / 256, None, ALU.mult), [KMS], [KMT])
                      build_t2_load(hm)

                  def moba_stage2(hm, Q, K, QBf, KBf):
                      for i in range(32):
                          c.op(pe, lambda e: e.matmul(pb[7][:, i * 16:(i + 1) * 16], Q[0:64, i * 128:(i + 1) * 128],
                                                      kmT[:, :], start=True, stop=True),
                               [QBf, KMT], [PB[7]], signal=(i == 31))
                      c.op(dve, lambda e: e.tensor_tensor(gsb[:], pb[7][:, :], pmask, ALU.add), [PB[7], CBB], [GSB])
                      for i in range(32):
                          c.op(dve, lambda e: e.max(m8[:, i * 8:(i + 1) * 8], gsb[:, i * 16:(i + 1) * 16]),
                               [GSB], [M8], signal=(i == 31))
                      m83 = m8[:].rearrange("p (i e) -> p i e", e=8)
                      c.op(dve, lambda e: e.tensor_scalar(thr[:], m83[:, :, 2], -1e29, None, ALU.max), [M8], [THR])
                      gs3 = gsb[:].rearrange("p (i n) -> p i n", n=16)
                      sel3 = selb[:].rearrange("p (i n) -> p i n", n=16)
                      c.op(dve, lambda e: e.tensor_tensor(sel3, gs3, thr[:].unsqueeze(2).to_broadcast([128, 32, 16]),
                                                          ALU.is_ge), [GSB, THR], [SELB])
                      c.op(dve, lambda e: e.scalar_tensor_tensor(selb[:], selb[:], -1.0, eown, ALU.add, ALU.max),
                           [SELB, CBB], [SELB])

                  def moba_stage3(hl, Q, K, QBf, KBf):
                      for i8 in range(4):
                          for i in range(8):
                              ii = i8 * 8 + i
                              c.op(pe, lambda e: e.transpose(pb16[7][0:16, i * 128:(i + 1) * 128],
                                                             selb[:, ii * 16:(ii + 1) * 16], ident[:]),
                                   [SELB, IDENT], [PB[7]], signal=(i == 7))
                          c.op(act, lambda e: e.copy(Q[64:80, i8 * 1024:(i8 + 1) * 1024], pb16[7][0:16, :]),
                               [PB[7]], [QBf])
                      build_t2_finish(hl)

                  def moba_attn(hl, hp, Q, K, QBf, KBf, hooks=None):
                      heads = [dict(kind="moba", p0=0, nk=80, hl=hl, hp=hp, t2=T2[:, hl, :], T2B=T2B[hl],
                                    Q=Q, K=K, QB=QBf, KB=KBf, bias=(lambda Qi, kt: None))]
                      run_attention(heads, hooks)

                  for hp in range(8):
                      wi = hp % 2
                      fox = hp < 4
                      gq, gk = (0, 1) if fox else (2, 3)
                      if hp + 1 < 8:
                          load_pair_weights(hp + 1, (hp + 1) % 2)
                      emit_conv(9)
                      if hp == 4:
                          for _ in bg:
                              pass
                      setA = (qb, kb, QB, KB)
                      setB = (qb2, kb2, QB2, KB2)
                      if hp < 6:
                          proj_qk(qb, QB, lambda kc: wbuf[wi][:, 0, kc, :], 128, gq, hp, dst2=qb2, DST2=QB2)
                          proj_qk(kb, KB, lambda kc: wbuf[wi][:, 1, kc, :], 128, gk, hp, dst2=kb2, DST2=KB2)
                          if fox:
                              fox_prep(2 * hp, 0, *setA)
                              fox_prep(2 * hp + 1, 1, *setB)
                              proj_v(hp)
                              fox_attn(0, hp, *setA)
                              if hp == 0:
                                  chk("fox0")
                              fox_attn(1, hp, *setB)
                          else:
                              hm = 2 * (hp - 4)
                              moba_stage1(hm, *setA)
                              moba_stage2(hm, *setA)
                              proj_v(hp)
                              moba_stage3(0, *setA)
                              hooks = {2: (lambda: moba_stage1(hm + 1, *setB)),
                                       30: (lambda: moba_stage2(hm + 1, *setB)),
                                       70: (lambda: moba_stage3(1, *setB))}
                              moba_attn(0, hp, *setA, hooks=hooks)
                              if hp == 4:
                                  chk("moba0")
                              moba_attn(1, hp, *setB)
                      else:
                          for hl in range(2):
                              proj_qk(qb, QB, lambda kc: wbuf[wi][:, 0, kc, 64 * hl:64 * hl + 64], 64, gq, hp)
                              proj_qk(kb, KB, lambda kc: wbuf[wi][:, 1, kc, 64 * hl:64 * hl + 64], 64, gk, hp)
                              hm = 2 * (hp - 4) + hl
                              moba_stage1(hm, *setA)
                              moba_stage2(hm, *setA)
                              if hl == 0:
                                  proj_v(hp)
                              moba_stage3(hl, *setA)
                              moba_attn(hl, hp, *setA)
                  emit_conv(100)
                  c.barrier()
              except _Stop:
                  c.barrier()
                  hit[0] = True
          if hit[0]:
              raise _Stop()
          es_h.close()

          with ExitStack() as e3:
              try:
                  wo = sbt(e3, "wo", [128, 8, D], BF16); WO = Buf("wo")
                  wdn = sbt(e3, "wdn", [128, NF, D], BF16); WDN = Buf("wdn")
                  x1 = [sbt(e3, "x1_%d" % i, [128, D], F32) for i in range(6)]
                  X1 = [Buf("x1_%d" % i) for i in range(6)]
                  h2T = sbt(e3, "h2T", [128, 8, 512], BF16); H2T = Buf("h2T")
                  actT = sbt(e3, "actT", [128, NF, 512], BF16); ACTT = Buf("actT")
                  gbc = sbt(e3, "gbc", [128, D], F32); GBC = Buf("gbc")
                  sg = [sbt(e3, "sg%d" % i, [128, 512], F32) for i in range(2)]
                  SG = [Buf("sg%d" % i) for i in range(2)]
                  xs2 = sbt(e3, "xs2", [128, D], BF16); XS2 = Buf("xs2")
                  wgu = [sbt(e3, "wgu%d" % i, [128, 2, 8, 128], BF16) for i in range(3)]
                  WGU = [Buf("wgu%d" % i) for i in range(3)]
                  junk2 = sbt(e3, "junk2", [128, D], BF16); JUNK2 = Buf("junk2")
                  ss2 = sbt(e3, "ss2", [128, 32], F32); SS2 = Buf("ss2")
                  ln2 = sbt(e3, "ln2", [128, 32], F32); LN2 = Buf("ln2")
                  rstd2 = sbt(e3, "rstd2", [128, 32], F32); RSTD2 = Buf("rstd2")
                  colrep = sbt(e3, "colrep", [128, 128], F32); COLREP = Buf("colrep")
                  OUTB = [Buf("outb%d" % i) for i in range(6)]
                  print("sbuf remaining inside FFN scope:", nc.sbuf_bytes_remaining)

                  for kc in range(0, 8, 4):
                      c.dma(sp, lambda e: e.dma_start(out=wo[:, kc:kc + 4, :], in_=wo_s[:, kc:kc + 4, :]), WO, [WSCR])
                  for f in range(0, NF, 11):
                      c.dma(sp, lambda e: e.dma_start(out=wdn[:, f:f + 11, :], in_=wdn_s[:, f:f + 11, :]), WDN, [WSCR])

                  def fold_gate(col0, W, WBUF_, nchunk):
                      for kc in range(8):
                          c.op(dve, lambda e: e.tensor_copy(colrep[:], modcol[:, col0 + kc:col0 + kc + 1].to_broadcast([128, 128])),
                               [MODCOL], [COLREP])
                          c.op(pe, lambda e: e.matmul(pb[kc // 4][:, (kc % 4) * 128:(kc % 4 + 1) * 128],
                                                      colrep[:], identf[:], start=True, stop=True),
                               [COLREP, IDENTF], [PB[kc // 4]])
                      for hh in range(2):
                          c.op(dve, lambda e: e.tensor_copy(gbc[:, hh * 512:(hh + 1) * 512], pb[hh][:, :]), [PB[hh]], [GBC])
                      for j in range(nchunk):
                          c.op(pool, lambda e: e.tensor_tensor(W[:, j, :], W[:, j, :], gbc[:], ALU.mult), [WBUF_, GBC], [WBUF_])

                  c.op(dve, lambda e: e.scalar_tensor_tensor(AB[:, 16:24], modcol[:, 32:40], 1.0, normcol[:, 8:16],
                                                             ALU.add, ALU.mult), [MODCOL, NORMCOL], [ABB])
                  c.op(dve, lambda e: e.tensor_copy(AB[:, 24:32], modcol[:, 24:32]), [MODCOL], [ABB])
                  fold_gate(16, wo, WO, 8)
                  fold_gate(40, wdn, WDN, NF)
                  chk("fold")

                  wgcnt = [0]
                  for T in range(8):
                      for s in range(4):
                          u = 4 * T + s
                          xi = u % 6
                          c.dma(sp, lambda e: e.dma_start(out=x1[xi][:], in_=x_r[:, u, :]), X1[xi])
                          for ch in range(2):
                              bank = ch
                              for kc in range(8):
                                  c.op(pe, lambda e: e.matmul(pb[bank][:, :], mixT[:, kc, u * 128:(u + 1) * 128],
                                                              wo[:, kc, ch * 512:(ch + 1) * 512],
                                                              start=(kc == 0), stop=(kc == 7)),
                                       [MIXT[kc], WO], [PB[bank]], signal=(kc == 7))
                              c.op(dve, lambda e: e.tensor_tensor(x1[xi][:, ch * 512:(ch + 1) * 512], pb[bank][:, :],
                                                                  x1[xi][:, ch * 512:(ch + 1) * 512], ALU.add),
                                   [PB[bank], X1[xi]], [X1[xi]])
                          c.op(act, lambda e: e.activation(junk2[:], x1[xi][:], AF.Square, accum_out=ss2[:, u:u + 1]),
                               [X1[xi]], [JUNK2, SS2])
                      c.op(act, lambda e: e.activation(ln2[:, 4 * T:4 * T + 4], ss2[:, 4 * T:4 * T + 4], AF.Ln, bias=EPS,
                                                       scale=1.0 / D), [SS2], [LN2])
                      c.op(act, lambda e: e.activation(rstd2[:, 4 * T:4 * T + 4], ln2[:, 4 * T:4 * T + 4], AF.Exp,
                                                       scale=-0.5), [LN2], [RSTD2])
                      for s in range(4):
                          u = 4 * T + s
                          xi = u % 6
                          c.op(act, lambda e: e.activation(xs2[:], x1[xi][:], AF.Copy, scale=rstd2[:, u:u + 1]),
                               [X1[xi], RSTD2], [XS2])
                          bank = 2 + s % 2
                          for kc in range(8):
                              c.op(pe, lambda e: e.transpose(pb16[bank][:, kc * 128:(kc + 1) * 128],
                                                             xs2[:, kc * 128:(kc + 1) * 128], ident[:]),
                                   [XS2, IDENT], [PB[bank]], signal=(kc == 7))
                          for kc in range(8):
                              c.op(dve, lambda e: e.tensor_scalar(h2T[:, kc, s * 128:(s + 1) * 128],
                                                                  pb16[bank][:, kc * 128:(kc + 1) * 128],
                                                                  AB[:, 16 + kc:17 + kc], AB[:, 24 + kc:25 + kc],
                                                                  ALU.mult, ALU.add),
                                   [PB[bank], ABB], [H2T], signal=(kc == 7))
                      for f in range(NF):
                          wi = wgcnt[0] % 3
                          wgcnt[0] += 1
                          c.dma(sp, lambda e: e.dma_start(out=wgu[wi][:], in_=wgu_s[f]), WGU[wi], [WSCR])
                          gb, ub = 4 + f % 2, 6 + f % 2
                          for kc in range(8):
                              c.op(pe, lambda e: e.matmul(pb[gb][:, :], wgu[wi][:, 0, kc, :], h2T[:, kc, :],
                                                          start=(kc == 0), stop=(kc == 7)),
                                   [WGU[wi], H2T], [PB[gb]], signal=(kc == 7))
                          for kc in range(8):
                              c.op(pe, lambda e: e.matmul(pb[ub][:, :], wgu[wi][:, 1, kc, :], h2T[:, kc, :],
                                                          start=(kc == 0), stop=(kc == 7)),
                                   [WGU[wi], H2T], [PB[ub]], signal=(kc == 7))
                          c.op(act, lambda e: e.activation(sg[f % 2][:], pb[gb][:, :], AF.Silu), [PB[gb]], [SG[f % 2]])
                          c.op(dve, lambda e: e.tensor_tensor(actT[:, f, :], pb[ub][:, :], sg[f % 2][:], ALU.mult),
                               [PB[ub], SG[f % 2]], [ACTT])
                      for s in range(4):
                          u = 4 * T + s
                          xi = u % 6
                          for ch in range(2):
                              bank = ch
                              for f in range(NF):
                                  c.op(pe, lambda e: e.matmul(pb[bank][:, :], actT[:, f, s * 128:(s + 1) * 128],
                                                              wdn[:, f, ch * 512:(ch + 1) * 512],
                                                              start=(f == 0), stop=(f == NF - 1)),
                                       [ACTT, WDN], [PB[bank]], signal=(f == NF - 1))
                              c.op(dve, lambda e: e.tensor_tensor(x1[xi][:, ch * 512:(ch + 1) * 512], pb[bank][:, :],
                                                                  x1[xi][:, ch * 512:(ch + 1) * 512], ALU.add),
                                   [PB[bank], X1[xi]], [X1[xi]])
                          c.dma(pool, lambda e: e.dma_start(out=out_r[:, u, :], in_=x1[xi][:]), OUTB[xi], [X1[xi]])
                  c.barrier()
              except _Stop:
                  c.barrier()
                  hit[0] = True
          if hit[0]:
              raise _Stop()
      except _Stop:
        es_h.close()
    return nc


_CACHE = {}


def kernel(**inputs):
    x = np.ascontiguousarray(inputs["x"], dtype=np.float32)
    cvec = np.asarray(inputs["c"], dtype=np.float32)
    f32c = lambda a: np.ascontiguousarray(a, dtype=np.float32)
    w_ada = f32c(inputs["w_ada"][0])
    b_ada = np.asarray(inputs["b_ada"][0], np.float32)
    cf, cb, khot, tab = host_consts()
    colform = lambda v, n: np.ascontiguousarray(v.reshape(n, 128).T)
    norm_col = np.concatenate([colform(np.asarray(inputs["norm1"][0], np.float32), 8),
                               colform(np.asarray(inputs["norm2"][0], np.float32), 8)], axis=1)
    dup = lambda g: np.concatenate([g, g]).astype(np.float32)
    qk_g = np.stack([dup(np.asarray(inputs["q_norm_fox"][0])), dup(np.asarray(inputs["k_norm_fox"][0])),
                     dup(np.asarray(inputs["q_norm_moba"][0])), dup(np.asarray(inputs["k_norm_moba"][0]))], axis=1)
    bf_bc = np.ascontiguousarray(np.broadcast_to(np.tile(np.asarray(inputs["b_forget"][0], np.float32), 32)[None, :],
                                                 (128, 256)))
    rb33 = np.concatenate([np.asarray(inputs["rel_bias"], np.float32), np.ones((1, 8), np.float32)], axis=0)
    shared = {
        "w_ada": w_ada,
        "b_ada_col": colform(b_ada, 48),
        "norm_col": f32c(norm_col),
        "w_in": f32c(inputs["w_in"][0]),
        "bf_bc": bf_bc,
        "bf_col": np.ascontiguousarray(np.asarray(inputs["b_forget"][0], np.float32).reshape(8, 1)),
        "qk_g": f32c(qk_g),
        "rb33": f32c(rb33),
        "w_o": f32c(inputs["w_o"][0]),
        "w_gate": f32c(inputs["w_gate"][0]),
        "w_up": f32c(inputs["w_up"][0]),
        "w_down": f32c(inputs["w_down"][0]),
        "cf32": cf, "cb16": cb, "khot": khot, "t5tab": tab,
    }
    if "nc" not in _CACHE:
        _CACHE["nc"] = build_program()
    nc = _CACHE["nc"]
    in_maps = []
    for b in range(NCORES):
        m = dict(shared)
        m["x"] = x[b]
        m["c_col"] = colform(cvec[b], 8)
        in_maps.append(m)
    res = run_bass_kernel_spmd(nc, in_maps, core_ids=list(range(NCORES)))
    out = np.stack([np.asarray(res.results[b]["out"], dtype=np.float32).reshape(S, D) for b in range(NCORES)], axis=0)
    return out
```

```python
import math
from contextlib import ExitStack

import numpy as np
import concourse.bass as bass
import concourse.mybir as mybir
from concourse.bass_utils import run_bass_kernel_spmd

F32 = mybir.dt.float32
BF16 = mybir.dt.bfloat16
ALU = mybir.AluOpType
AF = mybir.ActivationFunctionType
AX = mybir.AxisListType

S = 4096
D = 1024
NCORES = 8
DFF = 2816
NF = 22
INC = 3080
EPS = 1e-6
NEG = -32768.0
N_BUCKETS = 32
MAX_DISTANCE = 128


class EngW:
    def __init__(self, nc, es, eng, name):
        self.eng = eng
        self.name = name
        self.sem = es.enter_context(nc.semaphore("sem_" + name))
        self.count = 0
        self.waited = {}
        self.pend_r = []
        self.pend_w = []

    def wait(self, tok):
        sem, val = tok
        key = id(sem)
        if self.waited.get(key, 0) >= val:
            return
        self.eng.wait_ge(sem, val)
        self.waited[key] = val


class Buf:
    def __init__(self, name):
        self.name = name
        self.w = {}
        self.r = {}
        self.dsem = None
        self.dcount = 0

    def toks_w(self):
        return list(self.w.values())

    def toks_all(self):
        return list(self.w.values()) + list(self.r.values())


def _put(d, tok):
    k = id(tok[0])
    if k not in d or d[k][1] < tok[1]:
        d[k] = tok


class Ctx:
    def __init__(self, nc, es):
        self.nc = nc
        self.es = es
        self.pe = EngW(nc, es, nc.tensor, "pe")
        self.act = EngW(nc, es, nc.scalar, "act")
        self.dve = EngW(nc, es, nc.vector, "dve")
        self.pool = EngW(nc, es, nc.gpsimd, "pool")
        self.sp = EngW(nc, es, nc.sync, "sp")
        self.engs = [self.pe, self.act, self.dve, self.pool, self.sp]
        self.dma_bufs = []
        self.swdge_inflight = []
        self.SWDGE_MAX = 5

    def op(self, ew, fn, reads=(), writes=(), signal=True):
        for b in reads:
            for t in b.toks_w():
                ew.wait(t)
        for b in writes:
            for t in b.toks_all():
                ew.wait(t)
        ins = fn(ew.eng)
        ew.pend_r += list(reads)
        ew.pend_w += list(writes)
        if signal:
            ew.count += 1
            ins.then_inc(ew.sem, 1)
            tok = (ew.sem, ew.count)
            for b in ew.pend_r:
                _put(b.r, tok)
            for b in ew.pend_w:
                b.w = {id(tok[0]): tok}
                b.r = {}
            ew.pend_r = []
            ew.pend_w = []
        return ins

    def dma(self, ew, fn, dst, srcs=()):
        if dst.dsem is None:
            dst.dsem = self.es.enter_context(self.nc.semaphore("dsem_" + dst.name))
            self.dma_bufs.append(dst)
        for b in srcs:
            for t in b.toks_w():
                ew.wait(t)
        for k, t in list(dst.w.items()) + list(dst.r.items()):
            if t[0] is dst.dsem:
                continue
            ew.wait(t)
        if ew is self.pool:
            while len(self.swdge_inflight) >= self.SWDGE_MAX:
                b0 = self.swdge_inflight.pop(0)
                ew.wait((b0.dsem, b0.dcount))
                self.swdge_inflight = [b for b in self.swdge_inflight if b is not b0]
        ins = fn(ew.eng)
        dst.dcount += 16
        ins.then_inc(dst.dsem, 16)
        tok = (dst.dsem, dst.dcount)
        if ew is self.pool:
            self.swdge_inflight.append(dst)
        for b in srcs:
            _put(b.r, tok)
        dst.w[id(dst.dsem)] = tok
        dst.r = {}
        return ins

    def barrier(self):
        toks = []
        for e in self.engs:
            assert not e.pend_r and not e.pend_w, e.name
            if e.count > 0:
                toks.append((e.sem, e.count))
        for b in self.dma_bufs:
            toks.append((b.dsem, b.dcount))
        for e in self.engs:
            for t in toks:
                e.wait(t)


def t5_bucket_np(n):
    n = np.maximum(n, 0)
    max_exact = N_BUCKETS // 2
    nf = np.maximum(n, 1).astype(np.float32)
    large = max_exact + (np.log(nf / np.float32(max_exact)) / np.float32(math.log(MAX_DISTANCE / max_exact))
                         * np.float32(N_BUCKETS - max_exact)).astype(np.int32)
    large = np.minimum(large, N_BUCKETS - 1)
    return np.where(n < max_exact, n, large)


def host_consts():
    p = np.arange(128)
    cf = np.zeros((128, 4 * 128), np.float32)
    cf[:, 0:128] = (p[:, None] + p[None, :] == 127)
    cf[:, 128:256] = (p[:, None] <= p[None, :])
    cf[:, 256:384] = 1.0
    cf[:, 384:512] = (p[:, None] == 127)
    cb = np.zeros((128, 3 * 128 + 1024), np.float32)
    cb[:, 0:128] = np.eye(128)
    cb[:, 128:256] = (p[:, None] // 64 == p[None, :] // 64)
    cb[:, 256:384] = np.where(p[:, None] <= p[None, :], 0.0, NEG)
    i = np.arange(32)[:, None]
    n = np.arange(16)[None, :]
    pm = np.where(n < i // 2, 0.0, -1e30).astype(np.float32)
    eo = np.where(n == i // 2, 0.0, -1.0).astype(np.float32)
    cb[:, 384:384 + 512] = pm.reshape(1, 512)
    cb[:, 896:896 + 512] = eo.reshape(1, 512)
    khot = (np.arange(4096)[None, :] // 256 == np.arange(16)[:, None]).astype(np.float32) * 32768.0
    tab = np.zeros((33, 384), np.float32)
    d = np.arange(384) - 127
    bk = t5_bucket_np(d)
    for j in range(384):
        if d[j] >= 0:
            tab[bk[j], j] += 1.0
            tab[31, j] -= 1.0
        else:
            tab[32, j] = NEG
    return cf, cb, khot, tab


class _Stop(Exception):
    pass


def build_program(stop=None):
    nc = bass.Bass("TRN2", target_bir_lowering=False)

    def chk(tag):
        if stop == tag:
            raise _Stop()

    def din(name, shape):
        return nc.dram_tensor(name, list(shape), F32, kind="ExternalInput").ap()

    x_d = din("x", [S, D])
    c_d = din("c_col", [128, 8])
    wada_d = din("w_ada", [D, 6 * D])
    bada_d = din("b_ada_col", [128, 48])
    norm_d = din("norm_col", [128, 16])
    win_d = din("w_in", [D, INC])
    bf_d = din("bf_bc", [128, 256])
    bfc_d = din("bf_col", [8, 1])
    qkg_d = din("qk_g", [128, 4])
    rb_d = din("rb33", [33, 8])
    wo_d = din("w_o", [D, D])
    wg_d = din("w_gate", [D, DFF])
    wu_d = din("w_up", [D, DFF])
    wd_d = din("w_down", [DFF, D])
    cf_d = din("cf32", [128, 512])
    cb_d = din("cb16", [128, 384 + 1024])
    khot_d = din("khot", [16, 4096])
    tab_d = din("t5tab", [33, 384])
    out_d = nc.dram_tensor("out", [S, D], F32, kind="ExternalOutput").ap()

    wgu_s = nc.dram_tensor("wgu_s", [NF, 128, 2, 8, 128], BF16).ap()
    wdn_s = nc.dram_tensor("wdn_s", [128, NF, D], BF16).ap()
    wo_s = nc.dram_tensor("wo_s", [128, 8, D], BF16).ap()
    fv_t = nc.dram_tensor("fv_s", [8, 384], F32)
    arow_s = nc.dram_tensor("arow_s", [8, S], BF16).ap()
    fv_s = fv_t.ap()

    win_r = win_d.rearrange("(kc p) n -> p kc n", p=128)
    x_r = x_d.rearrange("(t p) d -> p t d", p=128)
    out_r = out_d.rearrange("(t p) d -> p t d", p=128)

    with ExitStack() as es:
      c = Ctx(nc, es)
      es_h = ExitStack()
      hit = [False]
      try:
          pe, act, dve, pool, sp = c.pe, c.act, c.dve, c.pool, c.sp

          def sbt(stack, name, shape, dt, side=None):
              name = "s_" + name
              if side is None:
                  return stack.enter_context(nc.sbuf_tensor(name, list(shape), dt))
              return stack.enter_context(nc.sbuf_tensor(name, list(shape), dt, side=side))

          pb = [es.enter_context(nc.psum_tensor("pb%d" % i, [128, 512], F32)) for i in range(8)]
          pb16 = [t.bitcast(BF16) for t in pb]
          PB = [Buf("pb%d" % i) for i in range(8)]

          ident = sbt(es, "ident", [128, 128], BF16); IDENT = Buf("ident")
          modcol = sbt(es, "modcol", [128, 48], F32); MODCOL = Buf("modcol")
          normcol = sbt(es, "normcol", [128, 16], F32); NORMCOL = Buf("normcol")
          AB = sbt(es, "AB", [128, 32], F32); ABB = Buf("AB")
          onesf = sbt(es, "onesf", [128, 128], F32); ONESF = Buf("onesf")
          identf = sbt(es, "identf", [128, 128], F32); IDENTF = Buf("identf")
          silc = sbt(es, "silc", [128, 8], F32); SILC = Buf("silc")
          badac = sbt(es, "badac", [128, 48], F32); BADAC = Buf("badac")

          c.dma(pool, lambda e: e.dma_start(out=ident[:], in_=cb_d[:, 0:128]), IDENT)
          c.dma(sp, lambda e: e.dma_start(out=normcol[:], in_=norm_d), NORMCOL)
          c.dma(sp, lambda e: e.dma_start(out=onesf[:], in_=cf_d[:, 256:384]), ONESF)
          c.dma(sp, lambda e: e.dma_start(out=identf[:], in_=cb_d[:, 0:128]), IDENTF)

          hT = sbt(es_h, "hT", [128, 8, S], BF16, side="right")
          HT = [Buf("hT%d" % g) for g in range(8)]

          with ExitStack() as e1:
              try:
                  cc = sbt(e1, "cc", [128, 8], F32); CC = Buf("cc")
                  acc = sbt(e1, "acc", [128, 2048], F32); ACC = Buf("acc")
                  wa = [sbt(e1, "wa%d" % i, [128, 2048], F32) for i in range(2)]
                  WA = [Buf("wa%d" % i) for i in range(2)]
                  xt = [sbt(e1, "xt%d" % i, [128, 4, D], F32) for i in range(2)]
                  XT = [Buf("xt%d" % i) for i in range(2)]
                  xs = [sbt(e1, "xs%d" % i, [128, 4, D], BF16) for i in range(2)]
                  XS = [Buf("xs%d" % i) for i in range(2)]
                  junk = sbt(e1, "junk", [128, D], BF16); JUNK = Buf("junk")
                  ss = sbt(e1, "ss", [128, 32], F32); SS = Buf("ss")
                  lnv = sbt(e1, "lnv", [128, 32], F32); LNV = Buf("lnv")
                  rstd = sbt(e1, "rstd", [128, 32], F32); RSTD = Buf("rstd")

                  c.dma(sp, lambda e: e.dma_start(out=cc[:], in_=c_d), CC)
                  c.dma(sp, lambda e: e.dma_start(out=badac[:], in_=bada_d), BADAC)
                  c.op(act, lambda e: e.activation(silc[:], cc[:], AF.Silu), [CC], [SILC])

                  wcnt = [0]

                  def mod_chunk(ck):
                      for kc in range(8):
                          wi = wcnt[0] % 2
                          wcnt[0] += 1
                          c.dma(sp, lambda e: e.dma_start(out=wa[wi][:], in_=wada_d[kc * 128:(kc + 1) * 128,
                                                                                ck * 2048:(ck + 1) * 2048]), WA[wi])
                          if kc == 0:
                              c.op(dve, lambda e: e.tensor_scalar(acc[:], wa[wi][:], silc[:, 0:1], None, ALU.mult),
                                   [WA[wi], SILC], [ACC])
                          else:
                              c.op(dve, lambda e: e.scalar_tensor_tensor(acc[:], wa[wi][:], silc[:, kc:kc + 1], acc[:],
                                                                         ALU.mult, ALU.add),
                                   [WA[wi], SILC, ACC], [ACC])
                      for j in range(16):
                          c.op(pe, lambda e: e.matmul(pb[7][:, j:j + 1], acc[:, j * 128:(j + 1) * 128], onesf[:, 0:1],
                                                      start=True, stop=True),
                               [ACC, ONESF], [PB[7]], signal=(j == 15))
                      c.op(dve, lambda e: e.tensor_tensor(modcol[:, ck * 16:(ck + 1) * 16], pb[7][:, 0:16],
                                                          badac[:, ck * 16:(ck + 1) * 16], ALU.add),
                           [PB[7], BADAC], [MODCOL])

                  mod_chunk(0)
                  c.op(dve, lambda e: e.scalar_tensor_tensor(AB[:, 0:8], modcol[:, 8:16], 1.0, normcol[:, 0:8],
                                                             ALU.add, ALU.mult), [MODCOL, NORMCOL], [ABB])
                  c.op(dve, lambda e: e.tensor_copy(AB[:, 8:16], modcol[:, 0:8]), [MODCOL], [ABB])

                  for g in range(8):
                      bi = g % 2
                      c.dma(sp, lambda e: e.dma_start(out=xt[bi][:], in_=x_r[:, 4 * g:4 * g + 4, :]), XT[bi])
                      for s in range(4):
                          c.op(act, lambda e: e.activation(junk[:], xt[bi][:, s, :], AF.Square,
                                                           accum_out=ss[:, 4 * g + s:4 * g + s + 1]),
                               [XT[bi]], [JUNK, SS])
                      c.op(act, lambda e: e.activation(lnv[:, 4 * g:4 * g + 4], ss[:, 4 * g:4 * g + 4], AF.Ln,
                                                       bias=EPS, scale=1.0 / D), [SS], [LNV])
                      c.op(act, lambda e: e.activation(rstd[:, 4 * g:4 * g + 4], lnv[:, 4 * g:4 * g + 4], AF.Exp,
                                                       scale=-0.5), [LNV], [RSTD])
                      for s in range(4):
                          c.op(act, lambda e: e.activation(xs[bi][:, s, :], xt[bi][:, s, :], AF.Copy,
                                                           scale=rstd[:, 4 * g + s:4 * g + s + 1]),
                               [XT[bi], RSTD], [XS[bi]])
                      for kc in range(8):
                          bank = (g % 2) * 4 + kc // 2
                          half = kc % 2
                          for s in range(4):
                              c.op(pe, lambda e: e.transpose(pb16[bank][:, half * 512 + s * 128: half * 512 + (s + 1) * 128],
                                                             xs[bi][:, s, kc * 128:(kc + 1) * 128], ident[:]),
                                   [XS[bi], IDENT], [PB[bank]], signal=(s == 3))
                          c.op(dve, lambda e: e.tensor_scalar(hT[:, kc, g * 512:(g + 1) * 512],
                                                              pb16[bank][:, half * 512:(half + 1) * 512],
                                                              AB[:, kc:kc + 1], AB[:, 8 + kc:9 + kc], ALU.mult, ALU.add),
                               [PB[bank], ABB], [HT[g]])

                  c.barrier()
                  chk("p1")
              except _Stop:
                  c.barrier()
                  hit[0] = True
          if hit[0]:
              raise _Stop()

          mixT = sbt(es, "mixT", [128, 8, S], BF16); MIXT = [Buf("mixT%d" % g) for g in range(8)]
          print("sbuf remaining before ATT scope:", nc.sbuf_bytes_remaining)

          WSCR = Buf("wscr")
          conv_jobs = []
          wg_r = wg_d.rearrange("(kc p) n -> p kc n", p=128)
          wu_r = wu_d.rearrange("(kc p) n -> p kc n", p=128)
          for f in range(NF):
              conv_jobs.append(lambda e, f=f: e.dma_start(out=wgu_s[f, :, 0, :, :], in_=wg_r[:, :, f * 128:(f + 1) * 128]))
              conv_jobs.append(lambda e, f=f: e.dma_start(out=wgu_s[f, :, 1, :, :], in_=wu_r[:, :, f * 128:(f + 1) * 128]))
          wd_r = wd_d.rearrange("(f p) n -> p f n", p=128)
          for f in range(0, NF, 2):
              conv_jobs.append(lambda e, f=f: e.dma_start(out=wdn_s[:, f:f + 2, :], in_=wd_r[:, f:f + 2, :]))
          wo_r = wo_d.rearrange("(kc p) n -> p kc n", p=128)
          for kc in range(0, 8, 2):
              conv_jobs.append(lambda e, kc=kc: e.dma_start(out=wo_s[:, kc:kc + 2, :], in_=wo_r[:, kc:kc + 2, :]))
          conv_pos = [0]

          def emit_conv(n):
              for _ in range(n):
                  if conv_pos[0] < len(conv_jobs):
                      c.dma(pool, conv_jobs[conv_pos[0]], WSCR)
                      conv_pos[0] += 1

          with ExitStack() as e2:
              try:
                  qb = sbt(e2, "qb", [128, S], BF16); QB = Buf("qb")
                  kb = sbt(e2, "kb", [128, S], BF16); KB = Buf("kb")
                  VO = [sbt(e2, "vo%d" % i, [128, 32, 128], BF16) for i in range(2)]
                  VOB = [Buf("vo%d" % i) for i in range(2)]
                  Pt = [sbt(e2, "pt%d" % i, [128, 512], BF16) for i in range(4)]
                  PT = [Buf("pt%d" % i) for i in range(4)]
                  sq = [sbt(e2, "sq%d" % i, [128, 512], BF16) for i in range(2)]
                  SQ = [Buf("sq%d" % i) for i in range(2)]
                  rs = sbt(e2, "rs", [128, 512], F32); RS = Buf("rs")
                  wbuf = [sbt(e2, "wb%d" % i, [128, 3, 8, 128], BF16) for i in range(2)]
                  WBUF = [Buf("wb%d" % i) for i in range(2)]
                  wf = sbt(e2, "wf", [128, 8, 8], BF16); WF = Buf("wf")
                  qkg = sbt(e2, "qkg", [128, 4], F32); QKG = Buf("qkg")
                  bfbc = sbt(e2, "bfbc", [128, 256], F32); BFBC = Buf("bfbc")
                  zl = sbt(e2, "zl", [128, 256], F32); ZL = Buf("zl")
                  Tsb = sbt(e2, "Tsb", [128, 256], F32); TSB = Buf("Tsb")
                  cs = sbt(e2, "cs", [128, 256], F32); CS = Buf("cs")
                  Gs = sbt(e2, "Gs", [128, 256], F32); GS_ = Buf("Gs")
                  GL = sbt(e2, "GL", [128, 256], F32); GLB = Buf("GL")
                  biasF = sbt(e2, "biasF", [128, 2, 8, 32], F32); BIASF = Buf("biasF")
                  T2 = sbt(e2, "T2", [128, 2, 256], BF16); T2B = [Buf("T2_0"), Buf("T2_1")]
                  Hh = sbt(e2, "Hh", [128, 256], F32); HHB = Buf("Hh")
                  rb33 = sbt(e2, "rb33", [33, 8], F32); RB33 = Buf("rb33")
                  tab = sbt(e2, "tab", [33, 384], F32); TAB = Buf("tab")
                  fvsb = sbt(e2, "fvsb", [8, 384], F32); FVSB = Buf("fvsb")
                  cf = sbt(e2, "cf", [128, 256], F32); CF = Buf("cf")
                  sel127 = sbt(e2, "sel127", [128, 128], F32); SEL127 = Buf("sel127")
                  cbb = sbt(e2, "cbb", [128, 256 + 1024], BF16); CBB = Buf("cbb")
                  gsb = sbt(e2, "gsb", [128, 512], F32); GSB = Buf("gsb")
                  m8 = sbt(e2, "m8", [128, 256], F32); M8 = Buf("m8")
                  thr = sbt(e2, "thr", [128, 32], F32); THR = Buf("thr")
                  selb = sbt(e2, "selb", [128, 512], BF16); SELB = Buf("selb")
                  kms = sbt(e2, "kms", [64, 16], F32); KMS = Buf("kms")
                  kmT = sbt(e2, "kmT", [64, 16], BF16); KMT = Buf("kmT")
                  rden = sbt(e2, "rden", [128, 512], F32); RDEN = Buf("rden")
                  FV = Buf("fv_dram")
                  ones8 = sbt(e2, "ones8", [8, 512], F32); ONES8 = Buf("ones8")
                  nbf = sbt(e2, "nbf", [8, 1], F32); NBF = Buf("nbf")
                  AROW = Buf("arow")
                  print("sbuf remaining inside ATT scope:", nc.sbuf_bytes_remaining)

                  J = cf[:, 0:128]
                  tri = cf[:, 128:256]
                  blockones = cbb[:, 0:128]
                  causal = cbb[:, 128:256]
                  pmask = cbb[:, 256:768]
                  eown = cbb[:, 768:1280]

                  c.dma(sp, lambda e: e.dma_start(out=cf[:], in_=cf_d[:, 0:256]), CF)
                  c.dma(sp, lambda e: e.dma_start(out=sel127[:], in_=cf_d[:, 384:512]), SEL127)
                  c.dma(pool, lambda e: e.dma_start(out=cbb[:], in_=cb_d[:, 128:128 + 1280]), CBB)
                  c.dma(sp, lambda e: e.dma_start(out=qkg[:], in_=qkg_d), QKG)
                  c.dma(sp, lambda e: e.dma_start(out=bfbc[:], in_=bf_d), BFBC)
                  c.dma(sp, lambda e: e.dma_start(out=rb33[:], in_=rb_d), RB33)
                  c.dma(sp, lambda e: e.dma_start(out=tab[:], in_=tab_d), TAB)
                  c.dma(pool, lambda e: e.dma_start(out=wf[:], in_=win_r[:, :, 1536:1544]), WF)
                  c.op(dve, lambda e: e.tensor_scalar(qkg[:, 0:1], qkg[:, 0:1], 0.125, None, ALU.mult), [QKG], [QKG])
                  c.op(dve, lambda e: e.tensor_scalar(qkg[:, 2:3], qkg[:, 2:3], 0.125, None, ALU.mult), [QKG], [QKG])
                  c.op(pool, lambda e: e.memset(VO[0][:, :, 64:128], 1.0), [], [VOB[0]])
                  c.op(pool, lambda e: e.memset(VO[1][:, :, 0:64], 1.0), [], [VOB[1]])

                  def load_pair_weights(hp, wi):
                      if hp < 4:
                          cq, ck, cv = hp * 128, 512 + hp * 128, 1024 + hp * 128
                      else:
                          cq, ck, cv = 1544 + (hp - 4) * 128, 2056 + (hp - 4) * 128, 2568 + (hp - 4) * 128
                      for j, c0 in enumerate((cq, ck, cv)):
                          c.dma(pool, lambda e: e.dma_start(out=wbuf[wi][:, j, :, :], in_=win_r[:, :, c0:c0 + 128]), WBUF[wi])

                  load_pair_weights(0, 0)

                  c.op(pe, lambda e: e.matmul(pb[6][0:8, 0:384], rb33[0:33, 0:8], tab[0:33, 0:384], start=True, stop=True),
                       [RB33, TAB], [PB[6]])
                  c.op(dve, lambda e: e.tensor_copy(fvsb[:], pb[6][0:8, 0:384]), [PB[6]], [FVSB])
                  c.dma(sp, lambda e: e.dma_start(out=fv_s, in_=fvsb[:]), FV, [FVSB])

                  def build_t2_load(h):
                      src = bass.AP(fv_t, h * 384, [[1, 128], [1, 256]])
                      c.dma(sp, lambda e: e.dma_start(out=Hh[:], in_=src), HHB, [FV])

                  def build_t2_finish(ti):
                      c.op(pe, lambda e: e.matmul(pb[7][:, 0:256], J, Hh[:], start=True, stop=True), [CF, HHB], [PB[7]])
                      c.op(dve, lambda e: e.tensor_copy(T2[:, ti, :], pb[7][:, 0:256]), [PB[7]], [T2B[ti]])

                  for t in range(32):
                      for kc in range(8):
                          c.op(pe, lambda e: e.matmul(pb[7][:, t * 8:(t + 1) * 8], hT[:, kc, t * 128:(t + 1) * 128],
                                                      wf[:, kc, :], start=(kc == 0), stop=(kc == 7)),
                               [HT[t // 4], WF], [PB[7]], signal=(kc == 7 and t % 4 == 3))
                  c.op(dve, lambda e: e.tensor_tensor(zl[:], pb[7][:, 0:256], bfbc[:], ALU.add), [PB[7], BFBC], [ZL])
                  c.op(act, lambda e: e.activation(zl[:], zl[:], AF.Exp, scale=-1.0), [ZL], [ZL])
                  c.op(act, lambda e: e.activation(zl[:], zl[:], AF.Ln, bias=1.0), [ZL], [ZL])
                  c.op(pe, lambda e: e.matmul(pb[7][:, 0:256], tri, zl[:], start=True, stop=True), [CF, ZL], [PB[7]])
                  c.op(pe, lambda e: e.matmul(pb[6][:, 0:256], onesf[:], zl[:], start=True, stop=True), [ONESF, ZL], [PB[6]])
                  c.op(dve, lambda e: e.tensor_copy(Tsb[:], pb[6][:, 0:256]), [PB[6]], [TSB])
                  T3 = Tsb[:].rearrange("p (t h) -> p t h", h=8)
                  cs3 = cs[:].rearrange("p (t h) -> p t h", h=8)
                  for h in range(8):
                      c.op(dve, lambda e: e.tensor_tensor_scan(cs3[:, :, h], onesf[:, 0:32], T3[:, :, h], 0.0,
                                                               ALU.mult, ALU.add), [ONESF, TSB], [CS])
                  c.op(dve, lambda e: e.tensor_tensor(Gs[:], pb[7][:, 0:256], cs[:], ALU.add), [PB[7], CS], [GS_])
                  c.op(dve, lambda e: e.tensor_tensor(Gs[:], Gs[:], Tsb[:], ALU.subtract), [GS_, TSB], [GS_])
                  c.op(pe, lambda e: e.matmul(pb[6][:, 0:256], sel127[:], Gs[:], start=True, stop=True), [SEL127, GS_], [PB[6]])
                  c.op(dve, lambda e: e.tensor_copy(GL[:], pb[6][:, 0:256]), [PB[6]], [GLB])
                  G3 = Gs[:].rearrange("p (t h) -> p t h", h=8)
                  c.op(pool, lambda e: e.memset(ones8[:], 1.0), [], [ONES8])
                  c.dma(sp, lambda e: e.dma_start(out=nbf[:], in_=bfc_d), NBF)
                  c.op(dve, lambda e: e.tensor_scalar(nbf[:], nbf[:], -1.0, None, ALU.mult), [NBF], [NBF])
                  for g in range(8):
                      for kc in range(8):
                          c.op(pe, lambda e: e.matmul(pb[5][0:8, :], wf[:, kc, :], hT[:, kc, g * 512:(g + 1) * 512],
                                                      start=(kc == 0), stop=(kc == 7)),
                               [HT[g], WF], [PB[5]], signal=(kc == 7))
                      c.op(act, lambda e: e.activation(rs[0:8, :], pb[5][0:8, :], AF.Exp, bias=nbf[0:8, 0:1], scale=-1.0),
                           [PB[5], NBF], [RS])
                      c.op(act, lambda e: e.activation(rs[0:8, :], rs[0:8, :], AF.Ln, bias=1.0), [RS], [RS])
                      c.op(dve, lambda e: e.tensor_tensor_scan(rden[0:8, :], ones8[:], rs[0:8, :], 0.0, ALU.mult, ALU.add),
                           [ONES8, RS], [RDEN])
                      c.op(dve, lambda e: e.tensor_scalar(sq[0][0:8, :], rden[0:8, :], -1.0, rden[0:8, 511:512],
                                                          ALU.mult, ALU.add), [RDEN], [SQ[0]])
                      c.dma(sp, lambda e: e.dma_start(out=arow_s[:, g * 512:(g + 1) * 512], in_=sq[0][0:8, :]), AROW, [SQ[0]])
                  chk("setup")

                  PBANKS = [0, 1, 4, 5]

                  def proj_qk(dst, DST, wap_fn, np_, gcol, hp, dst2=None, DST2=None):
                      def emit_proj(g):
                          bank = PBANKS[g % 4]
                          for kc in range(8):
                              c.op(pe, lambda e: e.matmul(pb[bank][0:np_, :], wap_fn(kc), hT[:, kc, g * 512:(g + 1) * 512],
                                                          start=(kc == 0), stop=(kc == 7)),
                                   [HT[g], WBUF[hp % 2]], [PB[bank]], signal=(kc == 7))

                      def emit_sq(g):
                          bank = PBANKS[g % 4]
                          c.op(act, lambda e: e.activation(sq[g % 2][0:np_, :], pb[bank][0:np_, :], AF.Square),
                               [PB[bank]], [SQ[g % 2]])

                      emit_proj(0)
                      emit_proj(1)
                      emit_sq(0)
                      for g in range(8):
                          bank = PBANKS[g % 4]
                          sbank = 2 + g % 2
                          c.op(pe, lambda e: e.matmul(pb[sbank][0:np_, :], blockones[0:np_, 0:np_], sq[g % 2][0:np_, :],
                                                      start=True, stop=True), [CBB, SQ[g % 2]], [PB[sbank]])
                          if g + 2 < 8:
                              emit_proj(g + 2)
                          if g + 1 < 8:
                              emit_sq(g + 1)
                          c.op(act, lambda e: e.activation(rs[0:np_, :], pb[sbank][0:np_, :], AF.Ln, bias=EPS,
                                                           scale=1.0 / 64), [PB[sbank]], [RS])
                          c.op(act, lambda e: e.activation(rs[0:np_, :], rs[0:np_, :], AF.Exp, scale=-0.5), [RS], [RS])
                          if dst2 is None:
                              c.op(dve, lambda e: e.scalar_tensor_tensor(dst[0:np_, g * 512:(g + 1) * 512], pb[bank][0:np_, :],
                                                                         qkg[0:np_, gcol:gcol + 1], rs[0:np_, :],
                                                                         ALU.mult, ALU.mult),
                                   [PB[bank], QKG, RS], [DST])
                          else:
                              c.op(dve, lambda e: e.scalar_tensor_tensor(dst[0:64, g * 512:(g + 1) * 512], pb[bank][0:64, :],
                                                                         qkg[0:64, gcol:gcol + 1], rs[0:64, :],
                                                                         ALU.mult, ALU.mult),
                                   [PB[bank], QKG, RS], [DST])
                              c.op(dve, lambda e: e.scalar_tensor_tensor(dst2[0:64, g * 512:(g + 1) * 512], pb[bank][64:128, :],
                                                                         qkg[64:128, gcol:gcol + 1], rs[64:128, :],
                                                                         ALU.mult, ALU.mult),
                                   [PB[bank], QKG, RS], [DST2])

                  def proj_v(hp):
                      wi = hp % 2
                      for t4 in range(8):
                          bank = 2 + t4 % 2
                          for s in range(4):
                              t = t4 * 4 + s
                              for kc in range(8):
                                  c.op(pe, lambda e: e.matmul(pb[bank][:, s * 128:(s + 1) * 128],
                                                              hT[:, kc, t * 128:(t + 1) * 128], wbuf[wi][:, 2, kc, :],
                                                              start=(kc == 0), stop=(kc == 7)),
                                       [HT[t // 4], WBUF[wi]], [PB[bank]], signal=(kc == 7 and s == 3))
                          pv = pb[bank][:].rearrange("p (t c) -> p t c", c=128)
                          c.op(act, lambda e: e.copy(VO[0][:, t4 * 4:t4 * 4 + 4, 0:64], pv[:, :, 0:64]), [PB[bank]], [VOB[0]])
                          c.op(act, lambda e: e.copy(VO[1][:, t4 * 4:t4 * 4 + 4, 64:128], pv[:, :, 64:128]),
                               [PB[bank]], [VOB[1]])

                  def vaug(hl, kt):
                      return VO[hl][:, kt, :]

                  state = {"item": 0, "qt": 0}

                  def run_attention(heads, hooks=None):
                      items = []
                      for hd in heads:
                          for Qi in range(8):
                              for kt in range(4 * Qi + 4):
                                  items.append((hd, Qi, kt))
                      DEPTH = 3
                      meta = {}

                      def emit_qk(idx):
                          hd, Qi, kt = items[idx]
                          gi = state["item"]
                          state["item"] += 1
                          sbk = gi % 4
                          j = kt - 4 * Qi
                          c0 = 128 * j if j > 0 else 0
                          p0, nk = hd["p0"], hd["nk"]
                          extra = None
                          if hd["kind"] == "fox":
                              if j >= 0:
                                  extra = (c0, 128, causal)
                          else:
                              if j >= 0:
                                  w = min(256, 512 - c0)
                                  extra = (c0, w, hd["t2"][:, 0:w])
                              elif j == -1:
                                  extra = (0, 128, hd["t2"][:, 128:256])
                          kq, kk = hd["Q"], hd["K"]
                          c.op(pe, lambda e: e.matmul(pb[sbk][:, c0:512], kk[p0:p0 + nk, kt * 128:(kt + 1) * 128],
                                                      kq[p0:p0 + nk, Qi * 512 + c0:(Qi + 1) * 512],
                                                      start=True, stop=(extra is None)),
                               [hd["KB"], hd["QB"]], [PB[sbk]], signal=(extra is None))
                          if extra is not None:
                              ec0, ew_, eap = extra
                              c.op(pe, lambda e: e.matmul(pb[sbk][:, ec0:ec0 + ew_], ident[:], eap, start=False, stop=True),
                                   [IDENT, CBB] + ([hd["T2B"]] if hd.get("T2B") is not None else []), [PB[sbk]])
                          bias = hd["bias"](Qi, kt)
                          if bias is None:
                              c.op(act, lambda e: e.activation(Pt[sbk][:, c0:512], pb[sbk][:, c0:512], AF.Exp),
                                   [PB[sbk]], [PT[sbk]])
                          else:
                              c.op(act, lambda e: e.activation(Pt[sbk][:, c0:512], pb[sbk][:, c0:512], AF.Exp, bias=bias),
                                   [PB[sbk], BIASF], [PT[sbk]])
                          meta[idx] = (sbk, c0)

                      def emit_pv(idx):
                          hd, Qi, kt = items[idx]
                          sbk, c0 = meta.pop(idx)
                          if kt == 0:
                              hd["obank"] = 4 + state["qt"] % 3
                              state["qt"] += 1
                          ob = hd["obank"]
                          last = (kt == 4 * Qi + 3)
                          c.op(pe, lambda e: e.matmul(pb[ob][:, c0:512], vaug(hd["hl"], kt), Pt[sbk][:, c0:512],
                                                      start=(kt == 0), stop=last),
                               [VOB[hd["hl"]], PT[sbk]], [PB[ob]])
                          if last:
                              if hd["hl"] == 0:
                                  orow, drow = slice(0, 64), slice(64, 128)
                              else:
                                  orow, drow = slice(64, 128), slice(0, 64)
                              c.op(dve, lambda e: e.reciprocal(rden[orow, :], pb[ob][drow, :]), [PB[ob]], [RDEN])
                              c.op(dve, lambda e: e.tensor_tensor(mixT[orow, hd["hp"], Qi * 512:(Qi + 1) * 512],
                                                                  pb[ob][orow, :], rden[orow, :], ALU.mult),
                                   [PB[ob], RDEN], [MIXT[hd["hp"]]])

                      for i in range(len(items) + DEPTH):
                          if hooks and i in hooks:
                              hooks[i]()
                          if i < len(items):
                              emit_qk(i)
                          if i >= DEPTH:
                              emit_pv(i - DEPTH)

                  class View2:
                      def __init__(self, t, ch):
                          self.t, self.ch = t, ch

                      def __getitem__(self, key):
                          r, cc_ = key
                          return self.t[r, self.ch, cc_]

                  qb2, kb2 = View2(mixT, 6), View2(mixT, 7)
                  QB2, KB2 = MIXT[6], MIXT[7]

                  def mod_bg():
                      v32 = mixT[:, 5, :].bitcast(F32)
                      accb = v32[:, 0:512]
                      wab = [v32[:, 512:1024], v32[:, 1024:1536], v32[:, 1536:2048]]
                      M5 = MIXT[5]
                      steps = [(ck, cp, kc) for ck in (1, 2) for cp in range(4) for kc in range(8)]

                      def dma_step(i):
                          ck, cp, kc = steps[i]
                          col0 = ck * 2048 + cp * 512
                          c.dma(sp, lambda e: e.dma_start(out=wab[i % 3], in_=wada_d[kc * 128:(kc + 1) * 128,
                                                                                   col0:col0 + 512]), M5)

                      dma_step(0)
                      dma_step(1)
                      yield
                      for i, (ck, cp, kc) in enumerate(steps):
                          if i + 2 < len(steps):
                              dma_step(i + 2)
                          if kc == 0:
                              c.op(dve, lambda e: e.tensor_scalar(accb, wab[i % 3], silc[:, 0:1], None, ALU.mult),
                                   [M5, SILC], [M5])
                          else:
                              c.op(dve, lambda e: e.scalar_tensor_tensor(accb, wab[i % 3], silc[:, kc:kc + 1], accb,
                                                                         ALU.mult, ALU.add), [M5, SILC], [M5])
                          yield
                          if kc == 7:
                              for j in range(4):
                                  c.op(pe, lambda e: e.matmul(pb[7][:, j:j + 1], accb[:, j * 128:(j + 1) * 128],
                                                              onesf[:, 0:1], start=True, stop=True),
                                       [M5, ONESF], [PB[7]], signal=(j == 3))
                              mc0 = ck * 16 + cp * 4
                              c.op(dve, lambda e: e.tensor_tensor(modcol[:, mc0:mc0 + 4], pb[7][:, 0:4],
                                                                  badac[:, mc0:mc0 + 4], ALU.add),
                                   [PB[7], BADAC], [MODCOL])
                              yield

                  bg = mod_bg()

                  def bg_step():
                      next(bg, None)

                  def fox_prep(h, hl, Q, K, QBf, KBf):
                      c.op(pool, lambda e: e.memset(K[64:65, :], 1.0), [], [KBf])
                      c.dma(sp, lambda e: e.dma_start(out=Q[64:65, :], in_=arow_s[h:h + 1, :]), QBf, [AROW])
                      for Qi in range(8):
                          c.op(dve, lambda e: e.tensor_scalar(biasF[:, hl, Qi, :], G3[:, :, h],
                                                              GL[:, (4 * Qi + 3) * 8 + h:(4 * Qi + 3) * 8 + h + 1], None,
                                                              ALU.subtract), [GS_, GLB], [BIASF])

                  def fox_attn(hl, hp, Q, K, QBf, KBf):
                      heads = [dict(kind="fox", p0=0, nk=65, hl=hl, hp=hp, t2=None, Q=Q, K=K, QB=QBf, KB=KBf,
                                    bias=(lambda Qi, kt, hl=hl: biasF[:, hl, Qi, kt:kt + 1]))]
                      run_attention(heads, {i: bg_step for i in range(6, 144, 8)})

                  def moba_stage1(hm, Q, K, QBf, KBf):
                      c.dma(pool, lambda e: e.dma_start(out=K[64:80, :].rearrange("p (a b) -> p a b", b=1024),
                                                        in_=khot_d.rearrange("p (a b) -> p a b", b=1024)), KBf)
                      c.op(dve, lambda e: e.tensor_reduce(kms[:, :], K[0:64, :].rearrange("p (n l) -> p n l", l=256),
                                                          AX.X, ALU.add), [KBf], [KMS])
                      c.op(dve, lambda e: e.tensor_scalar(kmT[:, :], kms[:, :], 1.0 / 256, None, ALU.mult), [KMS], [KMT])
                      build_t2_load(hm)

                  def moba_stage2(hm, Q, K, QBf, KBf):
                      for i in range(32):
                          c.op(pe, lambda e: e.matmul(pb[7][:, i * 16:(i + 1) * 16], Q[0:64, i * 128:(i + 1) * 128],
                                                      kmT[:, :], start=True, stop=True),
                               [QBf, KMT], [PB[7]], signal=(i == 31))
                      c.op(dve, lambda e: e.tensor_tensor(gsb[:], pb[7][:, :], pmask, ALU.add), [PB[7], CBB], [GSB])
                      for i in range(32):
                          c.op(dve, lambda e: e.max(m8[:, i * 8:(i + 1) * 8], gsb[:, i * 16:(i + 1) * 16]),
                               [GSB], [M8], signal=(i == 31))
                      m83 = m8[:].rearrange("p (i e) -> p i e", e=8)
                      c.op(dve, lambda e: e.tensor_scalar(thr[:], m83[:, :, 2], -1e29, None, ALU.max), [M8], [THR])
                      gs3 = gsb[:].rearrange("p (i n) -> p i n", n=16)
                      sel3 = selb[:].rearrange("p (i n) -> p i n", n=16)
                      c.op(dve, lambda e: e.tensor_tensor(sel3, gs3, thr[:].unsqueeze(2).to_broadcast([128, 32, 16]),
                                                          ALU.is_ge), [GSB, THR], [SELB])
                      c.op(dve, lambda e: e.scalar_tensor_tensor(selb[:], selb[:], -1.0, eown, ALU.add, ALU.max),
                           [SELB, CBB], [SELB])

                  def moba_stage3(hl, Q, K, QBf, KBf):
                      for i8 in range(4):
                          for i in range(8):
                              ii = i8 * 8 + i
                              c.op(pe, lambda e: e.transpose(pb16[7][0:16, i * 128:(i + 1) * 128],
                                                             selb[:, ii * 16:(ii + 1) * 16], ident[:]),
                                   [SELB, IDENT], [PB[7]], signal=(i == 7))
                          c.op(dve, lambda e: e.tensor_copy(Q[64:80, i8 * 1024:(i8 + 1) * 1024], pb16[7][0:16, :]),
                               [PB[7]], [QBf])
                      build_t2_finish(hl)

                  def moba_attn(hl, hp, Q, K, QBf, KBf, hooks=None):
                      heads = [dict(kind="moba", p0=0, nk=80, hl=hl, hp=hp, t2=T2[:, hl, :], T2B=T2B[hl],
                                    Q=Q, K=K, QB=QBf, KB=KBf, bias=(lambda Qi, kt: None))]
                      run_attention(heads, hooks)

                  for hp in range(8):
                      wi = hp % 2
                      fox = hp < 4
                      gq, gk = (0, 1) if fox else (2, 3)
                      if hp + 1 < 8:
                          load_pair_weights(hp + 1, (hp + 1) % 2)
                      emit_conv(9)
                      if hp == 4:
                          for _ in bg:
                              pass
                      setA = (qb, kb, QB, KB)
                      setB = (qb2, kb2, QB2, KB2)
                      if hp < 6:
                          proj_qk(qb, QB, lambda kc: wbuf[wi][:, 0, kc, :], 128, gq, hp, dst2=qb2, DST2=QB2)
                          proj_qk(kb, KB, lambda kc: wbuf[wi][:, 1, kc, :], 128, gk, hp, dst2=kb2, DST2=KB2)
                          if fox:
                              fox_prep(2 * hp, 0, *setA)
                              fox_prep(2 * hp + 1, 1, *setB)
                              proj_v(hp)
                              fox_attn(0, hp, *setA)
                              if hp == 0:
                                  chk("fox0")
                              fox_attn(1, hp, *setB)
                          else:
                              hm = 2 * (hp - 4)
                              moba_stage1(hm, *setA)
                              moba_stage2(hm, *setA)
                              proj_v(hp)
                              moba_stage3(0, *setA)
                              hooks = {2: (lambda: moba_stage1(hm + 1, *setB)),
                                       30: (lambda: moba_stage2(hm + 1, *setB)),
                                       70: (lambda: moba_stage3(1, *setB))}
                              moba_attn(0, hp, *setA, hooks=hooks)
                              if hp == 4:
                                  chk("moba0")
                              moba_attn(1, hp, *setB)
                      else:
                          for hl in range(2):
                              proj_qk(qb, QB, lambda kc: wbuf[wi][:, 0, kc, 64 * hl:64 * hl + 64], 64, gq, hp)
                              proj_qk(kb, KB, lambda kc: wbuf[wi][:, 1, kc, 64 * hl:64 * hl + 64], 64, gk, hp)
                              hm = 2 * (hp - 4) + hl
                              moba_stage1(hm, *setA)
                              moba_stage2(hm, *setA)
                              if hl == 0:
                                  proj_v(hp)
                              moba_stage3(hl, *setA)
                              moba_attn(hl, hp, *setA)
                  emit_conv(100)
                  c.barrier()
              except _Stop:
                  c.barrier()
                  hit[0] = True
          if hit[0]:
              raise _Stop()
          es_h.close()

          with ExitStack() as e3:
              try:
                  wo = sbt(e3, "wo", [128, 8, D], BF16); WO = Buf("wo")
                  wdn = sbt(e3, "wdn", [128, NF, D], BF16); WDN = Buf("wdn")
                  x1 = [sbt(e3, "x1_%d" % i, [128, D], F32) for i in range(6)]
                  X1 = [Buf("x1_%d" % i) for i in range(6)]
                  h2T = sbt(e3, "h2T", [128, 8, 512], BF16); H2T = Buf("h2T")
                  actT = sbt(e3, "actT", [128, NF, 512], BF16); ACTT = Buf("actT")
                  gbc = sbt(e3, "gbc", [128, D], F32); GBC = Buf("gbc")
                  sg = [sbt(e3, "sg%d" % i, [128, 512], F32) for i in range(2)]
                  SG = [Buf("sg%d" % i) for i in range(2)]
                  xs2 = sbt(e3, "xs2", [128, D], BF16); XS2 = Buf("xs2")
                  wgu = [sbt(e3, "wgu%d" % i, [128, 2, 8, 128], BF16) for i in range(3)]
                  WGU = [Buf("wgu%d" % i) for i in range(3)]
                  junk2 = sbt(e3, "junk2", [128, D], BF16); JUNK2 = Buf("junk2")
                  ss2 = sbt(e3, "ss2", [128, 32], F32); SS2 = Buf("ss2")
                  ln2 = sbt(e3, "ln2", [128, 32], F32); LN2 = Buf("ln2")
                  rstd2 = sbt(e3, "rstd2", [128, 32], F32); RSTD2 = Buf("rstd2")
                  colrep = sbt(e3, "colrep", [128, 128], F32); COLREP = Buf("colrep")
                  OUTB = [Buf("outb%d" % i) for i in range(6)]
                  print("sbuf remaining inside FFN scope:", nc.sbuf_bytes_remaining)

                  for kc in range(0, 8, 4):
                      c.dma(sp, lambda e: e.dma_start(out=wo[:, kc:kc + 4, :], in_=wo_s[:, kc:kc + 4, :]), WO, [WSCR])
                  for f in range(0, NF, 11):
                      c.dma(sp, lambda e: e.dma_start(out=wdn[:, f:f + 11, :], in_=wdn_s[:, f:f + 11, :]), WDN, [WSCR])

                  def fold_gate(col0, W, WBUF_, nchunk):
                      for kc in range(8):
                          c.op(dve, lambda e: e.tensor_copy(colrep[:], modcol[:, col0 + kc:col0 + kc + 1].to_broadcast([128, 128])),
                               [MODCOL], [COLREP])
                          c.op(pe, lambda e: e.matmul(pb[kc // 4][:, (kc % 4) * 128:(kc % 4 + 1) * 128],
                                                      colrep[:], identf[:], start=True, stop=True),
                               [COLREP, IDENTF], [PB[kc // 4]])
                      for hh in range(2):
                          c.op(dve, lambda e: e.tensor_copy(gbc[:, hh * 512:(hh + 1) * 512], pb[hh][:, :]), [PB[hh]], [GBC])
                      for j in range(nchunk):
                          c.op(pool, lambda e: e.tensor_tensor(W[:, j, :], W[:, j, :], gbc[:], ALU.mult), [WBUF_, GBC], [WBUF_])

                  c.op(dve, lambda e: e.scalar_tensor_tensor(AB[:, 16:24], modcol[:, 32:40], 1.0, normcol[:, 8:16],
                                                             ALU.add, ALU.mult), [MODCOL, NORMCOL], [ABB])
                  c.op(dve, lambda e: e.tensor_copy(AB[:, 24:32], modcol[:, 24:32]), [MODCOL], [ABB])
                  fold_gate(16, wo, WO, 8)
                  fold_gate(40, wdn, WDN, NF)
                  chk("fold")

                  wgcnt = [0]
                  for T in range(8):
                      for s in range(4):
                          u = 4 * T + s
                          xi = u % 6
                          c.dma(sp, lambda e: e.dma_start(out=x1[xi][:], in_=x_r[:, u, :]), X1[xi])
                          for ch in range(2):
                              bank = ch
                              for kc in range(8):
                                  c.op(pe, lambda e: e.matmul(pb[bank][:, :], mixT[:, kc, u * 128:(u + 1) * 128],
                                                              wo[:, kc, ch * 512:(ch + 1) * 512],
                                                              start=(kc == 0), stop=(kc == 7)),
                                       [MIXT[kc], WO], [PB[bank]], signal=(kc == 7))
                              c.op(dve, lambda e: e.tensor_tensor(x1[xi][:, ch * 512:(ch + 1) * 512], pb[bank][:, :],
                                                                  x1[xi][:, ch * 512:(ch + 1) * 512], ALU.add),
                                   [PB[bank], X1[xi]], [X1[xi]])
                          c.op(act, lambda e: e.activation(junk2[:], x1[xi][:], AF.Square, accum_out=ss2[:, u:u + 1]),
                               [X1[xi]], [JUNK2, SS2])
                      c.op(act, lambda e: e.activation(ln2[:, 4 * T:4 * T + 4], ss2[:, 4 * T:4 * T + 4], AF.Ln, bias=EPS,
                                                       scale=1.0 / D), [SS2], [LN2])
                      c.op(act, lambda e: e.activation(rstd2[:, 4 * T:4 * T + 4], ln2[:, 4 * T:4 * T + 4], AF.Exp,
                                                       scale=-0.5), [LN2], [RSTD2])
                      for s in range(4):
                          u = 4 * T + s
                          xi = u % 6
                          c.op(act, lambda e: e.activation(xs2[:], x1[xi][:], AF.Copy, scale=rstd2[:, u:u + 1]),
                               [X1[xi], RSTD2], [XS2])
                          bank = 2 + s % 2
                          for kc in range(8):
                              c.op(pe, lambda e: e.transpose(pb16[bank][:, kc * 128:(kc + 1) * 128],
                                                             xs2[:, kc * 128:(kc + 1) * 128], ident[:]),
                                   [XS2, IDENT], [PB[bank]], signal=(kc == 7))
                          for kc in range(8):
                              c.op(dve, lambda e: e.tensor_scalar(h2T[:, kc, s * 128:(s + 1) * 128],
                                                                  pb16[bank][:, kc * 128:(kc + 1) * 128],
                                                                  AB[:, 16 + kc:17 + kc], AB[:, 24 + kc:25 + kc],
                                                                  ALU.mult, ALU.add),
                                   [PB[bank], ABB], [H2T], signal=(kc == 7))
                      for f in range(NF):
                          wi = wgcnt[0] % 3
                          wgcnt[0] += 1
                          c.dma(sp, lambda e: e.dma_start(out=wgu[wi][:], in_=wgu_s[f]), WGU[wi], [WSCR])
                          gb, ub = 4 + f % 2, 6 + f % 2
                          for kc in range(8):
                              c.op(pe, lambda e: e.matmul(pb[gb][:, :], wgu[wi][:, 0, kc, :], h2T[:, kc, :],
                                                          start=(kc == 0), stop=(kc == 7)),
                                   [WGU[wi], H2T], [PB[gb]], signal=(kc == 7))
                          for kc in range(8):
                              c.op(pe, lambda e: e.matmul(pb[ub][:, :], wgu[wi][:, 1, kc, :], h2T[:, kc, :],
                                                          start=(kc == 0), stop=(kc == 7)),
                                   [WGU[wi], H2T], [PB[ub]], signal=(kc == 7))
                          c.op(act, lambda e: e.activation(sg[f % 2][:], pb[gb][:, :], AF.Silu), [PB[gb]], [SG[f % 2]])
                          c.op(dve, lambda e: e.tensor_tensor(actT[:, f, :], pb[ub][:, :], sg[f % 2][:], ALU.mult),
                               [PB[ub], SG[f % 2]], [ACTT])
                      for s in range(4):
                          u = 4 * T + s
                          xi = u % 6
                          for ch in range(2):
                              bank = ch
                              for f in range(NF):
                                  c.op(pe, lambda e: e.matmul(pb[bank][:, :], actT[:, f, s * 128:(s + 1) * 128],
                                                              wdn[:, f, ch * 512:(ch + 1) * 512],
                                                              start=(f == 0), stop=(f == NF - 1)),
                                       [ACTT, WDN], [PB[bank]], signal=(f == NF - 1))
                              c.op(dve, lambda e: e.tensor_tensor(x1[xi][:, ch * 512:(ch + 1) * 512], pb[bank][:, :],
                                                                  x1[xi][:, ch * 512:(ch + 1) * 512], ALU.add),
                                   [PB[bank], X1[xi]], [X1[xi]])
                          c.dma(pool, lambda e: e.dma_start(out=out_r[:, u, :], in_=x1[xi][:]), OUTB[xi], [X1[xi]])
                  c.barrier()
              except _Stop:
                  c.barrier()
                  hit[0] = True
          if hit[0]:
              raise _Stop()
      except _Stop:
        es_h.close()
    return nc


_CACHE = {}


def kernel(**inputs):
    x = np.ascontiguousarray(inputs["x"], dtype=np.float32)
    cvec = np.asarray(inputs["c"], dtype=np.float32)
    f32c = lambda a: np.ascontiguousarray(a, dtype=np.float32)
    w_ada = f32c(inputs["w_ada"][0])
    b_ada = np.asarray(inputs["b_ada"][0], np.float32)
    cf, cb, khot, tab = host_consts()
    colform = lambda v, n: np.ascontiguousarray(v.reshape(n, 128).T)
    norm_col = np.concatenate([colform(np.asarray(inputs["norm1"][0], np.float32), 8),
                               colform(np.asarray(inputs["norm2"][0], np.float32), 8)], axis=1)
    dup = lambda g: np.concatenate([g, g]).astype(np.float32)
    qk_g = np.stack([dup(np.asarray(inputs["q_norm_fox"][0])), dup(np.asarray(inputs["k_norm_fox"][0])),
                     dup(np.asarray(inputs["q_norm_moba"][0])), dup(np.asarray(inputs["k_norm_moba"][0]))], axis=1)
    bf_bc = np.ascontiguousarray(np.broadcast_to(np.tile(np.asarray(inputs["b_forget"][0], np.float32), 32)[None, :],
                                                 (128, 256)))
    rb33 = np.concatenate([np.asarray(inputs["rel_bias"], np.float32), np.ones((1, 8), np.float32)], axis=0)
    shared = {
        "w_ada": w_ada,
        "b_ada_col": colform(b_ada, 48),
        "norm_col": f32c(norm_col),
        "w_in": f32c(inputs["w_in"][0]),
        "bf_bc": bf_bc,
        "bf_col": np.ascontiguousarray(np.asarray(inputs["b_forget"][0], np.float32).reshape(8, 1)),
        "qk_g": f32c(qk_g),
        "rb33": f32c(rb33),
        "w_o": f32c(inputs["w_o"][0]),
        "w_gate": f32c(inputs["w_gate"][0]),
        "w_up": f32c(inputs["w_up"][0]),
        "w_down": f32c(inputs["w_down"][0]),
        "cf32": cf, "cb16": cb, "khot": khot, "t5tab": tab,
    }
    if "nc" not in _CACHE:
        _CACHE["nc"] = build_program()
    nc = _CACHE["nc"]
    in_maps = []
    for b in range(NCORES):
        m = dict(shared)
        m["x"] = x[b]
        m["c_col"] = colform(cvec[b], 8)
        in_maps.append(m)
    res = run_bass_kernel_spmd(nc, in_maps, core_ids=list(range(NCORES)))
    out = np.stack([np.asarray(res.results[b]["out"], dtype=np.float32).reshape(S, D) for b in range(NCORES)], axis=0)
    return out
```
